# Optimizing a Trainium2 kernel written in Bass

```python
import jax, jax.numpy as jnp
from jax import lax
import numpy as np

D_MODEL = 1024
BATCH = 8
SEQ = 4096
DEPTH = 2

CHUNK = 64
Q_BLOCK = 128

MLA_HEADS = 8
Q_LORA = 384
KV_LORA = 256
QK_NOPE = 64
QK_ROPE = 32
V_HEAD = 64
ROPE_THETA = 10000.0

MLSTM_HEADS = 8
MLSTM_HEAD_DIM = 64
MLSTM_W = MLSTM_HEADS * MLSTM_HEAD_DIM
MLSTM_CONV = 4

D_FF = 2816
FFN_CONV = 3

MLA_W = MLA_HEADS * V_HEAD
IN_SIZES = (Q_LORA, KV_LORA, QK_ROPE,
            MLSTM_W, MLSTM_W, MLSTM_W, MLSTM_W,
            MLSTM_HEADS, MLSTM_HEADS,
            D_MODEL, D_MODEL)
IN_COLS = sum(IN_SIZES)

kernel_name = "hybrid_mla_mlstm_convffn_adaln"


def _rmsnorm(x, w, eps=1e-6):
    x32 = x.astype(jnp.float32)
    y = x32 * lax.rsqrt(jnp.mean(x32 * x32, axis=-1, keepdims=True) + eps)
    return y.astype(x.dtype) * w


def _rope(x, positions):
    half = x.shape[-1] // 2
    inv = ROPE_THETA ** (-jnp.arange(half, dtype=jnp.float32) / half)
    ang = positions.astype(jnp.float32)[..., None] * inv
    if x.ndim == 4:
        ang = ang[:, :, None, :]
    cos = jnp.cos(ang).astype(x.dtype)
    sin = jnp.sin(ang).astype(x.dtype)
    x1, x2 = x[..., :half], x[..., half:]
    return jnp.concatenate([x1 * cos - x2 * sin, x2 * cos + x1 * sin], axis=-1)


def _causal_dwconv(x, w, b):
    k_w, ch = w.shape
    y = lax.conv_general_dilated(
        x, w[:, None, :].astype(x.dtype), window_strides=(1,),
        padding=[(k_w - 1, 0)], dimension_numbers=("NWC", "WIO", "NWC"),
        feature_group_count=ch)
    return y + b


def _mla_attention(q_nope, q_rope, k_nope, k_rope, v):
    s_len = q_nope.shape[1]
    scale = (QK_NOPE + QK_ROPE) ** -0.5
    outs = []
    for qb in range(s_len // Q_BLOCK):
        s0, e = qb * Q_BLOCK, (qb + 1) * Q_BLOCK
        sc = (jnp.einsum("bqhd,bkhd->bhqk", q_nope[:, s0:e], k_nope[:, :e])
              + jnp.einsum("bqhr,bkr->bhqk", q_rope[:, s0:e], k_rope[:, :e]))
        sc = sc.astype(jnp.float32) * scale
        q_chunk = (s0 + jnp.arange(Q_BLOCK)) // CHUNK
        k_chunk = jnp.arange(e) // CHUNK
        sc = jnp.where(k_chunk[None, :] <= q_chunk[:, None], sc, -jnp.inf)
        p = jax.nn.softmax(sc, axis=-1).astype(v.dtype)
        outs.append(jnp.einsum("bhqk,bkhd->bqhd", p, v[:, :e]))
    return jnp.concatenate(outs, axis=1)


def _mlstm_chunkwise(q, k, v, i_pre, log_f):
    bsz, s_len, nh, dk = q.shape
    dv = v.shape[-1]
    L = CHUNK
    nc = s_len // L
    f32 = jnp.float32

    def to_chunks(a):
        return jnp.moveaxis(a.astype(f32).reshape((bsz, nc, L) + a.shape[2:]), 1, 0)

    xs = tuple(to_chunks(a) for a in (q, k, v, i_pre, log_f))
    causal = jnp.tril(jnp.ones((L, L), dtype=bool))

    def step(carry, inp):
        C, n, m = carry
        qc, kc, vc, ic, fc = inp
        b = jnp.cumsum(fc, axis=1).transpose(0, 2, 1)
        ih = ic.transpose(0, 2, 1)
        log_d = jnp.where(causal, b[..., :, None] - b[..., None, :] + ih[..., None, :], -jnp.inf)
        log_inter = b + m[..., None]
        m_t = jnp.maximum(log_inter, jnp.max(log_d, axis=-1))
        w_intra = jnp.exp(log_d - m_t[..., None])
        w_inter = jnp.exp(log_inter - m_t)
        s = jnp.einsum("bthd,bshd->bhts", qc, kc) * w_intra
        num = (jnp.einsum("bhts,bshv->bhtv", s, vc)
               + w_inter[..., None] * jnp.einsum("bhvk,bthk->bhtv", C, qc))
        den = s.sum(-1) + w_inter * jnp.einsum("bhk,bthk->bht", n, qc)
        h = num / jnp.maximum(jnp.abs(den), jnp.exp(-m_t))[..., None]
        b_last = b[..., -1]
        log_w = b_last[..., None] - b + ih
        m_new = jnp.maximum(b_last + m, jnp.max(log_w, axis=-1))
        w_s = jnp.exp(log_w - m_new[..., None])
        decay = jnp.exp(b_last + m - m_new)
        C = decay[..., None, None] * C + jnp.einsum("bhs,bshv,bshk->bhvk", w_s, vc, kc)
        n = decay[..., None] * n + jnp.einsum("bhs,bshk->bhk", w_s, kc)
        return (C, n, m_new), h.transpose(0, 2, 1, 3)

    init = (jnp.zeros((bsz, nh, dv, dk), f32), jnp.zeros((bsz, nh, dk), f32),
            jnp.zeros((bsz, nh), f32))
    _, hs = lax.scan(step, init, xs)
    return jnp.moveaxis(hs, 0, 1).reshape(bsz, s_len, nh, dv).astype(v.dtype)


def _mixer(h, positions, w_in, q_norm_w, kv_norm_w, w_uq, w_ukv, conv_w, conv_b,
           gate_b, head_norm_w, w_br_mla, w_br_mlstm, w_out):
    bsz, s_len, _ = h.shape
    split_at = [int(s) for s in np.cumsum(IN_SIZES)[:-1]]
    proj = h @ w_in
    c_q, c_kv, k_rope, q_m, k_m, v_m, o_m, i_m, f_m, g_mla, g_mlstm = jnp.split(proj, split_at, axis=-1)

    q = (_rmsnorm(c_q, q_norm_w) @ w_uq).reshape(bsz, s_len, MLA_HEADS, QK_NOPE + QK_ROPE)
    q_nope, q_rope = q[..., :QK_NOPE], _rope(q[..., QK_NOPE:], positions)
    kv = (_rmsnorm(c_kv, kv_norm_w) @ w_ukv).reshape(bsz, s_len, MLA_HEADS, QK_NOPE + V_HEAD)
    k_nope, v = kv[..., :QK_NOPE], kv[..., QK_NOPE:]
    k_rope = _rope(k_rope, positions)
    y_mla = _mla_attention(q_nope, q_rope, k_nope, k_rope, v).reshape(bsz, s_len, MLA_W)

    qk = jax.nn.silu(_causal_dwconv(jnp.concatenate([q_m, k_m], axis=-1), conv_w, conv_b))
    q_m, k_m = jnp.split(qk, 2, axis=-1)
    i_pre, f_pre = jnp.split(jnp.concatenate([i_m, f_m], axis=-1) + gate_b, 2, axis=-1)
    hd = (bsz, s_len, MLSTM_HEADS, MLSTM_HEAD_DIM)
    hm = _mlstm_chunkwise(q_m.reshape(hd), k_m.reshape(hd) * (MLSTM_HEAD_DIM ** -0.5),
                          v_m.reshape(hd), i_pre.astype(jnp.float32),
                          jax.nn.log_sigmoid(f_pre.astype(jnp.float32)))
    hm = _rmsnorm(hm, head_norm_w.reshape(MLSTM_HEADS, MLSTM_HEAD_DIM))
    y_mlstm = jax.nn.sigmoid(o_m) * hm.reshape(bsz, s_len, MLSTM_W)

    merged = (jax.nn.sigmoid(g_mla) * (y_mla @ w_br_mla)
              + jax.nn.sigmoid(g_mlstm) * (y_mlstm @ w_br_mlstm))
    return merged @ w_out


def _conv_ffn(h, w_up, conv_w, conv_b, w_down):
    u = _causal_dwconv(h @ w_up, conv_w, conv_b)
    a, val = jnp.split(u, 2, axis=-1)
    return (jax.nn.gelu(a) * val) @ w_down


def setup_inputs(seed: int = 0) -> dict:
    key = jax.random.key(seed)
    ks = jax.random.split(key, 32)
    f32 = jnp.float32

    def nrm(k, shape, scale):
        return jax.random.normal(k, shape, f32) * scale

    def gain(k, shape):
        return 1.0 + 0.02 * jax.random.normal(k, shape, f32)

    offsets = jax.random.randint(ks[2], (BATCH, 1), 0, 4096, dtype=jnp.int32)
    positions = offsets + jnp.arange(SEQ, dtype=jnp.int32)[None, :]
    gate_b = jnp.concatenate([
        0.1 * jax.random.normal(ks[12], (DEPTH, MLSTM_HEADS), f32),
        3.0 + 3.0 * jax.random.uniform(ks[13], (DEPTH, MLSTM_HEADS), f32)], axis=-1)
    return {
        "x": nrm(ks[0], (BATCH, SEQ, D_MODEL), 1.0),
        "c": nrm(ks[1], (BATCH, D_MODEL), 1.0),
        "positions": positions,
        "ada_w": nrm(ks[3], (DEPTH, D_MODEL, 6 * D_MODEL), 0.5 * D_MODEL ** -0.5),
        "ada_b": nrm(ks[4], (DEPTH, 6 * D_MODEL), 0.02),
        "norm_mix_w": gain(ks[5], (DEPTH, D_MODEL)),
        "w_in": nrm(ks[6], (DEPTH, D_MODEL, IN_COLS), D_MODEL ** -0.5),
        "q_norm_w": gain(ks[7], (DEPTH, Q_LORA)),
        "kv_norm_w": gain(ks[8], (DEPTH, KV_LORA)),
        "w_uq": nrm(ks[9], (DEPTH, Q_LORA, MLA_HEADS * (QK_NOPE + QK_ROPE)), Q_LORA ** -0.5),
        "w_ukv": nrm(ks[10], (DEPTH, KV_LORA, MLA_HEADS * (QK_NOPE + V_HEAD)), KV_LORA ** -0.5),
        "mlstm_conv_w": nrm(ks[11], (DEPTH, MLSTM_CONV, 2 * MLSTM_W), MLSTM_CONV ** -0.5),
        "mlstm_conv_b": nrm(ks[14], (DEPTH, 2 * MLSTM_W), 0.02),
        "mlstm_gate_b": gate_b,
        "mlstm_head_norm_w": gain(ks[15], (DEPTH, MLSTM_W)),
        "w_br_mla": nrm(ks[16], (DEPTH, MLA_W, D_MODEL), MLA_W ** -0.5),
        "w_br_mlstm": nrm(ks[17], (DEPTH, MLSTM_W, D_MODEL), MLSTM_W ** -0.5),
        "w_out": nrm(ks[18], (DEPTH, D_MODEL, D_MODEL), D_MODEL ** -0.5),
        "norm_ffn_w": gain(ks[19], (DEPTH, D_MODEL)),
        "ffn_w_up": nrm(ks[20], (DEPTH, D_MODEL, 2 * D_FF), D_MODEL ** -0.5),
        "ffn_conv_w": nrm(ks[21], (DEPTH, FFN_CONV, 2 * D_FF), FFN_CONV ** -0.5),
        "ffn_conv_b": nrm(ks[22], (DEPTH, 2 * D_FF), 0.02),
        "ffn_w_down": nrm(ks[23], (DEPTH, D_FF, D_MODEL), D_FF ** -0.5),
        "final_norm_w": gain(ks[24], (D_MODEL,)),
    }


def reference(x, c, positions, ada_w, ada_b, norm_mix_w, w_in, q_norm_w, kv_norm_w,
              w_uq, w_ukv, mlstm_conv_w, mlstm_conv_b, mlstm_gate_b, mlstm_head_norm_w,
              w_br_mla, w_br_mlstm, w_out, norm_ffn_w, ffn_w_up, ffn_conv_w, ffn_conv_b,
              ffn_w_down, final_norm_w):
    c_act = jax.nn.silu(c)
    for l in range(DEPTH):
        mod = (c_act @ ada_w[l] + ada_b[l])[:, None, :]
        sh1, sc1, g1, sh2, sc2, g2 = jnp.split(mod, 6, axis=-1)
        h = _rmsnorm(x, norm_mix_w[l]) * (1 + sc1) + sh1
        x = x + g1 * _mixer(h, positions, w_in[l], q_norm_w[l], kv_norm_w[l], w_uq[l], w_ukv[l],
                            mlstm_conv_w[l], mlstm_conv_b[l], mlstm_gate_b[l],
                            mlstm_head_norm_w[l], w_br_mla[l], w_br_mlstm[l], w_out[l])
        h = _rmsnorm(x, norm_ffn_w[l]) * (1 + sc2) + sh2
        x = x + g2 * _conv_ffn(h, ffn_w_up[l], ffn_conv_w[l], ffn_conv_b[l], ffn_w_down[l])
    return _rmsnorm(x, final_norm_w)
```

```python
import contextlib
import numpy as np
import concourse.bass as bass
import concourse.mybir as mybir
from concourse.bass_utils import run_bass_kernel_spmd

F32 = mybir.dt.float32
BF16 = mybir.dt.bfloat16
I32 = mybir.dt.int32
AF = mybir.ActivationFunctionType
ALU = mybir.AluOpType
AX = mybir.AxisListType

ENG = ["pe", "act", "dve", "pool", "sp"]


class Sched:
    def __init__(self):
        self.ops = {e: [] for e in ENG}
        self.space = {}
        self.seen = {e: {} for e in ENG}
        self.dcount = {}
        self.outkeys = set()
        self.marks = []

    def mark(self, label):
        self.marks.append((label, {e: len(v) for e, v in self.ops.items()}))

    def _overl(self, name, lo, hi):
        return [en for en in self.space.get(name, []) if en[0] < hi and lo < en[1]]

    def add(self, eng, fn, reads=(), writes=(), dma=None, out=False):
        writes = list(writes) + [r for r in reads if r[0].startswith("ps") and r not in writes]
        raw = {}
        oth = {}

        def merge(dst, src):
            for k, v in src.items():
                if dst.get(k, -1) < v:
                    dst[k] = v

        for (name, lo, hi) in reads:
            for en in self._overl(name, lo, hi):
                merge(raw, en[2])
        for (name, lo, hi) in writes:
            for en in self._overl(name, lo, hi):
                merge(oth, en[2])
                merge(oth, en[3])
        deps = {}
        is_dma = dma is not None
        for src, israw in ((raw, True), (oth, False)):
            for k, v in src.items():
                if k[0] == "e" and k[1] == eng and not is_dma:
                    if eng == "pe" or not israw:
                        continue
                if deps.get(k, -1) < v:
                    deps[k] = v
        waits = []
        seen = self.seen[eng]
        for k, v in deps.items():
            if k[0] == "d":
                v = self.dcount[k[1]]
            if seen.get(k, -1) >= v:
                continue
            seen[k] = v
            waits.append((k, v))
            if k[0] == "e":
                self.ops[k[1]][v]["needed"] = True
        idx = len(self.ops[eng])
        if is_dma:
            self.dcount[dma] = self.dcount.get(dma, 0) + 1
            tok = (("d", dma), self.dcount[dma])
            if out:
                self.outkeys.add(dma)
        else:
            tok = (("e", eng), idx)
        import sys as _sys
        f = _sys._getframe(1)
        src = []
        for _ in range(4):
            if f is None:
                break
            src.append(f.f_lineno)
            f = f.f_back
        self.ops[eng].append(dict(fn=fn, waits=waits, needed=False, dma=dma, src=tuple(src)))
        for (name, lo, hi) in reads:
            lst = self.space.setdefault(name, [])
            for en in lst:
                if en[0] == lo and en[1] == hi:
                    if en[3].get(tok[0], -1) < tok[1]:
                        en[3][tok[0]] = tok[1]
                    break
            else:
                lst.append([lo, hi, {}, {tok[0]: tok[1]}])
        for (name, lo, hi) in writes:
            lst = self.space.setdefault(name, [])
            lst[:] = [en for en in lst if not (lo <= en[0] and en[1] <= hi)]
            lst.append([lo, hi, {tok[0]: tok[1]}, {}])

    def emit(self, nc, final_eng="sp"):
        fw = []
        for key, cnt in self.dcount.items():
            fw.append((("d", key), cnt))
        self.ops[final_eng].append(dict(fn=None, waits=fw, needed=False, dma=None))
        with contextlib.ExitStack() as st:
            esem = {e: st.enter_context(nc.semaphore("se_" + e)) for e in ENG}
            dsem = {k: st.enter_context(nc.semaphore("sd_%d" % i))
                    for i, k in enumerate(self.dcount.keys())}
            val = {}
            for e in ENG:
                c = 0
                for i, op in enumerate(self.ops[e]):
                    if op["needed"]:
                        c += 1
                        val[(e, i)] = c
            block = st.enter_context(nc.Block())

            def run(e, engine):
                for i, op in enumerate(self.ops[e]):
                    for (k, v) in op["waits"]:
                        if k[0] == "e":
                            engine.wait_ge(esem[k[1]], val[(k[1], v)])
                        else:
                            engine.wait_ge(dsem[k[1]], 16 * v)
                    if op["fn"] is None:
                        continue
                    ins = op["fn"](engine)
                    if op["dma"] is not None:
                        ins.then_inc(dsem[op["dma"]], 16)
                    elif op["needed"]:
                        ins.then_inc(esem[e], 1)

            @block.sync
            def _(eng):
                run("sp", eng)

            @block.scalar
            def _(eng):
                run("act", eng)

            @block.vector
            def _(eng):
                run("dve", eng)

            @block.gpsimd
            def _(eng):
                run("pool", eng)

            @block.tensor
            def _(eng):
                run("pe", eng)


D = 1024
NH = 8
QL, KVL, ROPE = 384, 256, 32
DFF = 2816
EPS = 1e-6
TT = 512
NPF = 285
NPR = 2576
W_SHAPES = dict(w_in=(8, 4944), w_uq=(3, 1536), w_ukv=(2, 1024), w_bra=(4, 1024),
                w_brb=(4, 1024), w_out=(8, 1024), w_up=(8, 5632), w_down=(22, 1024))
W_ORDER = ["w_in", "w_uq", "w_ukv", "w_bra", "w_brb", "w_out", "w_up", "w_down"]
C_CQ, C_CKV, C_KR, C_KRS, C_QM, C_KM, C_VM, C_OM, C_IF, C_GA, C_GB = (
    0, 384, 640, 736, 832, 1344, 1856, 2368, 2880, 2896, 3920)


def _prod(s):
    r = 1
    for a in s:
        r *= a
    return r


class Buf:
    def __init__(self, SB, off, shape, dt):
        self.es = 1 if dt == BF16 else 2
        self.shape = tuple(shape)
        n = _prod(shape)
        self.off = off
        self.units = n * self.es
        flat = SB[:, off:off + self.units]
        if dt != BF16:
            flat = flat.bitcast(dt)
        self.flat = flat
        if len(shape) == 1:
            self.v = flat
        elif len(shape) == 2:
            self.v = flat.rearrange("p (a b) -> p a b", b=shape[1])
        else:
            self.v = flat.rearrange("p (a b c) -> p a b c", b=shape[1], c=shape[2])
        self.inner = (n // shape[0]) * self.es

    def r(self, i0=None, i1=None):
        if i0 is None:
            return ("SB", self.off, self.off + self.units)
        if i1 is None:
            i1 = i0 + 1
        return ("SB", self.off + i0 * self.inner, self.off + i1 * self.inner)


class _Stop(Exception):
    pass


def build_nc(S_LEN):
    import os
    STOP = int(os.environ.get("KSTOP", "99"))
    holder = {}
    global _LAST_HOLDER
    _LAST_HOLDER = holder
    try:
        _build_nc(S_LEN, STOP, holder)
    except _Stop:
        pass
    return holder["nc"], holder["S"]


def _build_nc(S_LEN, STOP, holder):
    import os
    NT = S_LEN // TT
    NB = S_LEN // 128
    nc = bass.Bass("TRN2", target_bir_lowering=False)
    S = Sched()
    holder["nc"] = nc
    holder["S"] = S
    din = {}

    def inp(name, shape, dt=F32):
        din[name] = nc.dram_tensor(name, list(shape), dt, kind="ExternalInput").ap()
        return din[name]

    x_in = inp("x", [S_LEN, D])
    cT_in = inp("cT", [128, 8])
    pos_in = inp("pos", [1, S_LEN], I32)
    cst_in = inp("cst", [128, 328])
    pfm_in = inp("pfm", [2, 128, NPF])
    prow_in = inp("prow", [2, 1, NPR])
    fnw_in = inp("fnw", [1, D])
    ada_in = inp("ada_w", [2, 128, 8, 6144])
    w_in_d = {n: inp(n, [2, 128, kc, N]) for n, (kc, N) in W_SHAPES.items()}
    out_d = nc.dram_tensor("out", [S_LEN, D], F32, kind="ExternalOutput").ap()
    ws_d = {n: nc.dram_tensor("ws_" + n, [2, 128, kc, N], BF16, kind="Internal").ap()
            for n, (kc, N) in W_SHAPES.items()}
    kc_d = nc.dram_tensor("kcache", [NH, 96, S_LEN], BF16, kind="Internal").ap()
    vc_d = nc.dram_tensor("vcache", [NH, 128, NB, 65], BF16, kind="Internal").ap()
    xmid_d = nc.dram_tensor("xmid", [S_LEN, D], F32, kind="Internal").ap()

    with contextlib.ExitStack() as st:
        NUNITS = 106400
        SB = st.enter_context(nc.sbuf_tensor("SB", [128, NUNITS], BF16))
        PS = [st.enter_context(nc.psum_tensor("ps%d" % i, [128, 512], F32)) for i in range(8)]
        ptr = [0]

        def alloc(shape, dt):
            b = Buf(SB, ptr[0], shape, dt)
            ptr[0] += b.units + (b.units & 1)
            assert ptr[0] <= NUNITS, ptr[0]
            return b

        def pr(i):
            return ("ps%d" % i, 0, 512)

        def mm(pi, out, lhsT, lr, rhs, rr, start, stop, skip=False):
            if skip:
                S.add("pe", lambda e: e.matmul(out, lhsT=lhsT, rhs=rhs, start=start, stop=stop, skip_group_check=True),
                      reads=[lr, rr], writes=[pr(pi)])
            else:
                S.add("pe", lambda e: e.matmul(out, lhsT=lhsT, rhs=rhs, start=start, stop=stop),
                      reads=[lr, rr], writes=[pr(pi)])

        def actf(out, wr, in_, rd, func, scale=1.0, bias=None, accum=None, eng="act"):
            kw = {}
            if bias is not None:
                kw["bias"] = bias
            if accum is not None:
                kw["accum_out"] = accum
            S.add(eng, lambda e: e.activation(out=out, in_=in_, func=func, scale=scale, **kw),
                  reads=rd, writes=wr)

        def tt(eng, out, wr, in0, in1, rd, op):
            S.add(eng, lambda e: e.tensor_tensor(out=out, in0=in0, in1=in1, op=op), reads=rd, writes=wr)

        def ts(eng, out, wr, in0, rd, s1, s2, op0, op1=None):
            if op1 is None:
                S.add(eng, lambda e: e.tensor_scalar(out=out, in0=in0, scalar1=s1, scalar2=None, op0=op0),
                      reads=rd, writes=wr)
            else:
                S.add(eng, lambda e: e.tensor_scalar(out=out, in0=in0, scalar1=s1, scalar2=s2, op0=op0, op1=op1),
                      reads=rd, writes=wr)

        def stt(eng, out, wr, in0, scalar, in1, rd, op0, op1):
            S.add(eng, lambda e: e.scalar_tensor_tensor(out=out, in0=in0, scalar=scalar, in1=in1, op0=op0, op1=op1),
                  reads=rd, writes=wr)

        def cp(eng, out, wr, in_, rd):
            S.add(eng, lambda e: e.tensor_copy(out=out, in_=in_), reads=rd, writes=wr)

        def memset(eng, out, wr, val):
            S.add(eng, lambda e: e.memset(out, val), reads=[], writes=wr)

        def dma(eng, out, in_, rd, wr, key, outflag=False):
            S.add(eng, lambda e: e.dma_start(out=out, in_=in_), reads=rd, writes=wr, dma=key, out=outflag)

        def recip(eng, out, wr, in_, rd):
            S.add(eng, lambda e: e.reciprocal(out=out, in_=in_), reads=rd, writes=wr)

        def rsqrt_(buf_ap, res, scale, eps_):
            ts("dve", buf_ap, [res], buf_ap, [res], scale, eps_, ALU.mult, ALU.add)
            actf(buf_ap, [res], buf_ap, [res], AF.Sqrt)
            recip("dve", buf_ap, [res], buf_ap, [res])

        DBG = os.environ.get("KDBG")
        dbg_layout = {}
        holder["dbg_layout"] = dbg_layout
        if DBG:
            dbg_d = nc.dram_tensor("dbg", [128, 32768], F32, kind="ExternalOutput").ap()
            dbgbuf = alloc((4096,), F32)
        dbg_col = [0]

        def dump(name, buf, n=None):
            if not DBG or name in dbg_layout:
                return
            n = n or (buf.units // buf.es)
            c0 = dbg_col[0]
            dbg_layout[name] = (c0, n)
            done = 0
            while done < n:
                m = min(4096, n - done)
                cp("dve", dbgbuf.flat[:, 0:m], [dbgbuf.r()], buf.flat[:, done:done + m], [buf.r()])
                dma("pool", dbg_d[:, c0 + done:c0 + done + m], dbgbuf.flat[:, 0:m], [dbgbuf.r()],
                    [("dbg", c0 + done, c0 + done + m)], "dbgst", outflag=True)
                done += m
            dbg_col[0] += n

        def ck(k, T0=0):
            S.mark("ck%d" % k)
            if STOP == k:
                if k >= 2:
                    for s_ in range(4):
                        dma("pool", out_d[T0 + s_ * 128:T0 + (s_ + 1) * 128, :], xt.v[:, s_, :], [xt.r(s_)],
                            [("out", 0, 1)], "xst", outflag=True)
                S.emit(nc)
                raise _Stop()

        grot = [0]

        def G():
            i = 2 + grot[0] % 6
            grot[0] += 1
            return i

        evr = [0]

        def EV():
            evr[0] += 1
            return "act" if evr[0] % 2 else "dve"

        def evac(out, wr, in_, rd, eng=None, scale=None):
            eng = eng or EV()
            if eng == "act":
                actf(out, wr, in_, rd, AF.Copy if scale is None else AF.Copy, scale=1.0 if scale is None else scale)
            else:
                if scale is None:
                    cp("dve", out, wr, in_, rd)
                else:
                    ts("dve", out, wr, in_, rd, scale, None, ALU.mult)

        cst = alloc((328,), F32)
        identf = cst.v[:, 0:128]
        triUf = cst.v[:, 128:256]
        identb = alloc((128,), BF16)
        onesb = alloc((128,), BF16)
        onesf = alloc((128,), F32)
        triUb = alloc((128,), BF16)
        pfm = alloc((NPF,), F32)
        modT = alloc((48,), F32)
        gm = alloc((16,), F32)
        hnw_bc = alloc((512,), F32)
        gateb_bc = alloc((16,), F32)
        adabg_bc = alloc((2048,), F32)
        gbc = adabg_bc
        fnw_bc = alloc((1024,), F32)
        cact = alloc((8,), F32)
        cactb = alloc((8,), BF16)
        cbc = alloc((8, 128), F32)
        S32 = alloc((4, 65), F32)
        Sbf2 = [alloc((4, 65), BF16) for _ in range(2)]
        qkhalo = alloc((8, 3), F32)
        ftail = [alloc((44, 2), F32) for _ in range(2)]
        NW = 5
        wring = [alloc((4096,), BF16) for _ in range(NW)]
        KH = [alloc((S_LEN,), BF16) for _ in range(2)]
        VH = [alloc((NB, 65), BF16) for _ in range(2)]
        xtb = [alloc((4, D), F32) for _ in range(2)]
        xt = xtb[0]
        hT = alloc((8, TT), BF16)
        yTa = alloc((4, TT), BF16)
        yTb = alloc((4, TT), BF16)
        ARENA = ptr[0]

        def phase():
            ptr[0] = ARENA

        wr_i = [0]

        def wload(l, name, k0, k1, c0, c1, src=None, eng="sp"):
            slot = wr_i[0] % NW
            wr_i[0] += 1
            n = (k1 - k0) * (c1 - c0)
            assert n <= 4096
            view = wring[slot].flat[:, 0:n].rearrange("p (k n) -> p k n", n=c1 - c0)
            if src is None:
                dma(eng, view, ws_d[name][l, :, k0:k1, c0:c1], [("ws_%d_%s" % (l, name), 0, 64)],
                    [wring[slot].r()], "wr%d" % slot)
            else:
                dma(eng, view, src[l, :, k0:k1, c0:c1], [], [wring[slot].r()], "wr%d" % slot)
            return view, wring[slot].r()

        dma("sp", cst.flat, cst_in, [], [cst.r()], "cst")
        cp("dve", identb.flat, [identb.r()], identf, [cst.r()])
        cp("dve", triUb.flat, [triUb.r()], triUf, [cst.r()])
        memset("dve", onesb.flat, [onesb.r()], 1.0)
        memset("dve", onesf.flat, [onesf.r()], 1.0)
        dma("sp", fnw_bc.flat, fnw_in.partition_broadcast(128), [], [fnw_bc.r()], "fnw")
        dma("sp", cact.flat, cT_in, [], [cact.r()], "cact")
        actf(cact.flat, [cact.r()], cact.flat, [cact.r()], AF.Silu)
        cp("dve", cactb.flat, [cactb.r()], cact.flat, [cact.r()])
        for k in range(8):
            cp("dve", cbc.v[:, k, :], [cbc.r(k)], cact.flat[:, k:k + 1].to_broadcast([128, 128]), [cact.r()])
        cast_list = {l_: [(name, c) for name in W_ORDER for c in range(W_SHAPES[name][0])] for l_ in range(2)}

        def emit_casts(l, n=None):
            lst = cast_list[l]
            k = len(lst) if n is None else min(n, len(lst))
            for (name, c) in lst[:k]:
                dma("pool", ws_d[name][l, :, c, :], w_in_d[name][l, :, c, :], [],
                    [("ws_%d_%s" % (l, name), c, c + 1)], "ws_%d_%s" % (l, name))
            del lst[:k]

        for s_ in range(4):
            dma("pool", xtb[0].v[:, s_, :], x_in[s_ * 128:(s_ + 1) * 128, :], [], [xtb[0].r(s_)], "xt0")
        emit_casts(0, 13)

        SCALE = float((64 + 32) ** -0.5)
        ck(0)

        for l in range(2):
            x_src = x_in if l == 0 else xmid_d
            x_dst = xmid_d if l == 0 else out_d
            xsrc_res = [] if l == 0 else [("xmid", 0, S_LEN)]
            phase()
            dma("sp", pfm.flat, pfm_in[l], [], [pfm.r()], "pfm")
            dma("sp", hnw_bc.flat, prow_in[l, :, 16:528].partition_broadcast(128), [], [hnw_bc.r()], "hnw")
            dma("sp", gateb_bc.flat, prow_in[l, :, 0:16].partition_broadcast(128), [], [gateb_bc.r()], "gateb")
            dma("sp", adabg_bc.flat, prow_in[l, :, 528:2576].partition_broadcast(128), [], [adabg_bc.r()], "adabg")
            nmw = pfm.v[:, 0:8]
            nfw = pfm.v[:, 8:16]
            qnw = pfm.v[:, 16:19]
            kvnw = pfm.v[:, 19:21]
            mcw = pfm.v[:, 21:53].rearrange("p (c j) -> p c j", j=4)
            mcb = pfm.v[:, 53:61]
            fcw = pfm.v[:, 61:193].rearrange("p (c j) -> p c j", j=3)
            fcb = pfm.v[:, 193:237]
            adabT = pfm.v[:, 237:285]
            pT = 0
            for j in range(12):
                slabs_ = []
                for hk in range(2):
                    slot = wr_i[0] % NW
                    wr_i[0] += 1
                    slab = wring[slot].flat[:, 0:4096].bitcast(F32).rearrange("p (k n) -> p k n", n=512)
                    sr = wring[slot].r()
                    dma("sp", slab, ada_in[l, :, hk * 4:(hk + 1) * 4, j * 512:(j + 1) * 512], [], [sr], "wr%d" % slot)
                    slabs_.append((slab, sr))
                for m in range(4):
                    col = j * 4 + m
                    for kc in range(8):
                        slab, sr = slabs_[kc // 4]
                        mm(pT, PS[pT][:, col:col + 1], slab[:, kc % 4, m * 128:(m + 1) * 128], sr,
                           cact.flat[:, kc:kc + 1], cact.r(), kc == 0, kc == 7)
                if j in (4, 5, 10, 11):
                    gi = {4: 0, 5: 1, 10: 2, 11: 3}[j]
                    pb = G()
                    for kc in range(8):
                        slab, sr = slabs_[kc // 4]
                        mm(pb, PS[pb][:, :], cbc.v[:, kc, :], cbc.r(), slab[:, kc % 4, :], sr, kc == 0, kc == 7)
                    tt("dve", gbc.flat[:, gi * 512:(gi + 1) * 512], [gbc.r()], PS[pb][:, :],
                       gbc.flat[:, gi * 512:(gi + 1) * 512], [pr(pb), gbc.r()], ALU.add)
            tt("dve", modT.flat, [modT.r()], PS[pT][:, 0:48], adabT, [pr(pT), pfm.r()], ALU.add)
            stt("dve", gm.flat[:, 0:8], [gm.r()], modT.flat[:, 8:16], 1.0, nmw, [modT.r(), pfm.r()], ALU.add, ALU.mult)
            stt("dve", gm.flat[:, 8:16], [gm.r()], modT.flat[:, 32:40], 1.0, nfw, [modT.r(), pfm.r()], ALU.add, ALU.mult)
            ck(1)
            memset("dve", S32.flat, [S32.r()], 0.0)
            memset("dve", Sbf2[0].flat, [Sbf2[0].r()], 0.0)
            memset("dve", qkhalo.flat, [qkhalo.r()], 0.0)
            memset("dve", ftail[0].flat, [ftail[0].r()], 0.0)

            def norm_stage(gcol, shcol, xs):
                ss = alloc((4,), F32)
                junk = alloc((D,), BF16)
                for s in range(4):
                    actf(junk.flat, [junk.r()], xt.v[:, s, :], [xt.r(s)], AF.Square, accum=ss.flat[:, s:s + 1],
                         )
                    S.ops["act"][-1]
                return ss, junk

            for t in range(NT):
                T0 = t * TT
                phase()
                xt = xtb[(l * NT + t) % 2]

                def do_norm(gcol, shcol):
                    xs = alloc((4, D), BF16)
                    ss = alloc((4,), F32)
                    junk = alloc((D,), BF16)
                    for s in range(4):
                        actf(junk.flat, [junk.r(), ss.r()], xt.v[:, s, :], [xt.r(s)], AF.Square,
                             accum=ss.flat[:, s:s + 1])
                    rsqrt_(ss.flat, ss.r(), 1.0 / D, EPS)
                    for s in range(4):
                        if s % 2:
                            ts("dve", xs.v[:, s, :], [xs.r(s)], xt.v[:, s, :], [xt.r(s), ss.r()],
                               ss.flat[:, s:s + 1], None, ALU.mult)
                        else:
                            actf(xs.v[:, s, :], [xs.r(s)], xt.v[:, s, :], [xt.r(s), ss.r()], AF.Copy,
                                 scale=ss.flat[:, s:s + 1])
                    for c in range(8):
                        pi = G()
                        for s in range(4):
                            mm(pi, PS[pi][:, s * 128:(s + 1) * 128], xs.v[:, s, c * 128:(c + 1) * 128], xs.r(s),
                               identb.flat, identb.r(), True, True)
                        if c % 2:
                            actf(hT.v[:, c, :], [hT.r(c)], PS[pi][:, :], [pr(pi), gm.r(), modT.r()], AF.Identity,
                                 scale=gm.flat[:, gcol + c:gcol + c + 1], bias=modT.flat[:, shcol + c:shcol + c + 1])
                        else:
                            ts("dve", hT.v[:, c, :], [hT.r(c)], PS[pi][:, :], [pr(pi), gm.r(), modT.r()],
                               gm.flat[:, gcol + c:gcol + c + 1], modT.flat[:, shcol + c:shcol + c + 1],
                               ALU.mult, ALU.add)

                do_norm(0, 0)
                dump("hT", hT)
                ck(2, T0)

                phase()
                cqw = alloc((3, TT), BF16)
                sq = alloc((3, TT), BF16)
                ckvw = alloc((2, TT), BF16)
                rstdq = alloc((TT,), F32)
                rstdkv = alloc((TT,), F32)
                rstdkvt = alloc((4,), F32)
                rden = [alloc((TT,), F32) for _ in range(2)]
                posi = Buf(SB, rden[0].off, (TT,), I32)
                ang = alloc((TT,), F32)
                angk = alloc((TT,), F32)
                anki = Buf(SB, rden[1].off, (TT,), I32)
                cosT = alloc((TT,), F32)
                sinT = alloc((TT,), F32)
                cosr = alloc((TT,), F32)
                sinr = alloc((TT,), F32)
                t1 = alloc((TT,), F32)
                t2 = alloc((TT,), F32)
                QT = alloc((NH, TT), BF16)
                KTc = alloc((NH, TT), BF16)
                Vc = alloc((NH, 4, 65), BF16)
                PT = [alloc((TT,), BF16) for _ in range(4)]
                accsb = [alloc((TT,), F32) for _ in range(2)]
                for a_ in accsb:
                    memset("pool", a_.flat, [a_.r()], 0.0)
                RP = slice(64, 96)
                if t > 0:
                    for hh_ in range(2):
                        dma("pool", KH[hh_].flat[0:96, 0:T0], kc_d[hh_, :, 0:T0], [("kc", 0, T0)], [KH[hh_].r()], "kh%d" % hh_)
                        dma("pool", VH[hh_].v[:, 0:4 * t, :], vc_d[hh_, :, 0:4 * t, :], [("vc", 0, 4 * t)], [VH[hh_].r()], "vh%d" % hh_)
                dma("pool", posi.flat[RP, :], pos_in[:, T0:T0 + TT].partition_broadcast(32), [], [posi.r()], "posi")
                cp("dve", ang.flat[RP, :], [ang.r()], posi.flat[RP, :], [posi.r()])
                for (tab, phcol) in ((cosT, 257), (sinT, 258)):
                    ts("dve", angk.flat[RP, :], [angk.r()], ang.flat[RP, :], [ang.r(), cst.r()],
                       cst.v[RP, 256:257], cst.v[RP, phcol:phcol + 1], ALU.mult, ALU.add)
                    ts("dve", anki.flat[RP, :], [anki.r()], angk.flat[RP, :], [angk.r()],
                       float(1.0 / (2 * np.pi)), None, ALU.mult)
                    cp("dve", tab.flat[RP, :], [tab.r()], anki.flat[RP, :], [anki.r()])
                    stt("dve", angk.flat[RP, :], [angk.r()], tab.flat[RP, :], float(-2 * np.pi), angk.flat[RP, :],
                        [tab.r(), angk.r()], ALU.mult, ALU.add)
                    S.add("dve", lambda e, tab=tab: e.tensor_single_scalar(out=tab.flat[RP, :], in_=angk.flat[RP, :],
                                                                          scalar=float(np.pi), op=ALU.is_gt),
                          reads=[angk.r()], writes=[tab.r()])
                    stt("dve", angk.flat[RP, :], [angk.r()], tab.flat[RP, :], float(-2 * np.pi), angk.flat[RP, :],
                        [tab.r(), angk.r()], ALU.mult, ALU.add)
                    actf(tab.flat[RP, :], [tab.r()], angk.flat[RP, :], [angk.r()], AF.Sin)
                ck(21, T0)
                sq2 = alloc((2, TT), BF16)
                slabA, rA = wload(l, "w_in", 0, 8, C_CQ, C_CQ + 384)
                slabB, rB = wload(l, "w_in", 0, 8, C_CKV, C_CKV + 448)
                for m in range(3):
                    pi = G()
                    for kc in range(8):
                        mm(pi, PS[pi][:, :], slabA[:, kc, m * 128:(m + 1) * 128], rA, hT.v[:, kc, :], hT.r(kc),
                           kc == 0, kc == 7)
                    ts("dve", cqw.v[:, m, :], [cqw.r(m)], PS[pi][:, :], [pr(pi), pfm.r()], qnw[:, m:m + 1], None, ALU.mult)
                    actf(sq.v[:, m, :], [sq.r(m)], PS[pi][:, :], [pr(pi)], AF.Square)
                for m in range(2):
                    pi = G()
                    for kc in range(8):
                        mm(pi, PS[pi][:, :], slabB[:, kc, m * 128:(m + 1) * 128], rB, hT.v[:, kc, :], hT.r(kc),
                           kc == 0, kc == 7)
                    ts("dve", ckvw.v[:, m, :], [ckvw.r(m)], PS[pi][:, :], [pr(pi), pfm.r()], kvnw[:, m:m + 1], None, ALU.mult)
                    actf(sq2.v[:, m, :], [sq2.r(m)], PS[pi][:, :], [pr(pi)], AF.Square)
                pk1 = G()
                pk2 = G()
                for kc in range(8):
                    mm(pk1, PS[pk1][0:96, :], slabB[:, kc, 256:352], rB, hT.v[:, kc, :], hT.r(kc), kc == 0, kc == 7)
                for kc in range(8):
                    mm(pk2, PS[pk2][0:96, :], slabB[:, kc, 352:448], rB, hT.v[:, kc, :], hT.r(kc), kc == 0, kc == 7)
                tt("dve", t1.flat[RP, :], [t1.r()], PS[pk1][RP, :], cosT.flat[RP, :], [pr(pk1), cosT.r()], ALU.mult)
                tt("dve", t2.flat[RP, :], [t2.r()], PS[pk2][RP, :], sinT.flat[RP, :], [pr(pk2), sinT.r()], ALU.mult)
                tt("dve", t1.flat[RP, :], [t1.r()], t1.flat[RP, :], t2.flat[RP, :], [t1.r(), t2.r()], ALU.add)
                for h in range(NH):
                    cp("pool" if h % 2 else "dve", KTc.v[RP, h, :], [KTc.r(h)], t1.flat[RP, :], [t1.r()])
                pq = G()
                for m in range(3):
                    mm(pq, PS[pq][:, :], onesb.flat, onesb.r(), sq.v[:, m, :], sq.r(m), m == 0, m == 2)
                cp("dve", rstdq.flat, [rstdq.r()], PS[pq][:, :], [pr(pq)])
                pkv = G()
                for m in range(2):
                    mm(pkv, PS[pkv][:, :], onesb.flat, onesb.r(), sq2.v[:, m, :], sq2.r(m), m == 0, m == 1)
                cp("dve", rstdkv.flat, [rstdkv.r()], PS[pkv][:, :], [pr(pkv)])
                pkt = G()
                for s in range(4):
                    for m in range(2):
                        mm(pkt, PS[pkt][:, s:s + 1], sq2.v[:, m, s * 128:(s + 1) * 128], sq2.r(m), onesb.flat[:, 0:1], onesb.r(),
                           m == 0, m == 1)
                cp("dve", rstdkvt.flat, [rstdkvt.r()], PS[pkt][:, 0:4], [pr(pkt)])
                rsqrt_(rstdq.flat, rstdq.r(), 1.0 / QL, EPS)
                rsqrt_(rstdkv.flat, rstdkv.r(), 1.0 / KVL, EPS)
                rsqrt_(rstdkvt.flat, rstdkvt.r(), 1.0 / KVL, EPS)
                tt("dve", cosr.flat[RP, :], [cosr.r()], cosT.flat[RP, :], rstdq.flat[RP, :], [cosT.r(), rstdq.r()], ALU.mult)
                tt("dve", sinr.flat[RP, :], [sinr.r()], sinT.flat[RP, :], rstdq.flat[RP, :], [sinT.r(), rstdq.r()], ALU.mult)
                ck(215, T0)
                slabKV, rKV = wload(l, "w_ukv", 0, 2, 0, 1024)
                for h in range(NH):
                    pi = G()
                    for kc in range(2):
                        mm(pi, PS[pi][0:64, :], slabKV[:, kc, h * 64:(h + 1) * 64], rKV, ckvw.v[:, kc, :], ckvw.r(kc),
                           kc == 0, kc == 1)
                    tt("dve", KTc.v[0:64, h, :], [KTc.r(h)], PS[pi][0:64, :], rstdkv.flat[0:64, :], [pr(pi), rstdkv.r()], ALU.mult)
                memset("pool", Vc.v[:, :, :, 64:65], [Vc.r()], 1.0)
                for s in range(4):
                    pi = G()
                    for kc in range(2):
                        mm(pi, PS[pi][:, :], ckvw.v[:, kc, s * 128:(s + 1) * 128], ckvw.r(kc), slabKV[:, kc, 512:1024], rKV,
                           kc == 0, kc == 1)
                    ts("dve", Vc.v[:, :, s, 0:64], [Vc.r()], PS[pi][:, :].rearrange("p (h e) -> p h e", e=64),
                       [pr(pi), rstdkvt.r()], rstdkvt.flat[:, s:s + 1], None, ALU.mult)
                ck(22, T0)
                slabQ1, rQ1 = wload(l, "w_uq", 0, 3, 0, 768)
                slabQ2, rQ2 = wload(l, "w_uq", 0, 3, 768, 1536)
                tq = [(t1, t2), (ang, angk)]
                for h in range(NH):
                    p1 = G()
                    p2 = G()
                    ta, tb = tq[h % 2]
                    for kc in range(3):
                        mm(p1, PS[p1][0:96, :], slabQ1[:, kc, h * 96:(h + 1) * 96], rQ1, cqw.v[:, kc, :], cqw.r(kc),
                           kc == 0, kc == 2)
                    for kc in range(3):
                        mm(p2, PS[p2][0:96, :], slabQ2[:, kc, h * 96:(h + 1) * 96], rQ2, cqw.v[:, kc, :], cqw.r(kc),
                           kc == 0, kc == 2)
                    tt("dve", QT.v[0:64, h, :], [QT.r(h)], PS[p1][0:64, :], rstdq.flat[0:64, :], [pr(p1), rstdq.r()], ALU.mult)
                    tt("dve", ta.flat[RP, :], [ta.r()], PS[p1][RP, :], cosr.flat[RP, :], [pr(p1), cosr.r()], ALU.mult)
                    tt("dve", tb.flat[RP, :], [tb.r()], PS[p2][RP, :], sinr.flat[RP, :], [pr(p2), sinr.r()], ALU.mult)
                    tt("pool", QT.v[RP, h, :], [QT.r(h)], ta.flat[RP, :], tb.flat[RP, :], [ta.r(), tb.r()], ALU.add)
                if l == 0 and t == 0:
                    emit_casts(0)
                if t < NT - 1:
                    dma("pool", kc_d[:, :, T0:T0 + TT].rearrange("h d t -> d h t"), KTc.v[0:96, :, :], [KTc.r()],
                        [("kc", T0, T0 + TT)], "kcs")
                    dma("pool", vc_d[:, :, 4 * t:4 * t + 4, :].rearrange("h p s e -> p h s e"), Vc.v, [Vc.r()],
                        [("vc", 4 * t, 4 * t + 4)], "vcs")
                dump("QT", QT)
                dump("KTc", KTc)
                dump("Vc", Vc)
                ck(23, T0)
                LA = 2
                blocks = []
                for h in range(NH):
                    for kb in range(4 * t + 4):
                        blocks.append((h, kb))
                nkb = 4 * t + 4
                pend = {}

                def load_hist(hh):
                    sl_ = hh % 2
                    dma("pool", KH[sl_].flat[0:96, 0:T0], kc_d[hh, :, 0:T0], [("kc", 0, T0)], [KH[sl_].r()], "kh%d" % sl_)
                    dma("pool", VH[sl_].v[:, 0:4 * t, :], vc_d[hh, :, 0:4 * t, :], [("vc", 0, 4 * t)], [VH[sl_].r()], "vh%d" % sl_)

                def emit_score(i):
                    h, kb = blocks[i]
                    sl = h % 2
                    if t > 0 and kb == min(LA + 1, nkb - 1) and h + 1 < NH and h >= 1:
                        load_hist(h + 1)
                    if kb < 4 * t:
                        Ks, Kr = KH[sl].flat[0:96, kb * 128:(kb + 1) * 128], KH[sl].r()
                        Vs, Vr = VH[sl].v[:, kb, :], VH[sl].r()
                        q0 = 0
                    else:
                        j = kb - 4 * t
                        Ks, Kr = KTc.v[0:96, h, j * 128:(j + 1) * 128], KTc.r(h)
                        Vs, Vr = Vc.v[:, h, j, :], Vc.r()
                        q0 = j * 128
                    pi = G()
                    mm(pi, PS[pi][:, q0:TT], Ks, Kr, QT.v[0:96, h, q0:TT], QT.r(h), True, True)
                    pt = PT[i % len(PT)]
                    actf(pt.flat[:, q0:TT], [pt.r()], PS[pi][:, q0:TT], [pr(pi)], AF.Exp, scale=SCALE)
                    if kb >= 4 * t:
                        memset("pool", pt.flat[64:128, q0:q0 + 64], [pt.r()], 0.0)
                    pend[i] = (pt, Vs, Vr, q0)

                def emit_pv(i):
                    h, kb = blocks[i]
                    pt, Vs, Vr, q0 = pend.pop(i)
                    ai = h % 2
                    mm(ai, PS[ai][0:65, q0:TT], Vs, Vr, pt.flat[:, q0:TT], pt.r(), kb == 0, kb == nkb - 1)
                    if kb == nkb - 1:
                        ab = accsb[h % 2]
                        rd = rden[h % 2]
                        cp("dve", ab.flat[0:65, :], [ab.r()], PS[ai][0:65, :], [pr(ai)])
                        pd = G()
                        mm(pd, PS[pd][0:64, :], cst.v[:, 264:328], cst.r(), ab.flat, ab.r(), True, True)
                        recip("dve", rd.flat[0:64, :], [rd.r()], PS[pd][0:64, :], [pr(pd)])
                        r0 = (h % 2) * 64
                        tt("dve", yTa.v[r0:r0 + 64, h // 2, :], [yTa.r(h // 2)], ab.flat[0:64, :], rd.flat[0:64, :],
                           [ab.r(), rd.r()], ALU.mult)

                for i in range(len(blocks) + LA):
                    if i < len(blocks):
                        emit_score(i)
                    if i >= LA:
                        emit_pv(i - LA)
                dump("yTa", yTa)
                ck(3, T0)
                phase()
                if l == 0:
                    emit_casts(1, None if t == NT - 1 else (0 if (NT > 1 and t == 0) else 10))
                qkpre = alloc((8, TT + 3), F32)
                cacc = [alloc((TT,), F32) for _ in range(2)]
                QmZ = alloc((8, TT), BF16)
                memset("pool", QmZ.flat, [QmZ.r()], 0.0)
                KmT = alloc((4, TT), BF16)
                Km = alloc((4, TT), BF16)
                Vp = alloc((4, NH, 65), BF16)
                og = alloc((4, TT), BF16)
                gpre = alloc((4, 16), F32)
                lf = alloc((4, 8), F32)
                bcs = alloc((4, 8), F32)
                uu = alloc((4, 8), F32)
                bnd = alloc((4, 8), F32)
                ebl = alloc((8,), F32)
                pmall = [[alloc((4, 128), BF16) for _ in range(2)] for _ in range(4)]
                dd = alloc((8,), F32)
                hbuf4 = alloc((4, NH, 64), F32)
                hsq = alloc((NH, 64), F32)
                hss4 = alloc((4, 8), F32)
                stmp = alloc((4, 65), F32)
                ymls = alloc((4, TT), BF16)
                slabIF, rIF = wload(l, "w_in", 0, 8, C_IF, C_IF + 16)
                pg = G()
                for s in range(4):
                    for kc in range(8):
                        mm(pg, PS[pg][:, s * 16:(s + 1) * 16], hT.v[:, kc, s * 128:(s + 1) * 128], hT.r(kc),
                           slabIF[:, kc, :], rIF, kc == 0, kc == 7)
                tt("dve", gpre.v, [gpre.r()], PS[pg][:, 0:64].rearrange("p (s g) -> p s g", g=16),
                   gateb_bc.flat.unsqueeze(1).to_broadcast([128, 4, 16]), [pr(pg), gateb_bc.r()], ALU.add)
                actf(lf.v, [lf.r()], gpre.v[:, :, 8:16], [gpre.r()], AF.Exp, scale=-1.0)
                actf(lf.v, [lf.r()], lf.v, [lf.r()], AF.Ln, bias=1.0)
                ts("dve", lf.v, [lf.r()], lf.v, [lf.r()], -1.0, None, ALU.mult)
                cp("pool", qkpre.v[:, :, 0:3], [qkpre.r()], qkhalo.v, [qkhalo.r()])
                slabsD = [wload(l, "w_in", 0, 8, C_QM + half * 512, C_QM + (half + 1) * 512) for half in range(2)]

                def qk_mm(half, pr2):
                    slabD, rD = slabsD[half]
                    pair = []
                    for cc in (2 * pr2, 2 * pr2 + 1):
                        c = half * 4 + cc
                        pi = G()
                        for kc in range(8):
                            mm(pi, PS[pi][:, :], slabD[:, kc, cc * 128:(cc + 1) * 128], rD, hT.v[:, kc, :], hT.r(kc),
                               kc == 0, kc == 7)
                        actf(qkpre.v[:, c, 3:TT + 3], [qkpre.r(c)], PS[pi][:, :], [pr(pi)], AF.Copy)
                        pair.append((half, cc, c, cacc[cc % 2]))
                    return pair

                def qk_conv(pair):
                    for (half, cc, c, ca) in pair:
                        ts("dve", ca.flat, [ca.r()], qkpre.v[:, c, 0:TT], [qkpre.r(c), pfm.r()], mcw[:, c, 0:1], mcb[:, c:c + 1],
                           ALU.mult, ALU.add)
                    for j in range(1, 4):
                        for (half, cc, c, ca) in pair:
                            stt("dve", ca.flat, [ca.r()], qkpre.v[:, c, j:TT + j], mcw[:, c, j:j + 1], ca.flat,
                                [qkpre.r(c), pfm.r(), ca.r()], ALU.mult, ALU.add)
                    for (half, cc, c, ca) in pair:
                        if half == 0:
                            actf(QmZ.v[0:64, 2 * cc, :], [QmZ.r(2 * cc)], ca.flat[0:64, :], [ca.r()], AF.Silu)
                            actf(QmZ.v[64:128, 2 * cc + 1, :], [QmZ.r(2 * cc + 1)], ca.flat[64:128, :], [ca.r()], AF.Silu)
                        else:
                            actf(KmT.v[:, cc, :], [KmT.r(cc)], ca.flat, [ca.r()], AF.Silu)

                order = [(0, 0), (0, 1), (1, 0), (1, 1)]
                infl = [qk_mm(*order[0]), qk_mm(*order[1])]
                for k_ in range(4):
                    qk_conv(infl[k_])
                    if k_ + 2 < 4:
                        infl.append(qk_mm(*order[k_ + 2]))
                cp("pool", qkhalo.v, [qkhalo.r()], qkpre.v[:, :, TT:TT + 3], [qkpre.r()])
                ck(31, T0)
                slabV, rV = wload(l, "w_in", 0, 8, C_VM, C_VM + 512)
                pb = G()
                for s in range(4):
                    mm(pb, PS[pb][:, s * 8:(s + 1) * 8], triUf, cst.r(), lf.v[:, s, :], lf.r(), True, True)
                cp("dve", bcs.v, [bcs.r()], PS[pb][:, 0:32].rearrange("p (s h) -> p s h", h=8), [pr(pb)])
                tt("dve", uu.v, [uu.r()], gpre.v[:, :, 0:8], bcs.v, [gpre.r(), bcs.r()], ALU.subtract)
                actf(uu.v, [uu.r()], uu.v, [uu.r()], AF.Exp)
                actf(bnd.v, [bnd.r()], bcs.v, [bcs.r()], AF.Exp, scale=-1.0, bias=float(np.log(8.0)))
                for s in range(4):
                    pi = G()
                    for kc in range(8):
                        mm(pi, PS[pi][:, :], hT.v[:, kc, s * 128:(s + 1) * 128], hT.r(kc), slabV[:, kc, :], rV,
                           kc == 0, kc == 7)
                    tt("dve", Vp.v[:, s, :, 0:64], [Vp.r(s)], PS[pi][:, :].rearrange("p (h e) -> p h e", e=64),
                       uu.v[:, s, :].unsqueeze(2).to_broadcast([128, NH, 64]), [pr(pi), uu.r()], ALU.mult)
                    cp("dve", Vp.v[:, s, :, 64:65], [Vp.r(s)], uu.v[:, s, :].unsqueeze(2), [uu.r()])
                slabO, rO = wload(l, "w_in", 0, 8, C_OM, C_OM + 512)
                for s in range(4):
                    pi = G()
                    for kc in range(8):
                        mm(pi, PS[pi][:, :], hT.v[:, kc, s * 128:(s + 1) * 128], hT.r(kc), slabO[:, kc, :], rO,
                           kc == 0, kc == 7)
                    actf(og.v[:, s, :], [og.r(s)], PS[pi][:, :], [pr(pi)], AF.Sigmoid)
                    tt("pool", og.v[:, s, :], [og.r(s)], og.v[:, s, :], hnw_bc.flat, [og.r(s), hnw_bc.r()], ALU.mult)
                for s in range(4):
                    pi = G()
                    for c in range(4):
                        mm(pi, PS[pi][:, c * 128:(c + 1) * 128], KmT.v[:, c, s * 128:(s + 1) * 128], KmT.r(c),
                           identb.flat, identb.r(), True, True)
                    evac(Km.v[:, s, :], [Km.r(s)], PS[pi][:, :], [pr(pi)])
                ck(32, T0)
                snaps = [Sbf2[t % 2]] + [alloc((4, 65), BF16) for _ in range(3)] + [Sbf2[(t + 1) % 2]]
                ebls = alloc((4, 8), F32)
                for s in range(4):
                    pe_ = G()
                    mm(pe_, PS[pe_][:, 0:8], onesf.flat, onesf.r(), lf.v[:, s, :], lf.r(), True, True)
                    actf(ebls.v[:, s, :], [ebls.r(s)], PS[pe_][:, 0:8], [pr(pe_)], AF.Exp)
                for s in range(4):
                    pst = []
                    for b2 in range(2):
                        pu = G()
                        pst.append(pu)
                        for cc in range(2):
                            c = b2 * 2 + cc
                            mm(pu, PS[pu][:, cc * 130:(cc + 1) * 130], Km.v[:, s, c * 128:(c + 1) * 128], Km.r(s),
                               Vp.v[:, s, 2 * c:2 * c + 2, :], Vp.r(s), True, True)
                    eblv = ebls.v[:, s, :].rearrange("p (c two) -> p c two", two=2)
                    for b2 in range(2):
                        puv = PS[pst[b2]][:, 0:260].rearrange("p (c two e) -> p c two e", two=2, e=65)
                        for hf in range(2):
                            rs_ = slice(hf * 64, hf * 64 + 64)
                            tt("dve", stmp.v[rs_, b2 * 2:b2 * 2 + 2, :], [stmp.r()], puv[rs_, :, hf, :],
                               S32.v[rs_, b2 * 2:b2 * 2 + 2, :], [pr(pst[b2]), S32.r()], ALU.add)
                    for hf in range(2):
                        rs_ = slice(hf * 64, hf * 64 + 64)
                        tt("dve", S32.v[rs_, :, :], [S32.r()], stmp.v[rs_, :, :],
                           eblv[rs_, :, hf].unsqueeze(2).to_broadcast([64, 4, 65]), [stmp.r(), ebls.r(s)], ALU.mult)
                    cp("dve", snaps[s + 1].v, [snaps[s + 1].r()], S32.v, [S32.r()])
                for s in range(4):
                    tsl = slice(s * 128, (s + 1) * 128)
                    for b2 in range(2):
                        psc = G()
                        for hh in range(4):
                            h = b2 * 4 + hh
                            c = h // 2
                            mm(psc, PS[psc][:, hh * 128:(hh + 1) * 128], KmT.v[:, c, tsl], KmT.r(c),
                               QmZ.v[:, h, tsl], QmZ.r(h), True, True)
                        tt("dve", pmall[s][b2].v, [pmall[s][b2].r()], PS[psc][:, :].rearrange("p (h t) -> p h t", t=128),
                           triUb.flat.unsqueeze(1).to_broadcast([128, 4, 128]), [pr(psc), triUb.r()], ALU.mult)
                for s in range(4):
                    tsl = slice(s * 128, (s + 1) * 128)
                    nd = []
                    pm = pmall[s]
                    for b2 in range(2):
                        pn = G()
                        nd.append(pn)
                        for hh in range(4):
                            h = b2 * 4 + hh
                            c = h // 2
                            mm(pn, PS[pn][:, hh * 65:(hh + 1) * 65], pm[b2].v[:, hh, :], pm[b2].r(), Vp.v[:, s, h, :], Vp.r(s),
                               True, False)
                            mm(pn, PS[pn][:, hh * 65:(hh + 1) * 65], QmZ.v[:, h, tsl], QmZ.r(h),
                               snaps[s].v[:, c, :], snaps[s].r(), False, True)
                    ck(326, T0)
                    for b2 in range(2):
                        ndv = PS[nd[b2]][:, 0:260].rearrange("p (h e) -> p h e", e=65)
                        dsl = dd.flat[:, b2 * 4:(b2 + 1) * 4]
                        cp("dve", dsl, [dd.r()], ndv[:, :, 64], [pr(nd[b2])])
                        stt("dve", dsl, [dd.r()], dsl, -1.0, dsl, [dd.r()], ALU.mult, ALU.max)
                        tt("dve", dsl, [dd.r()], dsl, bnd.v[:, s, b2 * 4:(b2 + 1) * 4], [dd.r(), bnd.r()], ALU.max)
                        recip("dve", dsl, [dd.r()], dsl, [dd.r()])
                        tt("dve", hbuf4.v[:, s, b2 * 4:(b2 + 1) * 4, :], [hbuf4.r(s)], ndv[:, :, 0:64],
                           dsl.unsqueeze(2).to_broadcast([128, 4, 64]), [pr(nd[b2]), dd.r()], ALU.mult)
                    ck(33, T0)
                    tt("pool", hsq.v, [hsq.r()], hbuf4.v[:, s], hbuf4.v[:, s], [hbuf4.r(s)], ALU.mult)
                    S.add("dve", lambda e, s=s: e.tensor_reduce(out=hss4.v[:, s, :], in_=hsq.v, axis=AX.X, op=ALU.add),
                          reads=[hsq.r()], writes=[hss4.r(s)])
                    ck(34, T0)
                sa_all = Buf(SB, qkpre.off, (8, TT), BF16)
                sb_all = Buf(SB, qkpre.off + 8 * TT, (8, TT), BF16)
                for jj in range(2):
                    slabGa, rGa = wload(l, "w_in", 0, 8, C_GA + jj * 512, C_GA + (jj + 1) * 512)
                    slabGb, rGb = wload(l, "w_in", 0, 8, C_GB + jj * 512, C_GB + (jj + 1) * 512)
                    for j4 in range(4):
                        j = jj * 4 + j4
                        pga = G()
                        for kc in range(8):
                            mm(pga, PS[pga][:, :], slabGa[:, kc, j4 * 128:(j4 + 1) * 128], rGa, hT.v[:, kc, :], hT.r(kc),
                               kc == 0, kc == 7)
                        actf(sa_all.v[:, j, :], [sa_all.r(j)], PS[pga][:, :], [pr(pga)], AF.Sigmoid)
                        pgb = G()
                        for kc in range(8):
                            mm(pgb, PS[pgb][:, :], slabGb[:, kc, j4 * 128:(j4 + 1) * 128], rGb, hT.v[:, kc, :], hT.r(kc),
                               kc == 0, kc == 7)
                        actf(sb_all.v[:, j, :], [sb_all.r(j)], PS[pgb][:, :], [pr(pgb)], AF.Sigmoid)
                rsqrt_(hss4.flat, hss4.r(), 1.0 / 64, EPS)
                for s in range(4):
                    tt("dve", hbuf4.v[:, s], [hbuf4.r(s)], hbuf4.v[:, s],
                       hss4.v[:, s, :].unsqueeze(2).to_broadcast([128, NH, 64]), [hbuf4.r(s), hss4.r()], ALU.mult)
                    tt("dve", ymls.v[:, s, :].rearrange("p (h e) -> p h e", e=64), [ymls.r(s)], hbuf4.v[:, s],
                       og.v[:, s, :].rearrange("p (h e) -> p h e", e=64), [hbuf4.r(s), og.r(s)], ALU.mult)
                for c in range(4):
                    pi = G()
                    for s in range(4):
                        mm(pi, PS[pi][:, s * 128:(s + 1) * 128], ymls.v[:, s, c * 128:(c + 1) * 128], ymls.r(s),
                           identb.flat, identb.r(), True, True)
                    evac(yTb.v[:, c, :], [yTb.r(c)], PS[pi][:, :], [pr(pi)])

                dump("ymls", ymls)
                ck(4, T0)
                phase()
                _skip = alloc((8240,), BF16)
                m1 = [alloc((TT,), F32) for _ in range(2)]
                mT = alloc((8, TT), BF16)
                otmp = [alloc((TT,), F32) for _ in range(2)]
                for jj in range(2):
                    slabBa, rBa = wload(l, "w_bra", 0, 4, jj * 512, (jj + 1) * 512)
                    slabBb, rBb = wload(l, "w_brb", 0, 4, jj * 512, (jj + 1) * 512)
                    for j4 in range(4):
                        j = jj * 4 + j4
                        k2 = j % 2
                        pa, pb = G(), G()
                        for kc in range(4):
                            mm(pa, PS[pa][:, :], slabBa[:, kc, j4 * 128:(j4 + 1) * 128], rBa, yTa.v[:, kc, :], yTa.r(kc),
                               kc == 0, kc == 3)
                        for kc in range(4):
                            mm(pb, PS[pb][:, :], slabBb[:, kc, j4 * 128:(j4 + 1) * 128], rBb, yTb.v[:, kc, :], yTb.r(kc),
                               kc == 0, kc == 3)
                        tt("dve", m1[k2].flat, [m1[k2].r()], PS[pa][:, :], sa_all.v[:, j, :], [pr(pa), sa_all.r(j)], ALU.mult)
                        tt("dve", sb_all.v[:, j, :], [sb_all.r(j)], PS[pb][:, :], sb_all.v[:, j, :], [pr(pb), sb_all.r(j)], ALU.mult)
                        tt("dve" if (j % 2 == 1) else "pool", mT.v[:, j, :], [mT.r(j)], m1[k2].flat, sb_all.v[:, j, :],
                           [m1[k2].r(), sb_all.r(j)], ALU.add)

                def proj_residual(lhs_buf, nk, wname, gcol0):
                    oi = 0
                    for half in range(2):
                        slabs = []
                        k0 = 0
                        while k0 < nk:
                            k1 = min(nk, k0 + 8)
                            slabs.append((k0, k1) + wload(l, wname, k0, k1, half * 512, (half + 1) * 512))
                            k0 = k1
                        for s in range(4):
                            pi = G()
                            for (k0, k1, sv, sr) in slabs:
                                for kc in range(k0, k1):
                                    mm(pi, PS[pi][:, :], lhs_buf.v[:, kc, s * 128:(s + 1) * 128], lhs_buf.r(kc),
                                       sv[:, kc - k0, :], sr, kc == 0, kc == nk - 1)
                            ot = otmp[oi % 2]
                            oi += 1
                            gsl = gbc.flat[:, gcol0 + half * 512:gcol0 + (half + 1) * 512]
                            tt("dve", ot.flat, [ot.r()], PS[pi][:, :], gsl, [pr(pi), gbc.r()], ALU.mult)
                            xsl = xt.v[:, s, half * 512:(half + 1) * 512]
                            tt("dve" if (half == 1 and s >= 2) else "pool", xsl, [xt.r(s)], xsl, ot.flat, [xt.r(s), ot.r()], ALU.add)

                dump("mT", mT)
                proj_residual(mT, 8, "w_out", 0)
                dump("x1", xt)

                ck(5, T0)
                phase()
                gt_ = l * NT + t + 1
                if gt_ < 2 * NT:
                    l2_, t2_ = divmod(gt_, NT)
                    xn_ = xtb[gt_ % 2]
                    src_ = x_in if l2_ == 0 else xmid_d
                    res_ = [] if l2_ == 0 else [("xmid", t2_ * TT, (t2_ + 1) * TT)]
                    for s in range(4):
                        dma("pool", xn_.v[:, s, :], src_[t2_ * TT + s * 128:t2_ * TT + (s + 1) * 128, :], res_,
                            [xn_.r(s)], "xt%d" % (gt_ % 2))
                do_norm(8, 24)
                NRING = 6
                sta = [alloc((TT + 2,), F32) for _ in range(NRING)]
                facc = [alloc((TT,), F32) for _ in range(NRING)]
                ga = [alloc((TT,), F32) for _ in range(2)]
                actT = alloc((22, TT), BF16)
                otmp = [alloc((TT,), F32) for _ in range(2)]
                told = ftail[t % 2]
                tnew = ftail[(t + 1) % 2]
                slab_cache = {}

                def up_mm(jp):
                    sl_ = jp // 2
                    if sl_ not in slab_cache:
                        slab_cache.clear()
                        slab_cache[sl_] = wload(l, "w_up", 0, 8, sl_ * 512, (sl_ + 1) * 512)
                    slabU, rU = slab_cache[sl_]
                    grp = []
                    for av in range(2):
                        q4 = (jp % 2) * 2 + av
                        ch = jp * 2 + av
                        pi = G()
                        for kc in range(8):
                            mm(pi, PS[pi][:, :], slabU[:, kc, q4 * 128:(q4 + 1) * 128], rU,
                               hT.v[:, kc, :], hT.r(kc), kc == 0, kc == 7)
                        k_ = (jp * 2 + av) % NRING
                        sb_, fa = sta[k_], facc[k_]
                        actf(sb_.flat[:, 2:TT + 2], [sb_.r(2, TT + 2)], PS[pi][:, :], [pr(pi)], AF.Copy)
                        actf(fa.flat, [fa.r()], PS[pi][:, :], [pr(pi), pfm.r()], AF.Identity,
                             scale=fcw[:, ch, 2:3], bias=fcb[:, ch:ch + 1])
                        grp.append((ch, sb_, fa))
                    return grp

                def up_conv(jp, grp):
                    for (ch, sb_, fa) in grp:
                        cp("pool", sb_.flat[:, 0:2], [sb_.r(0, 2)], told.v[:, ch, :], [told.r(ch)])
                        cp("pool", tnew.v[:, ch, :], [tnew.r(ch)], sb_.flat[:, TT:TT + 2], [sb_.r(TT, TT + 2)])
                    for j in range(2):
                        for (ch, sb_, fa) in grp:
                            stt("dve", fa.flat, [fa.r()], sb_.flat[:, j:TT + j], fcw[:, ch, j:j + 1], fa.flat,
                                [sb_.r(j, TT + j), pfm.r(), fa.r()], ALU.mult, ALU.add)
                    g_ = ga[jp % 2]
                    fa_a = grp[0][2]
                    fa_v = grp[1][2]
                    actf(g_.flat, [g_.r()], fa_a.flat, [fa_a.r()], AF.Gelu)
                    tt("dve" if jp >= 20 else "pool", actT.v[:, jp, :], [actT.r(jp)], g_.flat, fa_v.flat, [g_.r(), fa_v.r()], ALU.mult)

                DEPTH = 2
                inflight = {}
                for jp in range(22 + DEPTH):
                    if jp < 22:
                        inflight[jp] = up_mm(jp)
                    if jp >= DEPTH:
                        up_conv(jp - DEPTH, inflight.pop(jp - DEPTH))
                proj_residual(actT, 22, "w_down", 1024)

                ck(6, T0)
                if l == 1:
                    ss2 = alloc((4,), F32)
                    junk2 = alloc((D,), BF16)
                    for s in range(4):
                        actf(junk2.flat, [junk2.r(), ss2.r()], xt.v[:, s, :], [xt.r(s)], AF.Square,
                             accum=ss2.flat[:, s:s + 1])
                    rsqrt_(ss2.flat, ss2.r(), 1.0 / D, EPS)
                    for s in range(4):
                        stt("dve", xt.v[:, s, :], [xt.r(s)], xt.v[:, s, :], ss2.flat[:, s:s + 1], fnw_bc.flat,
                            [xt.r(s), ss2.r(), fnw_bc.r()], ALU.mult, ALU.mult)
                for s in range(4):
                    dma("pool", x_dst[T0 + s * 128:T0 + (s + 1) * 128, :], xt.v[:, s, :], [xt.r(s)],
                        [("xmid" if l == 0 else "out", T0 + s * 128, T0 + (s + 1) * 128)],
                        "xst", outflag=(l == 1))
        S.emit(nc)


def _fm(v, k):
    return np.ascontiguousarray(v.reshape(k, 128).T)


def _wp(w):
    K, N = w.shape
    return np.ascontiguousarray(w.reshape(K // 128, 128, N).transpose(1, 0, 2))


def _prep_shared(inp):
    f32 = np.float32
    sh = {}
    cst = np.zeros((128, 328), f32)
    cst[64, 264:328] = 1.0
    cst[:, 0:128] = np.eye(128, dtype=f32)
    cst[:, 128:256] = np.triu(np.ones((128, 128), f32))
    half = ROPE // 2
    inv = (10000.0 ** (-np.arange(half, dtype=np.float32) / half)).astype(f32)
    for p in range(64, 96):
        cst[p, 256] = inv[(p - 64) % half]
        cst[p, 257] = np.pi / 2
        cst[p, 258] = np.pi if p < 80 else 0.0
    sh["cst"] = cst
    o = np.cumsum([0, 384, 256, 32, 512, 512, 512, 512, 8, 8, 1024, 1024])
    cq, ckv, kr, qm, km, vm, om, im, fm, ga, gb = [np.arange(o[i], o[i + 1]) for i in range(11)]
    krs = np.concatenate([kr[half:], kr[:half]])
    idx = np.concatenate([cq, ckv, ckv[:64], kr, ckv[:64], krs, qm, km, vm, om, im, fm, ga, gb])
    assert idx.size == 4944
    uq1, uq2 = [], []
    for h in range(NH):
        b = h * 96
        uq1.append(np.arange(b, b + 96))
        uq2.append(np.concatenate([np.arange(b, b + 64), np.arange(b + 64 + half, b + 96), np.arange(b + 64, b + 64 + half)]))
    uqi = np.concatenate(uq1 + uq2)
    kvi = np.concatenate([np.arange(h * 128, h * 128 + 64) for h in range(NH)] +
                         [np.arange(h * 128 + 64, h * 128 + 128) for h in range(NH)])
    upi = np.concatenate([np.concatenate([np.arange(j * 128, (j + 1) * 128), np.arange(DFF + j * 128, DFF + (j + 1) * 128)])
                          for j in range(22)])
    L = 2
    sh["w_in"] = np.stack([_wp(inp["w_in"][l][:, idx]) for l in range(L)])
    sh["w_uq"] = np.stack([_wp(inp["w_uq"][l][:, uqi]) for l in range(L)])
    sh["w_ukv"] = np.stack([_wp(inp["w_ukv"][l][:, kvi]) for l in range(L)])
    sh["w_bra"] = np.stack([_wp(inp["w_br_mla"][l]) for l in range(L)])
    sh["w_brb"] = np.stack([_wp(inp["w_br_mlstm"][l]) for l in range(L)])
    sh["w_out"] = np.stack([_wp(inp["w_out"][l]) for l in range(L)])
    sh["w_up"] = np.stack([_wp(inp["ffn_w_up"][l][:, upi]) for l in range(L)])
    sh["w_down"] = np.stack([_wp(inp["ffn_w_down"][l]) for l in range(L)])
    sh["ada_w"] = np.stack([_wp(inp["ada_w"][l]) for l in range(L)])
    pfm = np.zeros((L, 128, NPF), f32)
    prow = np.zeros((L, 1, NPR), f32)
    for l in range(L):
        pfm[l, :, 0:8] = _fm(inp["norm_mix_w"][l], 8)
        pfm[l, :, 8:16] = _fm(inp["norm_ffn_w"][l], 8)
        pfm[l, :, 16:19] = _fm(inp["q_norm_w"][l], 3)
        pfm[l, :, 19:21] = _fm(inp["kv_norm_w"][l], 2)
        cw = inp["mlstm_conv_w"][l]
        pfm[l, :, 21:53] = cw.reshape(4, 8, 128).transpose(2, 1, 0).reshape(128, 32)
        pfm[l, :, 53:61] = _fm(inp["mlstm_conv_b"][l], 8)
        fw = inp["ffn_conv_w"][l][:, upi]
        pfm[l, :, 61:193] = fw.reshape(3, 44, 128).transpose(2, 1, 0).reshape(128, 132)
        pfm[l, :, 193:237] = _fm(inp["ffn_conv_b"][l][upi], 44)
        pfm[l, :, 237:285] = _fm(inp["ada_b"][l], 48)
        prow[l, 0, 0:16] = inp["mlstm_gate_b"][l]
        prow[l, 0, 16:528] = inp["mlstm_head_norm_w"][l]
        prow[l, 0, 528:1552] = inp["ada_b"][l][2048:3072]
        prow[l, 0, 1552:2576] = inp["ada_b"][l][5120:6144]
    sh["pfm"] = pfm
    sh["prow"] = prow
    sh["fnw"] = np.ascontiguousarray(inp["final_norm_w"].reshape(1, D).astype(f32))
    return sh


_CACHE = {}


def kernel(**inputs):
    inp = {k: np.asarray(v) for k, v in inputs.items()}
    x = inp["x"]
    B, S_LEN, _ = x.shape
    sh = _prep_shared(inp)
    if S_LEN not in _CACHE:
        _CACHE[S_LEN] = build_nc(S_LEN)[0]
    nc = _CACHE[S_LEN]
    in_maps = []
    for b in range(B):
        m = dict(sh)
        m["x"] = np.ascontiguousarray(x[b].astype(np.float32))
        m["cT"] = _fm(inp["c"][b].astype(np.float32), 8)
        m["pos"] = np.ascontiguousarray(inp["positions"][b].reshape(1, S_LEN).astype(np.int32))
        in_maps.append(m)
    res = run_bass_kernel_spmd(nc, in_maps, core_ids=list(range(B)))
    out = np.stack([np.asarray(r["out"]) for r in res.results], axis=0)
    return out.astype(np.float32)
```

```python
import contextlib
import numpy as np
import concourse.bass as bass
import concourse.mybir as mybir
from concourse.bass_utils import run_bass_kernel_spmd

F32 = mybir.dt.float32
BF16 = mybir.dt.bfloat16
I32 = mybir.dt.int32
AF = mybir.ActivationFunctionType
ALU = mybir.AluOpType
AX = mybir.AxisListType

ENG = ["pe", "act", "dve", "pool", "sp"]


class Sched:
    def __init__(self):
        self.ops = {e: [] for e in ENG}
        self.space = {}
        self.seen = {e: {} for e in ENG}
        self.dcount = {}
        self.outkeys = set()
        self.marks = []

    def mark(self, label):
        self.marks.append((label, {e: len(v) for e, v in self.ops.items()}))

    def _overl(self, name, lo, hi):
        return [en for en in self.space.get(name, []) if en[0] < hi and lo < en[1]]

    def add(self, eng, fn, reads=(), writes=(), dma=None, out=False):
        writes = list(writes) + [r for r in reads if r[0].startswith("ps") and r not in writes]
        raw = {}
        oth = {}

        def merge(dst, src):
            for k, v in src.items():
                if dst.get(k, -1) < v:
                    dst[k] = v

        for (name, lo, hi) in reads:
            for en in self._overl(name, lo, hi):
                merge(raw, en[2])
        for (name, lo, hi) in writes:
            for en in self._overl(name, lo, hi):
                merge(oth, en[2])
                merge(oth, en[3])
        deps = {}
        is_dma = dma is not None
        for src, israw in ((raw, True), (oth, False)):
            for k, v in src.items():
                if k[0] == "e" and k[1] == eng and not is_dma:
                    if eng == "pe" or not israw:
                        continue
                if deps.get(k, -1) < v:
                    deps[k] = v
        waits = []
        seen = self.seen[eng]
        for k, v in deps.items():
            if k[0] == "d":
                v = self.dcount[k[1]]
            if seen.get(k, -1) >= v:
                continue
            seen[k] = v
            waits.append((k, v))
            if k[0] == "e":
                self.ops[k[1]][v]["needed"] = True
        idx = len(self.ops[eng])
        if is_dma:
            self.dcount[dma] = self.dcount.get(dma, 0) + 1
            tok = (("d", dma), self.dcount[dma])
            if out:
                self.outkeys.add(dma)
        else:
            tok = (("e", eng), idx)
        import sys as _sys
        f = _sys._getframe(1)
        src = []
        for _ in range(4):
            if f is None:
                break
            src.append(f.f_lineno)
            f = f.f_back
        self.ops[eng].append(dict(fn=fn, waits=waits, needed=False, dma=dma, src=tuple(src)))
        for (name, lo, hi) in reads:
            lst = self.space.setdefault(name, [])
            for en in lst:
                if en[0] == lo and en[1] == hi:
                    if en[3].get(tok[0], -1) < tok[1]:
                        en[3][tok[0]] = tok[1]
                    break
            else:
                lst.append([lo, hi, {}, {tok[0]: tok[1]}])
        for (name, lo, hi) in writes:
            lst = self.space.setdefault(name, [])
            lst[:] = [en for en in lst if not (lo <= en[0] and en[1] <= hi)]
            lst.append([lo, hi, {tok[0]: tok[1]}, {}])

    def emit(self, nc, final_eng="sp"):
        fw = []
        for key, cnt in self.dcount.items():
            fw.append((("d", key), cnt))
        self.ops[final_eng].append(dict(fn=None, waits=fw, needed=False, dma=None))
        with contextlib.ExitStack() as st:
            esem = {e: st.enter_context(nc.semaphore("se_" + e)) for e in ENG}
            dsem = {k: st.enter_context(nc.semaphore("sd_%d" % i))
                    for i, k in enumerate(self.dcount.keys())}
            val = {}
            for e in ENG:
                c = 0
                for i, op in enumerate(self.ops[e]):
                    if op["needed"]:
                        c += 1
                        val[(e, i)] = c
            block = st.enter_context(nc.Block())

            def run(e, engine):
                for i, op in enumerate(self.ops[e]):
                    for (k, v) in op["waits"]:
                        if k[0] == "e":
                            engine.wait_ge(esem[k[1]], val[(k[1], v)])
                        else:
                            engine.wait_ge(dsem[k[1]], 16 * v)
                    if op["fn"] is None:
                        continue
                    ins = op["fn"](engine)
                    if op["dma"] is not None:
                        ins.then_inc(dsem[op["dma"]], 16)
                    elif op["needed"]:
                        ins.then_inc(esem[e], 1)

            @block.sync
            def _(eng):
                run("sp", eng)

            @block.scalar
            def _(eng):
                run("act", eng)

            @block.vector
            def _(eng):
                run("dve", eng)

            @block.gpsimd
            def _(eng):
                run("pool", eng)

            @block.tensor
            def _(eng):
                run("pe", eng)


D = 1024
NH = 8
QL, KVL, ROPE = 384, 256, 32
DFF = 2816
EPS = 1e-6
TT = 512
NPF = 285
NPR = 2576
W_SHAPES = dict(w_in=(8, 4944), w_uq=(3, 1536), w_ukv=(2, 1024), w_bra=(4, 1024),
                w_brb=(4, 1024), w_out=(8, 1024), w_up=(8, 5632), w_down=(22, 1024))
W_ORDER = ["w_in", "w_uq", "w_ukv", "w_bra", "w_brb", "w_out", "w_up", "w_down"]
C_CQ, C_CKV, C_KR, C_KRS, C_QM, C_KM, C_VM, C_OM, C_IF, C_GA, C_GB = (
    0, 384, 640, 736, 832, 1344, 1856, 2368, 2880, 2896, 3920)


def _prod(s):
    r = 1
    for a in s:
        r *= a
    return r


class Buf:
    def __init__(self, SB, off, shape, dt):
        self.es = 1 if dt == BF16 else 2
        self.shape = tuple(shape)
        n = _prod(shape)
        self.off = off
        self.units = n * self.es
        flat = SB[:, off:off + self.units]
        if dt != BF16:
            flat = flat.bitcast(dt)
        self.flat = flat
        if len(shape) == 1:
            self.v = flat
        elif len(shape) == 2:
            self.v = flat.rearrange("p (a b) -> p a b", b=shape[1])
        else:
            self.v = flat.rearrange("p (a b c) -> p a b c", b=shape[1], c=shape[2])
        self.inner = (n // shape[0]) * self.es

    def r(self, i0=None, i1=None):
        if i0 is None:
            return ("SB", self.off, self.off + self.units)
        if i1 is None:
            i1 = i0 + 1
        return ("SB", self.off + i0 * self.inner, self.off + i1 * self.inner)


class _Stop(Exception):
    pass


def build_nc(S_LEN):
    import os
    STOP = int(os.environ.get("KSTOP", "99"))
    holder = {}
    global _LAST_HOLDER
    _LAST_HOLDER = holder
    try:
        _build_nc(S_LEN, STOP, holder)
    except _Stop:
        pass
    return holder["nc"], holder["S"]


def _build_nc(S_LEN, STOP, holder):
    import os
    NT = S_LEN // TT
    NB = S_LEN // 128
    nc = bass.Bass("TRN2", target_bir_lowering=False)
    S = Sched()
    holder["nc"] = nc
    holder["S"] = S
    din = {}

    def inp(name, shape, dt=F32):
        din[name] = nc.dram_tensor(name, list(shape), dt, kind="ExternalInput").ap()
        return din[name]

    x_in = inp("x", [S_LEN, D])
    cT_in = inp("cT", [128, 8])
    pos_in = inp("pos", [1, S_LEN], I32)
    cst_in = inp("cst", [128, 328])
    pfm_in = inp("pfm", [2, 128, NPF])
    prow_in = inp("prow", [2, 1, NPR])
    fnw_in = inp("fnw", [1, D])
    ada_in = inp("ada_w", [2, 128, 8, 6144])
    w_in_d = {n: inp(n, [2, 128, kc, N]) for n, (kc, N) in W_SHAPES.items()}
    out_d = nc.dram_tensor("out", [S_LEN, D], F32, kind="ExternalOutput").ap()
    ws_d = {n: nc.dram_tensor("ws_" + n, [2, 128, kc, N], BF16, kind="Internal").ap()
            for n, (kc, N) in W_SHAPES.items()}
    kc_d = nc.dram_tensor("kcache", [NH, 96, S_LEN], BF16, kind="Internal").ap()
    vc_d = nc.dram_tensor("vcache", [NH, 128, NB, 65], BF16, kind="Internal").ap()
    xmid_d = nc.dram_tensor("xmid", [S_LEN, D], F32, kind="Internal").ap()

    with contextlib.ExitStack() as st:
        NUNITS = 106400
        SB = st.enter_context(nc.sbuf_tensor("SB", [128, NUNITS], BF16))
        PS = [st.enter_context(nc.psum_tensor("ps%d" % i, [128, 512], F32)) for i in range(8)]
        ptr = [0]

        def alloc(shape, dt):
            b = Buf(SB, ptr[0], shape, dt)
            ptr[0] += b.units + (b.units & 1)
            assert ptr[0] <= NUNITS, ptr[0]
            return b

        def pr(i):
            return ("ps%d" % i, 0, 512)

        def mm(pi, out, lhsT, lr, rhs, rr, start, stop, skip=False):
            if skip:
                S.add("pe", lambda e: e.matmul(out, lhsT=lhsT, rhs=rhs, start=start, stop=stop, skip_group_check=True),
                      reads=[lr, rr], writes=[pr(pi)])
            else:
                S.add("pe", lambda e: e.matmul(out, lhsT=lhsT, rhs=rhs, start=start, stop=stop),
                      reads=[lr, rr], writes=[pr(pi)])

        def actf(out, wr, in_, rd, func, scale=1.0, bias=None, accum=None, eng="act"):
            kw = {}
            if bias is not None:
                kw["bias"] = bias
            if accum is not None:
                kw["accum_out"] = accum
            S.add(eng, lambda e: e.activation(out=out, in_=in_, func=func, scale=scale, **kw),
                  reads=rd, writes=wr)

        def tt(eng, out, wr, in0, in1, rd, op):
            S.add(eng, lambda e: e.tensor_tensor(out=out, in0=in0, in1=in1, op=op), reads=rd, writes=wr)

        def ts(eng, out, wr, in0, rd, s1, s2, op0, op1=None):
            if op1 is None:
                S.add(eng, lambda e: e.tensor_scalar(out=out, in0=in0, scalar1=s1, scalar2=None, op0=op0),
                      reads=rd, writes=wr)
            else:
                S.add(eng, lambda e: e.tensor_scalar(out=out, in0=in0, scalar1=s1, scalar2=s2, op0=op0, op1=op1),
                      reads=rd, writes=wr)

        def stt(eng, out, wr, in0, scalar, in1, rd, op0, op1):
            S.add(eng, lambda e: e.scalar_tensor_tensor(out=out, in0=in0, scalar=scalar, in1=in1, op0=op0, op1=op1),
                  reads=rd, writes=wr)

        def cp(eng, out, wr, in_, rd):
            S.add(eng, lambda e: e.tensor_copy(out=out, in_=in_), reads=rd, writes=wr)

        def memset(eng, out, wr, val):
            S.add(eng, lambda e: e.memset(out, val), reads=[], writes=wr)

        def dma(eng, out, in_, rd, wr, key, outflag=False):
            S.add(eng, lambda e: e.dma_start(out=out, in_=in_), reads=rd, writes=wr, dma=key, out=outflag)

        def recip(eng, out, wr, in_, rd):
            S.add(eng, lambda e: e.reciprocal(out=out, in_=in_), reads=rd, writes=wr)

        def rsqrt_(buf_ap, res, scale, eps_):
            ts("dve", buf_ap, [res], buf_ap, [res], scale, eps_, ALU.mult, ALU.add)
            actf(buf_ap, [res], buf_ap, [res], AF.Sqrt)
            recip("dve", buf_ap, [res], buf_ap, [res])

        DBG = os.environ.get("KDBG")
        dbg_layout = {}
        holder["dbg_layout"] = dbg_layout
        if DBG:
            dbg_d = nc.dram_tensor("dbg", [128, 32768], F32, kind="ExternalOutput").ap()
            dbgbuf = alloc((4096,), F32)
        dbg_col = [0]

        def dump(name, buf, n=None):
            if not DBG or name in dbg_layout:
                return
            n = n or (buf.units // buf.es)
            c0 = dbg_col[0]
            dbg_layout[name] = (c0, n)
            done = 0
            while done < n:
                m = min(4096, n - done)
                cp("dve", dbgbuf.flat[:, 0:m], [dbgbuf.r()], buf.flat[:, done:done + m], [buf.r()])
                dma("pool", dbg_d[:, c0 + done:c0 + done + m], dbgbuf.flat[:, 0:m], [dbgbuf.r()],
                    [("dbg", c0 + done, c0 + done + m)], "dbgst", outflag=True)
                done += m
            dbg_col[0] += n

        def ck(k, T0=0):
            S.mark("ck%d" % k)
            if STOP == k:
                if k >= 2:
                    for s_ in range(4):
                        dma("pool", out_d[T0 + s_ * 128:T0 + (s_ + 1) * 128, :], xt.v[:, s_, :], [xt.r(s_)],
                            [("out", 0, 1)], "xst", outflag=True)
                S.emit(nc)
                raise _Stop()

        grot = [0]

        def G():
            i = 2 + grot[0] % 6
            grot[0] += 1
            return i

        evr = [0]

        def EV():
            evr[0] += 1
            return "act" if evr[0] % 2 else "dve"

        def evac(out, wr, in_, rd, eng=None, scale=None):
            eng = eng or EV()
            if eng == "act":
                actf(out, wr, in_, rd, AF.Copy if scale is None else AF.Copy, scale=1.0 if scale is None else scale)
            else:
                if scale is None:
                    cp("dve", out, wr, in_, rd)
                else:
                    ts("dve", out, wr, in_, rd, scale, None, ALU.mult)

        cst = alloc((328,), F32)
        identf = cst.v[:, 0:128]
        triUf = cst.v[:, 128:256]
        identb = alloc((128,), BF16)
        onesb = alloc((128,), BF16)
        onesf = alloc((128,), F32)
        triUb = alloc((128,), BF16)
        pfm = alloc((NPF,), F32)
        modT = alloc((48,), F32)
        gm = alloc((16,), F32)
        hnw_bc = alloc((512,), F32)
        gateb_bc = alloc((16,), F32)
        adabg_bc = alloc((2048,), F32)
        gbc = adabg_bc
        fnw_bc = alloc((1024,), F32)
        cact = alloc((8,), F32)
        cactb = alloc((8,), BF16)
        cbc = alloc((8, 128), F32)
        S32 = alloc((4, 65), F32)
        Sbf2 = [alloc((4, 65), BF16) for _ in range(2)]
        qkhalo = alloc((8, 3), F32)
        ftail = [alloc((44, 2), F32) for _ in range(2)]
        NW = 5
        wring = [alloc((4096,), BF16) for _ in range(NW)]
        KH = [alloc((S_LEN,), BF16) for _ in range(2)]
        VH = [alloc((NB, 65), BF16) for _ in range(2)]
        xtb = [alloc((4, D), F32) for _ in range(2)]
        xt = xtb[0]
        hT = alloc((8, TT), BF16)
        yTa = alloc((4, TT), BF16)
        yTb = alloc((4, TT), BF16)
        ARENA = ptr[0]

        def phase():
            ptr[0] = ARENA

        wr_i = [0]

        def wload(l, name, k0, k1, c0, c1, src=None, eng="sp"):
            slot = wr_i[0] % NW
            wr_i[0] += 1
            n = (k1 - k0) * (c1 - c0)
            assert n <= 4096
            view = wring[slot].flat[:, 0:n].rearrange("p (k n) -> p k n", n=c1 - c0)
            if src is None:
                dma(eng, view, ws_d[name][l, :, k0:k1, c0:c1], [("ws_%d_%s" % (l, name), 0, 64)],
                    [wring[slot].r()], "wr%d" % slot)
            else:
                dma(eng, view, src[l, :, k0:k1, c0:c1], [], [wring[slot].r()], "wr%d" % slot)
            return view, wring[slot].r()

        dma("sp", cst.flat, cst_in, [], [cst.r()], "cst")
        cp("dve", identb.flat, [identb.r()], identf, [cst.r()])
        cp("dve", triUb.flat, [triUb.r()], triUf, [cst.r()])
        memset("dve", onesb.flat, [onesb.r()], 1.0)
        memset("dve", onesf.flat, [onesf.r()], 1.0)
        dma("sp", fnw_bc.flat, fnw_in.partition_broadcast(128), [], [fnw_bc.r()], "fnw")
        dma("sp", cact.flat, cT_in, [], [cact.r()], "cact")
        actf(cact.flat, [cact.r()], cact.flat, [cact.r()], AF.Silu)
        cp("dve", cactb.flat, [cactb.r()], cact.flat, [cact.r()])
        for k in range(8):
            cp("dve", cbc.v[:, k, :], [cbc.r(k)], cact.flat[:, k:k + 1].to_broadcast([128, 128]), [cact.r()])
        cast_list = {l_: [(name, c) for name in W_ORDER for c in range(W_SHAPES[name][0])] for l_ in range(2)}

        def emit_casts(l, n=None):
            lst = cast_list[l]
            k = len(lst) if n is None else min(n, len(lst))
            for (name, c) in lst[:k]:
                dma("pool", ws_d[name][l, :, c, :], w_in_d[name][l, :, c, :], [],
                    [("ws_%d_%s" % (l, name), c, c + 1)], "ws_%d_%s" % (l, name))
            del lst[:k]

        for s_ in range(4):
            dma("pool", xtb[0].v[:, s_, :], x_in[s_ * 128:(s_ + 1) * 128, :], [], [xtb[0].r(s_)], "xt0")
        emit_casts(0, 13)

        SCALE = float((64 + 32) ** -0.5)
        ck(0)

        for l in range(2):
            x_src = x_in if l == 0 else xmid_d
            x_dst = xmid_d if l == 0 else out_d
            xsrc_res = [] if l == 0 else [("xmid", 0, S_LEN)]
            phase()
            dma("sp", pfm.flat, pfm_in[l], [], [pfm.r()], "pfm")
            dma("sp", hnw_bc.flat, prow_in[l, :, 16:528].partition_broadcast(128), [], [hnw_bc.r()], "hnw")
            dma("sp", gateb_bc.flat, prow_in[l, :, 0:16].partition_broadcast(128), [], [gateb_bc.r()], "gateb")
            dma("sp", adabg_bc.flat, prow_in[l, :, 528:2576].partition_broadcast(128), [], [adabg_bc.r()], "adabg")
            nmw = pfm.v[:, 0:8]
            nfw = pfm.v[:, 8:16]
            qnw = pfm.v[:, 16:19]
            kvnw = pfm.v[:, 19:21]
            mcw = pfm.v[:, 21:53].rearrange("p (c j) -> p c j", j=4)
            mcb = pfm.v[:, 53:61]
            fcw = pfm.v[:, 61:193].rearrange("p (c j) -> p c j", j=3)
            fcb = pfm.v[:, 193:237]
            adabT = pfm.v[:, 237:285]
            pT = 0
            for j in range(12):
                slabs_ = []
                for hk in range(2):
                    slot = wr_i[0] % NW
                    wr_i[0] += 1
                    slab = wring[slot].flat[:, 0:4096].bitcast(F32).rearrange("p (k n) -> p k n", n=512)
                    sr = wring[slot].r()
                    dma("sp", slab, ada_in[l, :, hk * 4:(hk + 1) * 4, j * 512:(j + 1) * 512], [], [sr], "wr%d" % slot)
                    slabs_.append((slab, sr))
                for m in range(4):
                    col = j * 4 + m
                    for kc in range(8):
                        slab, sr = slabs_[kc // 4]
                        mm(pT, PS[pT][:, col:col + 1], slab[:, kc % 4, m * 128:(m + 1) * 128], sr,
                           cact.flat[:, kc:kc + 1], cact.r(), kc == 0, kc == 7)
                if j in (4, 5, 10, 11):
                    gi = {4: 0, 5: 1, 10: 2, 11: 3}[j]
                    pb = G()
                    for kc in range(8):
                        slab, sr = slabs_[kc // 4]
                        mm(pb, PS[pb][:, :], cbc.v[:, kc, :], cbc.r(), slab[:, kc % 4, :], sr, kc == 0, kc == 7)
                    tt("dve", gbc.flat[:, gi * 512:(gi + 1) * 512], [gbc.r()], PS[pb][:, :],
                       gbc.flat[:, gi * 512:(gi + 1) * 512], [pr(pb), gbc.r()], ALU.add)
            tt("dve", modT.flat, [modT.r()], PS[pT][:, 0:48], adabT, [pr(pT), pfm.r()], ALU.add)
            stt("dve", gm.flat[:, 0:8], [gm.r()], modT.flat[:, 8:16], 1.0, nmw, [modT.r(), pfm.r()], ALU.add, ALU.mult)
            stt("dve", gm.flat[:, 8:16], [gm.r()], modT.flat[:, 32:40], 1.0, nfw, [modT.r(), pfm.r()], ALU.add, ALU.mult)
            ck(1)
            memset("dve", S32.flat, [S32.r()], 0.0)
            memset("dve", Sbf2[0].flat, [Sbf2[0].r()], 0.0)
            memset("dve", qkhalo.flat, [qkhalo.r()], 0.0)
            memset("dve", ftail[0].flat, [ftail[0].r()], 0.0)

            def norm_stage(gcol, shcol, xs):
                ss = alloc((4,), F32)
                junk = alloc((D,), BF16)
                for s in range(4):
                    actf(junk.flat, [junk.r()], xt.v[:, s, :], [xt.r(s)], AF.Square, accum=ss.flat[:, s:s + 1],
                         )
                    S.ops["act"][-1]
                return ss, junk

            for t in range(NT):
                T0 = t * TT
                phase()
                xt = xtb[(l * NT + t) % 2]

                def do_norm(gcol, shcol):
                    xs = alloc((4, D), BF16)
                    ss = alloc((4,), F32)
                    junk = alloc((D,), BF16)
                    for s in range(4):
                        actf(junk.flat, [junk.r(), ss.r()], xt.v[:, s, :], [xt.r(s)], AF.Square,
                             accum=ss.flat[:, s:s + 1])
                    rsqrt_(ss.flat, ss.r(), 1.0 / D, EPS)
                    for s in range(4):
                        if s % 2:
                            ts("dve", xs.v[:, s, :], [xs.r(s)], xt.v[:, s, :], [xt.r(s), ss.r()],
                               ss.flat[:, s:s + 1], None, ALU.mult)
                        else:
                            actf(xs.v[:, s, :], [xs.r(s)], xt.v[:, s, :], [xt.r(s), ss.r()], AF.Copy,
                                 scale=ss.flat[:, s:s + 1])
                    for c in range(8):
                        pi = G()
                        for s in range(4):
                            mm(pi, PS[pi][:, s * 128:(s + 1) * 128], xs.v[:, s, c * 128:(c + 1) * 128], xs.r(s),
                               identb.flat, identb.r(), True, True)
                        if c % 2:
                            actf(hT.v[:, c, :], [hT.r(c)], PS[pi][:, :], [pr(pi), gm.r(), modT.r()], AF.Identity,
                                 scale=gm.flat[:, gcol + c:gcol + c + 1], bias=modT.flat[:, shcol + c:shcol + c + 1])
                        else:
                            ts("dve", hT.v[:, c, :], [hT.r(c)], PS[pi][:, :], [pr(pi), gm.r(), modT.r()],
                               gm.flat[:, gcol + c:gcol + c + 1], modT.flat[:, shcol + c:shcol + c + 1],
                               ALU.mult, ALU.add)

                do_norm(0, 0)
                dump("hT", hT)
                ck(2, T0)

                phase()
                cqw = alloc((3, TT), BF16)
                sq = alloc((3, TT), BF16)
                ckvw = alloc((2, TT), BF16)
                rstdq = alloc((TT,), F32)
                rstdkv = alloc((TT,), F32)
                rstdkvt = alloc((4,), F32)
                rden = [alloc((TT,), F32) for _ in range(2)]
                posi = Buf(SB, rden[0].off, (TT,), I32)
                ang = alloc((TT,), F32)
                angk = alloc((TT,), F32)
                anki = Buf(SB, rden[1].off, (TT,), I32)
                cosT = alloc((TT,), F32)
                sinT = alloc((TT,), F32)
                cosr = alloc((TT,), F32)
                sinr = alloc((TT,), F32)
                t1 = alloc((TT,), F32)
                t2 = alloc((TT,), F32)
                QT = alloc((NH, TT), BF16)
                KTc = alloc((NH, TT), BF16)
                Vc = alloc((NH, 4, 65), BF16)
                PT = [alloc((TT,), BF16) for _ in range(4)]
                accsb = [alloc((TT,), F32) for _ in range(2)]
                for a_ in accsb:
                    memset("pool", a_.flat, [a_.r()], 0.0)
                RP = slice(64, 96)
                if t > 0:
                    for hh_ in range(2):
                        dma("pool", KH[hh_].flat[0:96, 0:T0], kc_d[hh_, :, 0:T0], [("kc", 0, T0)], [KH[hh_].r()], "kh%d" % hh_)
                        dma("pool", VH[hh_].v[:, 0:4 * t, :], vc_d[hh_, :, 0:4 * t, :], [("vc", 0, 4 * t)], [VH[hh_].r()], "vh%d" % hh_)
                dma("pool", posi.flat[RP, :], pos_in[:, T0:T0 + TT].partition_broadcast(32), [], [posi.r()], "posi")
                cp("dve", ang.flat[RP, :], [ang.r()], posi.flat[RP, :], [posi.r()])
                for (tab, phcol) in ((cosT, 257), (sinT, 258)):
                    ts("dve", angk.flat[RP, :], [angk.r()], ang.flat[RP, :], [ang.r(), cst.r()],
                       cst.v[RP, 256:257], cst.v[RP, phcol:phcol + 1], ALU.mult, ALU.add)
                    ts("dve", anki.flat[RP, :], [anki.r()], angk.flat[RP, :], [angk.r()],
                       float(1.0 / (2 * np.pi)), None, ALU.mult)
                    cp("dve", tab.flat[RP, :], [tab.r()], anki.flat[RP, :], [anki.r()])
                    stt("dve", angk.flat[RP, :], [angk.r()], tab.flat[RP, :], float(-2 * np.pi), angk.flat[RP, :],
                        [tab.r(), angk.r()], ALU.mult, ALU.add)
                    S.add("dve", lambda e, tab=tab: e.tensor_single_scalar(out=tab.flat[RP, :], in_=angk.flat[RP, :],
                                                                          scalar=float(np.pi), op=ALU.is_gt),
                          reads=[angk.r()], writes=[tab.r()])
                    stt("dve", angk.flat[RP, :], [angk.r()], tab.flat[RP, :], float(-2 * np.pi), angk.flat[RP, :],
                        [tab.r(), angk.r()], ALU.mult, ALU.add)
                    actf(tab.flat[RP, :], [tab.r()], angk.flat[RP, :], [angk.r()], AF.Sin)
                ck(21, T0)
                sq2 = alloc((2, TT), BF16)
                slabA, rA = wload(l, "w_in", 0, 8, C_CQ, C_CQ + 384)
                slabB, rB = wload(l, "w_in", 0, 8, C_CKV, C_CKV + 448)
                for m in range(3):
                    pi = G()
                    for kc in range(8):
                        mm(pi, PS[pi][:, :], slabA[:, kc, m * 128:(m + 1) * 128], rA, hT.v[:, kc, :], hT.r(kc),
                           kc == 0, kc == 7)
                    ts("dve", cqw.v[:, m, :], [cqw.r(m)], PS[pi][:, :], [pr(pi), pfm.r()], qnw[:, m:m + 1], None, ALU.mult)
                    actf(sq.v[:, m, :], [sq.r(m)], PS[pi][:, :], [pr(pi)], AF.Square)
                for m in range(2):
                    pi = G()
                    for kc in range(8):
                        mm(pi, PS[pi][:, :], slabB[:, kc, m * 128:(m + 1) * 128], rB, hT.v[:, kc, :], hT.r(kc),
                           kc == 0, kc == 7)
                    ts("dve", ckvw.v[:, m, :], [ckvw.r(m)], PS[pi][:, :], [pr(pi), pfm.r()], kvnw[:, m:m + 1], None, ALU.mult)
                    actf(sq2.v[:, m, :], [sq2.r(m)], PS[pi][:, :], [pr(pi)], AF.Square)
                pk1 = G()
                pk2 = G()
                for kc in range(8):
                    mm(pk1, PS[pk1][0:96, :], slabB[:, kc, 256:352], rB, hT.v[:, kc, :], hT.r(kc), kc == 0, kc == 7)
                for kc in range(8):
                    mm(pk2, PS[pk2][0:96, :], slabB[:, kc, 352:448], rB, hT.v[:, kc, :], hT.r(kc), kc == 0, kc == 7)
                tt("dve", t1.flat[RP, :], [t1.r()], PS[pk1][RP, :], cosT.flat[RP, :], [pr(pk1), cosT.r()], ALU.mult)
                tt("dve", t2.flat[RP, :], [t2.r()], PS[pk2][RP, :], sinT.flat[RP, :], [pr(pk2), sinT.r()], ALU.mult)
                tt("dve", t1.flat[RP, :], [t1.r()], t1.flat[RP, :], t2.flat[RP, :], [t1.r(), t2.r()], ALU.add)
                for h in range(NH):
                    cp("pool" if h % 2 else "dve", KTc.v[RP, h, :], [KTc.r(h)], t1.flat[RP, :], [t1.r()])
                pq = G()
                for m in range(3):
                    mm(pq, PS[pq][:, :], onesb.flat, onesb.r(), sq.v[:, m, :], sq.r(m), m == 0, m == 2)
                cp("dve", rstdq.flat, [rstdq.r()], PS[pq][:, :], [pr(pq)])
                pkv = G()
                for m in range(2):
                    mm(pkv, PS[pkv][:, :], onesb.flat, onesb.r(), sq2.v[:, m, :], sq2.r(m), m == 0, m == 1)
                cp("dve", rstdkv.flat, [rstdkv.r()], PS[pkv][:, :], [pr(pkv)])
                pkt = G()
                for s in range(4):
                    for m in range(2):
                        mm(pkt, PS[pkt][:, s:s + 1], sq2.v[:, m, s * 128:(s + 1) * 128], sq2.r(m), onesb.flat[:, 0:1], onesb.r(),
                           m == 0, m == 1)
                cp("dve", rstdkvt.flat, [rstdkvt.r()], PS[pkt][:, 0:4], [pr(pkt)])
                rsqrt_(rstdq.flat, rstdq.r(), 1.0 / QL, EPS)
                rsqrt_(rstdkv.flat, rstdkv.r(), 1.0 / KVL, EPS)
                rsqrt_(rstdkvt.flat, rstdkvt.r(), 1.0 / KVL, EPS)
                tt("dve", cosr.flat[RP, :], [cosr.r()], cosT.flat[RP, :], rstdq.flat[RP, :], [cosT.r(), rstdq.r()], ALU.mult)
                tt("dve", sinr.flat[RP, :], [sinr.r()], sinT.flat[RP, :], rstdq.flat[RP, :], [sinT.r(), rstdq.r()], ALU.mult)
                ck(215, T0)
                slabKV, rKV = wload(l, "w_ukv", 0, 2, 0, 1024)
                for h in range(NH):
                    pi = G()
                    for kc in range(2):
                        mm(pi, PS[pi][0:64, :], slabKV[:, kc, h * 64:(h + 1) * 64], rKV, ckvw.v[:, kc, :], ckvw.r(kc),
                           kc == 0, kc == 1)
                    tt("dve", KTc.v[0:64, h, :], [KTc.r(h)], PS[pi][0:64, :], rstdkv.flat[0:64, :], [pr(pi), rstdkv.r()], ALU.mult)
                memset("pool", Vc.v[:, :, :, 64:65], [Vc.r()], 1.0)
                for s in range(4):
                    pi = G()
                    for kc in range(2):
                        mm(pi, PS[pi][:, :], ckvw.v[:, kc, s * 128:(s + 1) * 128], ckvw.r(kc), slabKV[:, kc, 512:1024], rKV,
                           kc == 0, kc == 1)
                    ts("dve", Vc.v[:, :, s, 0:64], [Vc.r()], PS[pi][:, :].rearrange("p (h e) -> p h e", e=64),
                       [pr(pi), rstdkvt.r()], rstdkvt.flat[:, s:s + 1], None, ALU.mult)
                ck(22, T0)
                slabQ1, rQ1 = wload(l, "w_uq", 0, 3, 0, 768)
                slabQ2, rQ2 = wload(l, "w_uq", 0, 3, 768, 1536)
                tq = [(t1, t2), (ang, angk)]
                for h in range(NH):
                    p1 = G()
                    p2 = G()
                    ta, tb = tq[h % 2]
                    for kc in range(3):
                        mm(p1, PS[p1][0:96, :], slabQ1[:, kc, h * 96:(h + 1) * 96], rQ1, cqw.v[:, kc, :], cqw.r(kc),
                           kc == 0, kc == 2)
                    for kc in range(3):
                        mm(p2, PS[p2][0:96, :], slabQ2[:, kc, h * 96:(h + 1) * 96], rQ2, cqw.v[:, kc, :], cqw.r(kc),
                           kc == 0, kc == 2)
                    tt("dve", QT.v[0:64, h, :], [QT.r(h)], PS[p1][0:64, :], rstdq.flat[0:64, :], [pr(p1), rstdq.r()], ALU.mult)
                    tt("dve", ta.flat[RP, :], [ta.r()], PS[p1][RP, :], cosr.flat[RP, :], [pr(p1), cosr.r()], ALU.mult)
                    tt("dve", tb.flat[RP, :], [tb.r()], PS[p2][RP, :], sinr.flat[RP, :], [pr(p2), sinr.r()], ALU.mult)
                    tt("pool", QT.v[RP, h, :], [QT.r(h)], ta.flat[RP, :], tb.flat[RP, :], [ta.r(), tb.r()], ALU.add)
                if l == 0 and t == 0:
                    emit_casts(0)
                if t < NT - 1:
                    dma("pool", kc_d[:, :, T0:T0 + TT].rearrange("h d t -> d h t"), KTc.v[0:96, :, :], [KTc.r()],
                        [("kc", T0, T0 + TT)], "kcs")
                    dma("pool", vc_d[:, :, 4 * t:4 * t + 4, :].rearrange("h p s e -> p h s e"), Vc.v, [Vc.r()],
                        [("vc", 4 * t, 4 * t + 4)], "vcs")
                dump("QT", QT)
                dump("KTc", KTc)
                dump("Vc", Vc)
                ck(23, T0)
                LA = 2
                blocks = []
                for h in range(NH):
                    for kb in range(4 * t + 4):
                        blocks.append((h, kb))
                nkb = 4 * t + 4
                pend = {}

                def load_hist(hh):
                    sl_ = hh % 2
                    dma("pool", KH[sl_].flat[0:96, 0:T0], kc_d[hh, :, 0:T0], [("kc", 0, T0)], [KH[sl_].r()], "kh%d" % sl_)
                    dma("pool", VH[sl_].v[:, 0:4 * t, :], vc_d[hh, :, 0:4 * t, :], [("vc", 0, 4 * t)], [VH[sl_].r()], "vh%d" % sl_)

                def emit_score(i):
                    h, kb = blocks[i]
                    sl = h % 2
                    if t > 0 and kb == min(LA + 1, nkb - 1) and h + 1 < NH and h >= 1:
                        load_hist(h + 1)
                    if kb < 4 * t:
                        Ks, Kr = KH[sl].flat[0:96, kb * 128:(kb + 1) * 128], KH[sl].r()
                        Vs, Vr = VH[sl].v[:, kb, :], VH[sl].r()
                        q0 = 0
                    else:
                        j = kb - 4 * t
                        Ks, Kr = KTc.v[0:96, h, j * 128:(j + 1) * 128], KTc.r(h)
                        Vs, Vr = Vc.v[:, h, j, :], Vc.r()
                        q0 = j * 128
                    pi = G()
                    mm(pi, PS[pi][:, q0:TT], Ks, Kr, QT.v[0:96, h, q0:TT], QT.r(h), True, True)
                    pt = PT[i % len(PT)]
                    actf(pt.flat[:, q0:TT], [pt.r()], PS[pi][:, q0:TT], [pr(pi)], AF.Exp, scale=SCALE)
                    if kb >= 4 * t:
                        memset("pool", pt.flat[64:128, q0:q0 + 64], [pt.r()], 0.0)
                    pend[i] = (pt, Vs, Vr, q0)

                def emit_pv(i):
                    h, kb = blocks[i]
                    pt, Vs, Vr, q0 = pend.pop(i)
                    ai = h % 2
                    mm(ai, PS[ai][0:65, q0:TT], Vs, Vr, pt.flat[:, q0:TT], pt.r(), kb == 0, kb == nkb - 1)
                    if kb == nkb - 1:
                        ab = accsb[h % 2]
                        rd = rden[h % 2]
                        cp("dve", ab.flat[0:65, :], [ab.r()], PS[ai][0:65, :], [pr(ai)])
                        pd = G()
                        mm(pd, PS[pd][0:64, :], cst.v[:, 264:328], cst.r(), ab.flat, ab.r(), True, True)
                        recip("dve", rd.flat[0:64, :], [rd.r()], PS[pd][0:64, :], [pr(pd)])
                        r0 = (h % 2) * 64
                        tt("dve", yTa.v[r0:r0 + 64, h // 2, :], [yTa.r(h // 2)], ab.flat[0:64, :], rd.flat[0:64, :],
                           [ab.r(), rd.r()], ALU.mult)

                for i in range(len(blocks) + LA):
                    if i < len(blocks):
                        emit_score(i)
                    if i >= LA:
                        emit_pv(i - LA)
                dump("yTa", yTa)
                ck(3, T0)
                phase()
                if l == 0:
                    emit_casts(1, None if t == NT - 1 else (0 if (NT > 1 and t == 0) else 10))
                qkpre = alloc((8, TT + 3), F32)
                cacc = [alloc((TT,), F32) for _ in range(2)]
                QmZ = alloc((8, TT), BF16)
                memset("pool", QmZ.flat, [QmZ.r()], 0.0)
                KmT = alloc((4, TT), BF16)
                Km = alloc((4, TT), BF16)
                Vp = alloc((4, NH, 65), BF16)
                og = alloc((4, TT), BF16)
                gpre = alloc((4, 16), F32)
                lf = alloc((4, 8), F32)
                bcs = alloc((4, 8), F32)
                uu = alloc((4, 8), F32)
                bnd = alloc((4, 8), F32)
                ebl = alloc((8,), F32)
                pmall = [[alloc((4, 128), BF16) for _ in range(2)] for _ in range(4)]
                dd = alloc((8,), F32)
                hbuf4 = alloc((4, NH, 64), F32)
                hsq = alloc((NH, 64), F32)
                hss4 = alloc((4, 8), F32)
                stmp = alloc((4, 65), F32)
                ymls = alloc((4, TT), BF16)
                slabIF, rIF = wload(l, "w_in", 0, 8, C_IF, C_IF + 16)
                pg = G()
                for s in range(4):
                    for kc in range(8):
                        mm(pg, PS[pg][:, s * 16:(s + 1) * 16], hT.v[:, kc, s * 128:(s + 1) * 128], hT.r(kc),
                           slabIF[:, kc, :], rIF, kc == 0, kc == 7)
                tt("dve", gpre.v, [gpre.r()], PS[pg][:, 0:64].rearrange("p (s g) -> p s g", g=16),
                   gateb_bc.flat.unsqueeze(1).to_broadcast([128, 4, 16]), [pr(pg), gateb_bc.r()], ALU.add)
                actf(lf.v, [lf.r()], gpre.v[:, :, 8:16], [gpre.r()], AF.Exp, scale=-1.0)
                actf(lf.v, [lf.r()], lf.v, [lf.r()], AF.Ln, bias=1.0)
                ts("dve", lf.v, [lf.r()], lf.v, [lf.r()], -1.0, None, ALU.mult)
                cp("pool", qkpre.v[:, :, 0:3], [qkpre.r()], qkhalo.v, [qkhalo.r()])
                slabsD = [wload(l, "w_in", 0, 8, C_QM + half * 512, C_QM + (half + 1) * 512) for half in range(2)]

                def qk_mm(half, pr2):
                    slabD, rD = slabsD[half]
                    pair = []
                    for cc in (2 * pr2, 2 * pr2 + 1):
                        c = half * 4 + cc
                        pi = G()
                        for kc in range(8):
                            mm(pi, PS[pi][:, :], slabD[:, kc, cc * 128:(cc + 1) * 128], rD, hT.v[:, kc, :], hT.r(kc),
                               kc == 0, kc == 7)
                        actf(qkpre.v[:, c, 3:TT + 3], [qkpre.r(c)], PS[pi][:, :], [pr(pi)], AF.Copy)
                        pair.append((half, cc, c, cacc[cc % 2]))
                    return pair

                def qk_conv(pair):
                    for (half, cc, c, ca) in pair:
                        ts("dve", ca.flat, [ca.r()], qkpre.v[:, c, 0:TT], [qkpre.r(c), pfm.r()], mcw[:, c, 0:1], mcb[:, c:c + 1],
                           ALU.mult, ALU.add)
                    for j in range(1, 4):
                        for (half, cc, c, ca) in pair:
                            stt("dve", ca.flat, [ca.r()], qkpre.v[:, c, j:TT + j], mcw[:, c, j:j + 1], ca.flat,
                                [qkpre.r(c), pfm.r(), ca.r()], ALU.mult, ALU.add)
                    for (half, cc, c, ca) in pair:
                        if half == 0:
                            actf(QmZ.v[0:64, 2 * cc, :], [QmZ.r(2 * cc)], ca.flat[0:64, :], [ca.r()], AF.Silu)
                            actf(QmZ.v[64:128, 2 * cc + 1, :], [QmZ.r(2 * cc + 1)], ca.flat[64:128, :], [ca.r()], AF.Silu)
                        else:
                            actf(KmT.v[:, cc, :], [KmT.r(cc)], ca.flat, [ca.r()], AF.Silu)

                order = [(0, 0), (0, 1), (1, 0), (1, 1)]
                infl = [qk_mm(*order[0]), qk_mm(*order[1])]
                for k_ in range(4):
                    qk_conv(infl[k_])
                    if k_ + 2 < 4:
                        infl.append(qk_mm(*order[k_ + 2]))
                cp("pool", qkhalo.v, [qkhalo.r()], qkpre.v[:, :, TT:TT + 3], [qkpre.r()])
                ck(31, T0)
                slabV, rV = wload(l, "w_in", 0, 8, C_VM, C_VM + 512)
                pb = G()
                for s in range(4):
                    mm(pb, PS[pb][:, s * 8:(s + 1) * 8], triUf, cst.r(), lf.v[:, s, :], lf.r(), True, True)
                cp("dve", bcs.v, [bcs.r()], PS[pb][:, 0:32].rearrange("p (s h) -> p s h", h=8), [pr(pb)])
                tt("dve", uu.v, [uu.r()], gpre.v[:, :, 0:8], bcs.v, [gpre.r(), bcs.r()], ALU.subtract)
                actf(uu.v, [uu.r()], uu.v, [uu.r()], AF.Exp)
                actf(bnd.v, [bnd.r()], bcs.v, [bcs.r()], AF.Exp, scale=-1.0, bias=float(np.log(8.0)))
                for s in range(4):
                    pi = G()
                    for kc in range(8):
                        mm(pi, PS[pi][:, :], hT.v[:, kc, s * 128:(s + 1) * 128], hT.r(kc), slabV[:, kc, :], rV,
                           kc == 0, kc == 7)
                    tt("dve", Vp.v[:, s, :, 0:64], [Vp.r(s)], PS[pi][:, :].rearrange("p (h e) -> p h e", e=64),
                       uu.v[:, s, :].unsqueeze(2).to_broadcast([128, NH, 64]), [pr(pi), uu.r()], ALU.mult)
                    cp("dve", Vp.v[:, s, :, 64:65], [Vp.r(s)], uu.v[:, s, :].unsqueeze(2), [uu.r()])
                slabO, rO = wload(l, "w_in", 0, 8, C_OM, C_OM + 512)
                for s in range(4):
                    pi = G()
                    for kc in range(8):
                        mm(pi, PS[pi][:, :], hT.v[:, kc, s * 128:(s + 1) * 128], hT.r(kc), slabO[:, kc, :], rO,
                           kc == 0, kc == 7)
                    actf(og.v[:, s, :], [og.r(s)], PS[pi][:, :], [pr(pi)], AF.Sigmoid)
                    tt("pool", og.v[:, s, :], [og.r(s)], og.v[:, s, :], hnw_bc.flat, [og.r(s), hnw_bc.r()], ALU.mult)
                for s in range(4):
                    pi = G()
                    for c in range(4):
                        mm(pi, PS[pi][:, c * 128:(c + 1) * 128], KmT.v[:, c, s * 128:(s + 1) * 128], KmT.r(c),
                           identb.flat, identb.r(), True, True)
                    evac(Km.v[:, s, :], [Km.r(s)], PS[pi][:, :], [pr(pi)])
                ck(32, T0)
                snaps = [Sbf2[t % 2]] + [alloc((4, 65), BF16) for _ in range(3)] + [Sbf2[(t + 1) % 2]]
                ebls = alloc((4, 8), F32)
                for s in range(4):
                    pe_ = G()
                    mm(pe_, PS[pe_][:, 0:8], onesf.flat, onesf.r(), lf.v[:, s, :], lf.r(), True, True)
                    actf(ebls.v[:, s, :], [ebls.r(s)], PS[pe_][:, 0:8], [pr(pe_)], AF.Exp)
                for s in range(4):
                    pst = []
                    for b2 in range(2):
                        pu = G()
                        pst.append(pu)
                        for cc in range(2):
                            c = b2 * 2 + cc
                            mm(pu, PS[pu][:, cc * 130:(cc + 1) * 130], Km.v[:, s, c * 128:(c + 1) * 128], Km.r(s),
                               Vp.v[:, s, 2 * c:2 * c + 2, :], Vp.r(s), True, True)
                    eblv = ebls.v[:, s, :].rearrange("p (c two) -> p c two", two=2)
                    for b2 in range(2):
                        puv = PS[pst[b2]][:, 0:260].rearrange("p (c two e) -> p c two e", two=2, e=65)
                        for hf in range(2):
                            rs_ = slice(hf * 64, hf * 64 + 64)
                            tt("dve", stmp.v[rs_, b2 * 2:b2 * 2 + 2, :], [stmp.r()], puv[rs_, :, hf, :],
                               S32.v[rs_, b2 * 2:b2 * 2 + 2, :], [pr(pst[b2]), S32.r()], ALU.add)
                    for hf in range(2):
                        rs_ = slice(hf * 64, hf * 64 + 64)
                        tt("dve", S32.v[rs_, :, :], [S32.r()], stmp.v[rs_, :, :],
                           eblv[rs_, :, hf].unsqueeze(2).to_broadcast([64, 4, 65]), [stmp.r(), ebls.r(s)], ALU.mult)
                    cp("dve", snaps[s + 1].v, [snaps[s + 1].r()], S32.v, [S32.r()])
                for s in range(4):
                    tsl = slice(s * 128, (s + 1) * 128)
                    for b2 in range(2):
                        psc = G()
                        for hh in range(4):
                            h = b2 * 4 + hh
                            c = h // 2
                            mm(psc, PS[psc][:, hh * 128:(hh + 1) * 128], KmT.v[:, c, tsl], KmT.r(c),
                               QmZ.v[:, h, tsl], QmZ.r(h), True, True)
                        tt("dve", pmall[s][b2].v, [pmall[s][b2].r()], PS[psc][:, :].rearrange("p (h t) -> p h t", t=128),
                           triUb.flat.unsqueeze(1).to_broadcast([128, 4, 128]), [pr(psc), triUb.r()], ALU.mult)
                for s in range(4):
                    tsl = slice(s * 128, (s + 1) * 128)
                    nd = []
                    pm = pmall[s]
                    for b2 in range(2):
                        pn = G()
                        nd.append(pn)
                        for hh in range(4):
                            h = b2 * 4 + hh
                            c = h // 2
                            mm(pn, PS[pn][:, hh * 65:(hh + 1) * 65], pm[b2].v[:, hh, :], pm[b2].r(), Vp.v[:, s, h, :], Vp.r(s),
                               True, False)
                            mm(pn, PS[pn][:, hh * 65:(hh + 1) * 65], QmZ.v[:, h, tsl], QmZ.r(h),
                               snaps[s].v[:, c, :], snaps[s].r(), False, True)
                    ck(326, T0)
                    for b2 in range(2):
                        ndv = PS[nd[b2]][:, 0:260].rearrange("p (h e) -> p h e", e=65)
                        dsl = dd.flat[:, b2 * 4:(b2 + 1) * 4]
                        cp("dve", dsl, [dd.r()], ndv[:, :, 64], [pr(nd[b2])])
                        stt("dve", dsl, [dd.r()], dsl, -1.0, dsl, [dd.r()], ALU.mult, ALU.max)
                        tt("dve", dsl, [dd.r()], dsl, bnd.v[:, s, b2 * 4:(b2 + 1) * 4], [dd.r(), bnd.r()], ALU.max)
                        recip("dve", dsl, [dd.r()], dsl, [dd.r()])
                        tt("dve", hbuf4.v[:, s, b2 * 4:(b2 + 1) * 4, :], [hbuf4.r(s)], ndv[:, :, 0:64],
                           dsl.unsqueeze(2).to_broadcast([128, 4, 64]), [pr(nd[b2]), dd.r()], ALU.mult)
                    ck(33, T0)
                    tt("pool", hsq.v, [hsq.r()], hbuf4.v[:, s], hbuf4.v[:, s], [hbuf4.r(s)], ALU.mult)
                    S.add("dve", lambda e, s=s: e.tensor_reduce(out=hss4.v[:, s, :], in_=hsq.v, axis=AX.X, op=ALU.add),
                          reads=[hsq.r()], writes=[hss4.r(s)])
                    ck(34, T0)
                sa_all = Buf(SB, qkpre.off, (8, TT), BF16)
                sb_all = Buf(SB, qkpre.off + 8 * TT, (8, TT), BF16)
                for jj in range(2):
                    slabGa, rGa = wload(l, "w_in", 0, 8, C_GA + jj * 512, C_GA + (jj + 1) * 512)
                    slabGb, rGb = wload(l, "w_in", 0, 8, C_GB + jj * 512, C_GB + (jj + 1) * 512)
                    for j4 in range(4):
                        j = jj * 4 + j4
                        pga = G()
                        for kc in range(8):
                            mm(pga, PS[pga][:, :], slabGa[:, kc, j4 * 128:(j4 + 1) * 128], rGa, hT.v[:, kc, :], hT.r(kc),
                               kc == 0, kc == 7)
                        actf(sa_all.v[:, j, :], [sa_all.r(j)], PS[pga][:, :], [pr(pga)], AF.Sigmoid)
                        pgb = G()
                        for kc in range(8):
                            mm(pgb, PS[pgb][:, :], slabGb[:, kc, j4 * 128:(j4 + 1) * 128], rGb, hT.v[:, kc, :], hT.r(kc),
                               kc == 0, kc == 7)
                        actf(sb_all.v[:, j, :], [sb_all.r(j)], PS[pgb][:, :], [pr(pgb)], AF.Sigmoid)
                rsqrt_(hss4.flat, hss4.r(), 1.0 / 64, EPS)
                for s in range(4):
                    tt("dve", hbuf4.v[:, s], [hbuf4.r(s)], hbuf4.v[:, s],
                       hss4.v[:, s, :].unsqueeze(2).to_broadcast([128, NH, 64]), [hbuf4.r(s), hss4.r()], ALU.mult)
                    tt("dve", ymls.v[:, s, :].rearrange("p (h e) -> p h e", e=64), [ymls.r(s)], hbuf4.v[:, s],
                       og.v[:, s, :].rearrange("p (h e) -> p h e", e=64), [hbuf4.r(s), og.r(s)], ALU.mult)
                for c in range(4):
                    pi = G()
                    for s in range(4):
                        mm(pi, PS[pi][:, s * 128:(s + 1) * 128], ymls.v[:, s, c * 128:(c + 1) * 128], ymls.r(s),
                           identb.flat, identb.r(), True, True)
                    evac(yTb.v[:, c, :], [yTb.r(c)], PS[pi][:, :], [pr(pi)])

                dump("ymls", ymls)
                ck(4, T0)
                phase()
                _skip = alloc((8240,), BF16)
                m1 = [alloc((TT,), F32) for _ in range(2)]
                mT = alloc((8, TT), BF16)
                otmp = [alloc((TT,), F32) for _ in range(2)]
                for jj in range(2):
                    slabBa, rBa = wload(l, "w_bra", 0, 4, jj * 512, (jj + 1) * 512)
                    slabBb, rBb = wload(l, "w_brb", 0, 4, jj * 512, (jj + 1) * 512)
                    for j4 in range(4):
                        j = jj * 4 + j4
                        k2 = j % 2
                        pa, pb = G(), G()
                        for kc in range(4):
                            mm(pa, PS[pa][:, :], slabBa[:, kc, j4 * 128:(j4 + 1) * 128], rBa, yTa.v[:, kc, :], yTa.r(kc),
                               kc == 0, kc == 3)
                        for kc in range(4):
                            mm(pb, PS[pb][:, :], slabBb[:, kc, j4 * 128:(j4 + 1) * 128], rBb, yTb.v[:, kc, :], yTb.r(kc),
                               kc == 0, kc == 3)
                        tt("dve", m1[k2].flat, [m1[k2].r()], PS[pa][:, :], sa_all.v[:, j, :], [pr(pa), sa_all.r(j)], ALU.mult)
                        tt("dve", sb_all.v[:, j, :], [sb_all.r(j)], PS[pb][:, :], sb_all.v[:, j, :], [pr(pb), sb_all.r(j)], ALU.mult)
                        tt("dve" if (j % 2 == 1) else "pool", mT.v[:, j, :], [mT.r(j)], m1[k2].flat, sb_all.v[:, j, :],
                           [m1[k2].r(), sb_all.r(j)], ALU.add)

                def proj_residual(lhs_buf, nk, wname, gcol0):
                    oi = 0
                    for half in range(2):
                        slabs = []
                        k0 = 0
                        while k0 < nk:
                            k1 = min(nk, k0 + 8)
                            slabs.append((k0, k1) + wload(l, wname, k0, k1, half * 512, (half + 1) * 512))
                            k0 = k1
                        for s in range(4):
                            pi = G()
                            for (k0, k1, sv, sr) in slabs:
                                for kc in range(k0, k1):
                                    mm(pi, PS[pi][:, :], lhs_buf.v[:, kc, s * 128:(s + 1) * 128], lhs_buf.r(kc),
                                       sv[:, kc - k0, :], sr, kc == 0, kc == nk - 1)
                            ot = otmp[oi % 2]
                            oi += 1
                            gsl = gbc.flat[:, gcol0 + half * 512:gcol0 + (half + 1) * 512]
                            tt("dve", ot.flat, [ot.r()], PS[pi][:, :], gsl, [pr(pi), gbc.r()], ALU.mult)
                            xsl = xt.v[:, s, half * 512:(half + 1) * 512]
                            tt("dve" if (half == 1 and s >= 2) else "pool", xsl, [xt.r(s)], xsl, ot.flat, [xt.r(s), ot.r()], ALU.add)

                dump("mT", mT)
                proj_residual(mT, 8, "w_out", 0)
                dump("x1", xt)

                ck(5, T0)
                phase()
                gt_ = l * NT + t + 1
                if gt_ < 2 * NT:
                    l2_, t2_ = divmod(gt_, NT)
                    xn_ = xtb[gt_ % 2]
                    src_ = x_in if l2_ == 0 else xmid_d
                    res_ = [] if l2_ == 0 else [("xmid", t2_ * TT, (t2_ + 1) * TT)]
                    for s in range(4):
                        dma("pool", xn_.v[:, s, :], src_[t2_ * TT + s * 128:t2_ * TT + (s + 1) * 128, :], res_,
                            [xn_.r(s)], "xt%d" % (gt_ % 2))
                do_norm(8, 24)
                NRING = 6
                sta = [alloc((TT + 2,), F32) for _ in range(NRING)]
                facc = [alloc((TT,), F32) for _ in range(NRING)]
                ga = [alloc((TT,), F32) for _ in range(2)]
                actT = alloc((22, TT), BF16)
                otmp = [alloc((TT,), F32) for _ in range(2)]
                told = ftail[t % 2]
                tnew = ftail[(t + 1) % 2]
                slab_cache = {}

                def up_mm(jp):
                    sl_ = jp // 2
                    if sl_ not in slab_cache:
                        slab_cache.clear()
                        slab_cache[sl_] = wload(l, "w_up", 0, 8, sl_ * 512, (sl_ + 1) * 512)
                    slabU, rU = slab_cache[sl_]
                    grp = []
                    for av in range(2):
                        q4 = (jp % 2) * 2 + av
                        ch = jp * 2 + av
                        pi = G()
                        for kc in range(8):
                            mm(pi, PS[pi][:, :], slabU[:, kc, q4 * 128:(q4 + 1) * 128], rU,
                               hT.v[:, kc, :], hT.r(kc), kc == 0, kc == 7)
                        k_ = (jp * 2 + av) % NRING
                        sb_, fa = sta[k_], facc[k_]
                        actf(sb_.flat[:, 2:TT + 2], [sb_.r(2, TT + 2)], PS[pi][:, :], [pr(pi)], AF.Copy)
                        actf(fa.flat, [fa.r()], PS[pi][:, :], [pr(pi), pfm.r()], AF.Identity,
                             scale=fcw[:, ch, 2:3], bias=fcb[:, ch:ch + 1])
                        grp.append((ch, sb_, fa))
                    return grp

                def up_conv(jp, grp):
                    for (ch, sb_, fa) in grp:
                        cp("pool", sb_.flat[:, 0:2], [sb_.r(0, 2)], told.v[:, ch, :], [told.r(ch)])
                        cp("pool", tnew.v[:, ch, :], [tnew.r(ch)], sb_.flat[:, TT:TT + 2], [sb_.r(TT, TT + 2)])
                    for j in range(2):
                        for (ch, sb_, fa) in grp:
                            stt("dve", fa.flat, [fa.r()], sb_.flat[:, j:TT + j], fcw[:, ch, j:j + 1], fa.flat,
                                [sb_.r(j, TT + j), pfm.r(), fa.r()], ALU.mult, ALU.add)
                    g_ = ga[jp % 2]
                    fa_a = grp[0][2]
                    fa_v = grp[1][2]
                    actf(g_.flat, [g_.r()], fa_a.flat, [fa_a.r()], AF.Gelu)
                    tt("dve" if (jp >= 20 or jp % 3 == 0) else "pool", actT.v[:, jp, :], [actT.r(jp)], g_.flat, fa_v.flat, [g_.r(), fa_v.r()], ALU.mult)

                DEPTH = 2
                inflight = {}
                for jp in range(22 + DEPTH):
                    if jp < 22:
                        inflight[jp] = up_mm(jp)
                    if jp >= DEPTH:
                        up_conv(jp - DEPTH, inflight.pop(jp - DEPTH))
                proj_residual(actT, 22, "w_down", 1024)

                ck(6, T0)
                if l == 1:
                    ss2 = alloc((4,), F32)
                    junk2 = alloc((D,), BF16)
                    for s in range(4):
                        actf(junk2.flat, [junk2.r(), ss2.r()], xt.v[:, s, :], [xt.r(s)], AF.Square,
                             accum=ss2.flat[:, s:s + 1])
                    rsqrt_(ss2.flat, ss2.r(), 1.0 / D, EPS)
                    for s in range(4):
                        stt("dve", xt.v[:, s, :], [xt.r(s)], xt.v[:, s, :], ss2.flat[:, s:s + 1], fnw_bc.flat,
                            [xt.r(s), ss2.r(), fnw_bc.r()], ALU.mult, ALU.mult)
                for s in range(4):
                    dma("pool", x_dst[T0 + s * 128:T0 + (s + 1) * 128, :], xt.v[:, s, :], [xt.r(s)],
                        [("xmid" if l == 0 else "out", T0 + s * 128, T0 + (s + 1) * 128)],
                        "xst", outflag=(l == 1))
        S.emit(nc)


def _fm(v, k):
    return np.ascontiguousarray(v.reshape(k, 128).T)


def _wp(w):
    K, N = w.shape
    return np.ascontiguousarray(w.reshape(K // 128, 128, N).transpose(1, 0, 2))


def _prep_shared(inp):
    f32 = np.float32
    sh = {}
    cst = np.zeros((128, 328), f32)
    cst[64, 264:328] = 1.0
    cst[:, 0:128] = np.eye(128, dtype=f32)
    cst[:, 128:256] = np.triu(np.ones((128, 128), f32))
    half = ROPE // 2
    inv = (10000.0 ** (-np.arange(half, dtype=np.float32) / half)).astype(f32)
    for p in range(64, 96):
        cst[p, 256] = inv[(p - 64) % half]
        cst[p, 257] = np.pi / 2
        cst[p, 258] = np.pi if p < 80 else 0.0
    sh["cst"] = cst
    o = np.cumsum([0, 384, 256, 32, 512, 512, 512, 512, 8, 8, 1024, 1024])
    cq, ckv, kr, qm, km, vm, om, im, fm, ga, gb = [np.arange(o[i], o[i + 1]) for i in range(11)]
    krs = np.concatenate([kr[half:], kr[:half]])
    idx = np.concatenate([cq, ckv, ckv[:64], kr, ckv[:64], krs, qm, km, vm, om, im, fm, ga, gb])
    assert idx.size == 4944
    uq1, uq2 = [], []
    for h in range(NH):
        b = h * 96
        uq1.append(np.arange(b, b + 96))
        uq2.append(np.concatenate([np.arange(b, b + 64), np.arange(b + 64 + half, b + 96), np.arange(b + 64, b + 64 + half)]))
    uqi = np.concatenate(uq1 + uq2)
    kvi = np.concatenate([np.arange(h * 128, h * 128 + 64) for h in range(NH)] +
                         [np.arange(h * 128 + 64, h * 128 + 128) for h in range(NH)])
    upi = np.concatenate([np.concatenate([np.arange(j * 128, (j + 1) * 128), np.arange(DFF + j * 128, DFF + (j + 1) * 128)])
                          for j in range(22)])
    L = 2
    sh["w_in"] = np.stack([_wp(inp["w_in"][l][:, idx]) for l in range(L)])
    sh["w_uq"] = np.stack([_wp(inp["w_uq"][l][:, uqi]) for l in range(L)])
    sh["w_ukv"] = np.stack([_wp(inp["w_ukv"][l][:, kvi]) for l in range(L)])
    sh["w_bra"] = np.stack([_wp(inp["w_br_mla"][l]) for l in range(L)])
    sh["w_brb"] = np.stack([_wp(inp["w_br_mlstm"][l]) for l in range(L)])
    sh["w_out"] = np.stack([_wp(inp["w_out"][l]) for l in range(L)])
    sh["w_up"] = np.stack([_wp(inp["ffn_w_up"][l][:, upi]) for l in range(L)])
    sh["w_down"] = np.stack([_wp(inp["ffn_w_down"][l]) for l in range(L)])
    sh["ada_w"] = np.stack([_wp(inp["ada_w"][l]) for l in range(L)])
    pfm = np.zeros((L, 128, NPF), f32)
    prow = np.zeros((L, 1, NPR), f32)
    for l in range(L):
        pfm[l, :, 0:8] = _fm(inp["norm_mix_w"][l], 8)
        pfm[l, :, 8:16] = _fm(inp["norm_ffn_w"][l], 8)
        pfm[l, :, 16:19] = _fm(inp["q_norm_w"][l], 3)
        pfm[l, :, 19:21] = _fm(inp["kv_norm_w"][l], 2)
        cw = inp["mlstm_conv_w"][l]
        pfm[l, :, 21:53] = cw.reshape(4, 8, 128).transpose(2, 1, 0).reshape(128, 32)
        pfm[l, :, 53:61] = _fm(inp["mlstm_conv_b"][l], 8)
        fw = inp["ffn_conv_w"][l][:, upi]
        pfm[l, :, 61:193] = fw.reshape(3, 44, 128).transpose(2, 1, 0).reshape(128, 132)
        pfm[l, :, 193:237] = _fm(inp["ffn_conv_b"][l][upi], 44)
        pfm[l, :, 237:285] = _fm(inp["ada_b"][l], 48)
        prow[l, 0, 0:16] = inp["mlstm_gate_b"][l]
        prow[l, 0, 16:528] = inp["mlstm_head_norm_w"][l]
        prow[l, 0, 528:1552] = inp["ada_b"][l][2048:3072]
        prow[l, 0, 1552:2576] = inp["ada_b"][l][5120:6144]
    sh["pfm"] = pfm
    sh["prow"] = prow
    sh["fnw"] = np.ascontiguousarray(inp["final_norm_w"].reshape(1, D).astype(f32))
    return sh


_CACHE = {}


def kernel(**inputs):
    inp = {k: np.asarray(v) for k, v in inputs.items()}
    x = inp["x"]
    B, S_LEN, _ = x.shape
    sh = _prep_shared(inp)
    if S_LEN not in _CACHE:
        _CACHE[S_LEN] = build_nc(S_LEN)[0]
    nc = _CACHE[S_LEN]
    in_maps = []
    for b in range(B):
        m = dict(sh)
        m["x"] = np.ascontiguousarray(x[b].astype(np.float32))
        m["cT"] = _fm(inp["c"][b].astype(np.float32), 8)
        m["pos"] = np.ascontiguousarray(inp["positions"][b].reshape(1, S_LEN).astype(np.int32))
        in_maps.append(m)
    res = run_bass_kernel_spmd(nc, in_maps, core_ids=list(range(B)))
    out = np.stack([np.asarray(r["out"]) for r in res.results], axis=0)
    return out.astype(np.float32)
```

```python
import contextlib
import numpy as np
import concourse.bass as bass
import concourse.mybir as mybir
from concourse.bass_utils import run_bass_kernel_spmd

F32 = mybir.dt.float32
BF16 = mybir.dt.bfloat16
I32 = mybir.dt.int32
AF = mybir.ActivationFunctionType
ALU = mybir.AluOpType
AX = mybir.AxisListType

ENG = ["pe", "act", "dve", "pool", "sp"]


class Sched:
    def __init__(self):
        self.ops = {e: [] for e in ENG}
        self.space = {}
        self.seen = {e: {} for e in ENG}
        self.dcount = {}
        self.outkeys = set()
        self.marks = []

    def mark(self, label):
        self.marks.append((label, {e: len(v) for e, v in self.ops.items()}))

    def _overl(self, name, lo, hi):
        return [en for en in self.space.get(name, []) if en[0] < hi and lo < en[1]]

    def add(self, eng, fn, reads=(), writes=(), dma=None, out=False):
        writes = list(writes) + [r for r in reads if r[0].startswith("ps") and r not in writes]
        raw = {}
        oth = {}

        def merge(dst, src):
            for k, v in src.items():
                if dst.get(k, -1) < v:
                    dst[k] = v

        for (name, lo, hi) in reads:
            for en in self._overl(name, lo, hi):
                merge(raw, en[2])
        for (name, lo, hi) in writes:
            for en in self._overl(name, lo, hi):
                merge(oth, en[2])
                merge(oth, en[3])
        deps = {}
        is_dma = dma is not None
        for src, israw in ((raw, True), (oth, False)):
            for k, v in src.items():
                if k[0] == "e" and k[1] == eng and not is_dma:
                    if eng == "pe" or not israw:
                        continue
                if deps.get(k, -1) < v:
                    deps[k] = v
        waits = []
        seen = self.seen[eng]
        for k, v in deps.items():
            if k[0] == "d":
                v = self.dcount[k[1]]
            if seen.get(k, -1) >= v:
                continue
            seen[k] = v
            waits.append((k, v))
            if k[0] == "e":
                self.ops[k[1]][v]["needed"] = True
        idx = len(self.ops[eng])
        if is_dma:
            self.dcount[dma] = self.dcount.get(dma, 0) + 1
            tok = (("d", dma), self.dcount[dma])
            if out:
                self.outkeys.add(dma)
        else:
            tok = (("e", eng), idx)
        import sys as _sys
        f = _sys._getframe(1)
        src = []
        for _ in range(4):
            if f is None:
                break
            src.append(f.f_lineno)
            f = f.f_back
        self.ops[eng].append(dict(fn=fn, waits=waits, needed=False, dma=dma, src=tuple(src)))
        for (name, lo, hi) in reads:
            lst = self.space.setdefault(name, [])
            for en in lst:
                if en[0] == lo and en[1] == hi:
                    if en[3].get(tok[0], -1) < tok[1]:
                        en[3][tok[0]] = tok[1]
                    break
            else:
                lst.append([lo, hi, {}, {tok[0]: tok[1]}])
        for (name, lo, hi) in writes:
            lst = self.space.setdefault(name, [])
            lst[:] = [en for en in lst if not (lo <= en[0] and en[1] <= hi)]
            lst.append([lo, hi, {tok[0]: tok[1]}, {}])

    def emit(self, nc, final_eng="sp"):
        fw = []
        for key, cnt in self.dcount.items():
            fw.append((("d", key), cnt))
        self.ops[final_eng].append(dict(fn=None, waits=fw, needed=False, dma=None))
        with contextlib.ExitStack() as st:
            esem = {e: st.enter_context(nc.semaphore("se_" + e)) for e in ENG}
            dsem = {k: st.enter_context(nc.semaphore("sd_%d" % i))
                    for i, k in enumerate(self.dcount.keys())}
            val = {}
            for e in ENG:
                c = 0
                for i, op in enumerate(self.ops[e]):
                    if op["needed"]:
                        c += 1
                        val[(e, i)] = c
            block = st.enter_context(nc.Block())

            def run(e, engine):
                for i, op in enumerate(self.ops[e]):
                    for (k, v) in op["waits"]:
                        if k[0] == "e":
                            engine.wait_ge(esem[k[1]], val[(k[1], v)])
                        else:
                            engine.wait_ge(dsem[k[1]], 16 * v)
                    if op["fn"] is None:
                        continue
                    ins = op["fn"](engine)
                    if op["dma"] is not None:
                        ins.then_inc(dsem[op["dma"]], 16)
                    elif op["needed"]:
                        ins.then_inc(esem[e], 1)

            @block.sync
            def _(eng):
                run("sp", eng)

            @block.scalar
            def _(eng):
                run("act", eng)

            @block.vector
            def _(eng):
                run("dve", eng)

            @block.gpsimd
            def _(eng):
                run("pool", eng)

            @block.tensor
            def _(eng):
                run("pe", eng)


D = 1024
NH = 8
QL, KVL, ROPE = 384, 256, 32
DFF = 2816
EPS = 1e-6
TT = 512
NPF = 285
NPR = 2576
W_SHAPES = dict(w_in=(8, 4944), w_uq=(3, 1536), w_ukv=(2, 1024), w_bra=(4, 1024),
                w_brb=(4, 1024), w_out=(8, 1024), w_up=(8, 5632), w_down=(22, 1024))
W_ORDER = ["w_in", "w_uq", "w_ukv", "w_bra", "w_brb", "w_out", "w_up", "w_down"]
C_CQ, C_CKV, C_KR, C_KRS, C_QM, C_KM, C_VM, C_OM, C_IF, C_GA, C_GB = (
    0, 384, 640, 736, 832, 1344, 1856, 2368, 2880, 2896, 3920)


def _prod(s):
    r = 1
    for a in s:
        r *= a
    return r


class Buf:
    def __init__(self, SB, off, shape, dt):
        self.es = 1 if dt == BF16 else 2
        self.shape = tuple(shape)
        n = _prod(shape)
        self.off = off
        self.units = n * self.es
        flat = SB[:, off:off + self.units]
        if dt != BF16:
            flat = flat.bitcast(dt)
        self.flat = flat
        if len(shape) == 1:
            self.v = flat
        elif len(shape) == 2:
            self.v = flat.rearrange("p (a b) -> p a b", b=shape[1])
        else:
            self.v = flat.rearrange("p (a b c) -> p a b c", b=shape[1], c=shape[2])
        self.inner = (n // shape[0]) * self.es

    def r(self, i0=None, i1=None):
        if i0 is None:
            return ("SB", self.off, self.off + self.units)
        if i1 is None:
            i1 = i0 + 1
        return ("SB", self.off + i0 * self.inner, self.off + i1 * self.inner)


class _Stop(Exception):
    pass


def build_nc(S_LEN):
    import os
    STOP = int(os.environ.get("KSTOP", "99"))
    holder = {}
    global _LAST_HOLDER
    _LAST_HOLDER = holder
    try:
        _build_nc(S_LEN, STOP, holder)
    except _Stop:
        pass
    return holder["nc"], holder["S"]


def _build_nc(S_LEN, STOP, holder):
    import os
    NT = S_LEN // TT
    NB = S_LEN // 128
    nc = bass.Bass("TRN2", target_bir_lowering=False)
    S = Sched()
    holder["nc"] = nc
    holder["S"] = S
    din = {}

    def inp(name, shape, dt=F32):
        din[name] = nc.dram_tensor(name, list(shape), dt, kind="ExternalInput").ap()
        return din[name]

    x_in = inp("x", [S_LEN, D])
    cT_in = inp("cT", [128, 8])
    pos_in = inp("pos", [1, S_LEN], I32)
    cst_in = inp("cst", [128, 328])
    pfm_in = inp("pfm", [2, 128, NPF])
    prow_in = inp("prow", [2, 1, NPR])
    fnw_in = inp("fnw", [1, D])
    ada_in = inp("ada_w", [2, 128, 8, 6144])
    w_in_d = {n: inp(n, [2, 128, kc, N]) for n, (kc, N) in W_SHAPES.items()}
    out_d = nc.dram_tensor("out", [S_LEN, D], F32, kind="ExternalOutput").ap()
    ws_d = {n: nc.dram_tensor("ws_" + n, [2, 128, kc, N], BF16, kind="Internal").ap()
            for n, (kc, N) in W_SHAPES.items()}
    kc_d = nc.dram_tensor("kcache", [NH, 96, S_LEN], BF16, kind="Internal").ap()
    vc_d = nc.dram_tensor("vcache", [NH, 128, NB, 65], BF16, kind="Internal").ap()
    xmid_d = nc.dram_tensor("xmid", [S_LEN, D], F32, kind="Internal").ap()

    with contextlib.ExitStack() as st:
        NUNITS = 106400
        SB = st.enter_context(nc.sbuf_tensor("SB", [128, NUNITS], BF16))
        PS = [st.enter_context(nc.psum_tensor("ps%d" % i, [128, 512], F32)) for i in range(8)]
        ptr = [0]

        def alloc(shape, dt):
            b = Buf(SB, ptr[0], shape, dt)
            ptr[0] += b.units + (b.units & 1)
            assert ptr[0] <= NUNITS, ptr[0]
            return b

        def pr(i):
            return ("ps%d" % i, 0, 512)

        def mm(pi, out, lhsT, lr, rhs, rr, start, stop, skip=False):
            if skip:
                S.add("pe", lambda e: e.matmul(out, lhsT=lhsT, rhs=rhs, start=start, stop=stop, skip_group_check=True),
                      reads=[lr, rr], writes=[pr(pi)])
            else:
                S.add("pe", lambda e: e.matmul(out, lhsT=lhsT, rhs=rhs, start=start, stop=stop),
                      reads=[lr, rr], writes=[pr(pi)])

        def actf(out, wr, in_, rd, func, scale=1.0, bias=None, accum=None, eng="act"):
            kw = {}
            if bias is not None:
                kw["bias"] = bias
            if accum is not None:
                kw["accum_out"] = accum
            S.add(eng, lambda e: e.activation(out=out, in_=in_, func=func, scale=scale, **kw),
                  reads=rd, writes=wr)

        def tt(eng, out, wr, in0, in1, rd, op):
            S.add(eng, lambda e: e.tensor_tensor(out=out, in0=in0, in1=in1, op=op), reads=rd, writes=wr)

        def ts(eng, out, wr, in0, rd, s1, s2, op0, op1=None):
            if op1 is None:
                S.add(eng, lambda e: e.tensor_scalar(out=out, in0=in0, scalar1=s1, scalar2=None, op0=op0),
                      reads=rd, writes=wr)
            else:
                S.add(eng, lambda e: e.tensor_scalar(out=out, in0=in0, scalar1=s1, scalar2=s2, op0=op0, op1=op1),
                      reads=rd, writes=wr)

        def stt(eng, out, wr, in0, scalar, in1, rd, op0, op1):
            S.add(eng, lambda e: e.scalar_tensor_tensor(out=out, in0=in0, scalar=scalar, in1=in1, op0=op0, op1=op1),
                  reads=rd, writes=wr)

        def cp(eng, out, wr, in_, rd):
            S.add(eng, lambda e: e.tensor_copy(out=out, in_=in_), reads=rd, writes=wr)

        def memset(eng, out, wr, val):
            S.add(eng, lambda e: e.memset(out, val), reads=[], writes=wr)

        def dma(eng, out, in_, rd, wr, key, outflag=False):
            S.add(eng, lambda e: e.dma_start(out=out, in_=in_), reads=rd, writes=wr, dma=key, out=outflag)

        def recip(eng, out, wr, in_, rd):
            S.add(eng, lambda e: e.reciprocal(out=out, in_=in_), reads=rd, writes=wr)

        def rsqrt_(buf_ap, res, scale, eps_):
            ts("dve", buf_ap, [res], buf_ap, [res], scale, eps_, ALU.mult, ALU.add)
            actf(buf_ap, [res], buf_ap, [res], AF.Sqrt)
            recip("dve", buf_ap, [res], buf_ap, [res])

        DBG = os.environ.get("KDBG")
        dbg_layout = {}
        holder["dbg_layout"] = dbg_layout
        if DBG:
            dbg_d = nc.dram_tensor("dbg", [128, 32768], F32, kind="ExternalOutput").ap()
            dbgbuf = alloc((4096,), F32)
        dbg_col = [0]

        def dump(name, buf, n=None):
            if not DBG or name in dbg_layout:
                return
            n = n or (buf.units // buf.es)
            c0 = dbg_col[0]
            dbg_layout[name] = (c0, n)
            done = 0
            while done < n:
                m = min(4096, n - done)
                cp("dve", dbgbuf.flat[:, 0:m], [dbgbuf.r()], buf.flat[:, done:done + m], [buf.r()])
                dma("pool", dbg_d[:, c0 + done:c0 + done + m], dbgbuf.flat[:, 0:m], [dbgbuf.r()],
                    [("dbg", c0 + done, c0 + done + m)], "dbgst", outflag=True)
                done += m
            dbg_col[0] += n

        def ck(k, T0=0):
            S.mark("ck%d" % k)
            if STOP == k:
                if k >= 2:
                    for s_ in range(4):
                        dma("pool", out_d[T0 + s_ * 128:T0 + (s_ + 1) * 128, :], xt.v[:, s_, :], [xt.r(s_)],
                            [("out", 0, 1)], "xst", outflag=True)
                S.emit(nc)
                raise _Stop()

        grot = [0]

        def G():
            i = 2 + grot[0] % 6
            grot[0] += 1
            return i

        evr = [0]

        def EV():
            evr[0] += 1
            return "act" if evr[0] % 2 else "dve"

        def evac(out, wr, in_, rd, eng=None, scale=None):
            eng = eng or EV()
            if eng == "act":
                actf(out, wr, in_, rd, AF.Copy if scale is None else AF.Copy, scale=1.0 if scale is None else scale)
            else:
                if scale is None:
                    cp("dve", out, wr, in_, rd)
                else:
                    ts("dve", out, wr, in_, rd, scale, None, ALU.mult)

        cst = alloc((328,), F32)
        identf = cst.v[:, 0:128]
        triUf = cst.v[:, 128:256]
        identb = alloc((128,), BF16)
        onesb = alloc((128,), BF16)
        onesf = alloc((128,), F32)
        triUb = alloc((128,), BF16)
        pfm = alloc((NPF,), F32)
        modT = alloc((48,), F32)
        gm = alloc((16,), F32)
        hnw_bc = alloc((512,), F32)
        gateb_bc = alloc((16,), F32)
        adabg_bc = alloc((2048,), F32)
        gbc = adabg_bc
        fnw_bc = alloc((1024,), F32)
        cact = alloc((8,), F32)
        cactb = alloc((8,), BF16)
        cbc = alloc((8, 128), F32)
        S32 = alloc((4, 65), F32)
        Sbf2 = [alloc((4, 65), BF16) for _ in range(2)]
        qkhalo = alloc((8, 3), F32)
        ftail = [alloc((44, 2), F32) for _ in range(2)]
        NW = 5
        wring = [alloc((4096,), BF16) for _ in range(NW)]
        KH = [alloc((S_LEN,), BF16) for _ in range(2)]
        VH = [alloc((NB, 65), BF16) for _ in range(2)]
        xtb = [alloc((4, D), F32) for _ in range(2)]
        xt = xtb[0]
        hT = alloc((8, TT), BF16)
        yTa = alloc((4, TT), BF16)
        yTb = alloc((4, TT), BF16)
        ARENA = ptr[0]

        def phase():
            ptr[0] = ARENA

        wr_i = [0]

        def wload(l, name, k0, k1, c0, c1, src=None, eng="sp"):
            slot = wr_i[0] % NW
            wr_i[0] += 1
            n = (k1 - k0) * (c1 - c0)
            assert n <= 4096
            view = wring[slot].flat[:, 0:n].rearrange("p (k n) -> p k n", n=c1 - c0)
            if src is None:
                dma(eng, view, ws_d[name][l, :, k0:k1, c0:c1], [("ws_%d_%s" % (l, name), 0, 64)],
                    [wring[slot].r()], "wr%d" % slot)
            else:
                dma(eng, view, src[l, :, k0:k1, c0:c1], [], [wring[slot].r()], "wr%d" % slot)
            return view, wring[slot].r()

        dma("sp", cst.flat, cst_in, [], [cst.r()], "cst")
        cp("dve", identb.flat, [identb.r()], identf, [cst.r()])
        cp("dve", triUb.flat, [triUb.r()], triUf, [cst.r()])
        memset("dve", onesb.flat, [onesb.r()], 1.0)
        memset("dve", onesf.flat, [onesf.r()], 1.0)
        dma("sp", fnw_bc.flat, fnw_in.partition_broadcast(128), [], [fnw_bc.r()], "fnw")
        dma("sp", cact.flat, cT_in, [], [cact.r()], "cact")
        actf(cact.flat, [cact.r()], cact.flat, [cact.r()], AF.Silu)
        cp("dve", cactb.flat, [cactb.r()], cact.flat, [cact.r()])
        for k in range(8):
            cp("dve", cbc.v[:, k, :], [cbc.r(k)], cact.flat[:, k:k + 1].to_broadcast([128, 128]), [cact.r()])
        cast_list = {l_: [(name, c) for name in W_ORDER for c in range(W_SHAPES[name][0])] for l_ in range(2)}

        def emit_casts(l, n=None):
            lst = cast_list[l]
            k = len(lst) if n is None else min(n, len(lst))
            for (name, c) in lst[:k]:
                dma("pool", ws_d[name][l, :, c, :], w_in_d[name][l, :, c, :], [],
                    [("ws_%d_%s" % (l, name), c, c + 1)], "ws_%d_%s" % (l, name))
            del lst[:k]

        for s_ in range(4):
            dma("pool", xtb[0].v[:, s_, :], x_in[s_ * 128:(s_ + 1) * 128, :], [], [xtb[0].r(s_)], "xt0")
        emit_casts(0, 13)

        SCALE = float((64 + 32) ** -0.5)
        ck(0)

        for l in range(2):
            x_src = x_in if l == 0 else xmid_d
            x_dst = xmid_d if l == 0 else out_d
            xsrc_res = [] if l == 0 else [("xmid", 0, S_LEN)]
            phase()
            dma("sp", pfm.flat, pfm_in[l], [], [pfm.r()], "pfm")
            dma("sp", hnw_bc.flat, prow_in[l, :, 16:528].partition_broadcast(128), [], [hnw_bc.r()], "hnw")
            dma("sp", gateb_bc.flat, prow_in[l, :, 0:16].partition_broadcast(128), [], [gateb_bc.r()], "gateb")
            dma("sp", adabg_bc.flat, prow_in[l, :, 528:2576].partition_broadcast(128), [], [adabg_bc.r()], "adabg")
            nmw = pfm.v[:, 0:8]
            nfw = pfm.v[:, 8:16]
            qnw = pfm.v[:, 16:19]
            kvnw = pfm.v[:, 19:21]
            mcw = pfm.v[:, 21:53].rearrange("p (c j) -> p c j", j=4)
            mcb = pfm.v[:, 53:61]
            fcw = pfm.v[:, 61:193].rearrange("p (c j) -> p c j", j=3)
            fcb = pfm.v[:, 193:237]
            adabT = pfm.v[:, 237:285]
            pT = 0
            for j in range(12):
                slabs_ = []
                for hk in range(2):
                    slot = wr_i[0] % NW
                    wr_i[0] += 1
                    slab = wring[slot].flat[:, 0:4096].bitcast(F32).rearrange("p (k n) -> p k n", n=512)
                    sr = wring[slot].r()
                    dma("sp", slab, ada_in[l, :, hk * 4:(hk + 1) * 4, j * 512:(j + 1) * 512], [], [sr], "wr%d" % slot)
                    slabs_.append((slab, sr))
                for m in range(4):
                    col = j * 4 + m
                    for kc in range(8):
                        slab, sr = slabs_[kc // 4]
                        mm(pT, PS[pT][:, col:col + 1], slab[:, kc % 4, m * 128:(m + 1) * 128], sr,
                           cact.flat[:, kc:kc + 1], cact.r(), kc == 0, kc == 7)
                if j in (4, 5, 10, 11):
                    gi = {4: 0, 5: 1, 10: 2, 11: 3}[j]
                    pb = G()
                    for kc in range(8):
                        slab, sr = slabs_[kc // 4]
                        mm(pb, PS[pb][:, :], cbc.v[:, kc, :], cbc.r(), slab[:, kc % 4, :], sr, kc == 0, kc == 7)
                    tt("dve", gbc.flat[:, gi * 512:(gi + 1) * 512], [gbc.r()], PS[pb][:, :],
                       gbc.flat[:, gi * 512:(gi + 1) * 512], [pr(pb), gbc.r()], ALU.add)
            tt("dve", modT.flat, [modT.r()], PS[pT][:, 0:48], adabT, [pr(pT), pfm.r()], ALU.add)
            stt("dve", gm.flat[:, 0:8], [gm.r()], modT.flat[:, 8:16], 1.0, nmw, [modT.r(), pfm.r()], ALU.add, ALU.mult)
            stt("dve", gm.flat[:, 8:16], [gm.r()], modT.flat[:, 32:40], 1.0, nfw, [modT.r(), pfm.r()], ALU.add, ALU.mult)
            ck(1)
            memset("dve", S32.flat, [S32.r()], 0.0)
            memset("dve", Sbf2[0].flat, [Sbf2[0].r()], 0.0)
            memset("dve", qkhalo.flat, [qkhalo.r()], 0.0)
            memset("dve", ftail[0].flat, [ftail[0].r()], 0.0)

            def norm_stage(gcol, shcol, xs):
                ss = alloc((4,), F32)
                junk = alloc((D,), BF16)
                for s in range(4):
                    actf(junk.flat, [junk.r()], xt.v[:, s, :], [xt.r(s)], AF.Square, accum=ss.flat[:, s:s + 1],
                         )
                    S.ops["act"][-1]
                return ss, junk

            for t in range(NT):
                T0 = t * TT
                phase()
                xt = xtb[(l * NT + t) % 2]

                def do_norm(gcol, shcol):
                    xs = alloc((4, D), BF16)
                    ss = alloc((4,), F32)
                    junk = alloc((D,), BF16)
                    for s in range(4):
                        actf(junk.flat, [junk.r(), ss.r()], xt.v[:, s, :], [xt.r(s)], AF.Square,
                             accum=ss.flat[:, s:s + 1])
                    rsqrt_(ss.flat, ss.r(), 1.0 / D, EPS)
                    for s in range(4):
                        if s % 2:
                            ts("dve", xs.v[:, s, :], [xs.r(s)], xt.v[:, s, :], [xt.r(s), ss.r()],
                               ss.flat[:, s:s + 1], None, ALU.mult)
                        else:
                            actf(xs.v[:, s, :], [xs.r(s)], xt.v[:, s, :], [xt.r(s), ss.r()], AF.Copy,
                                 scale=ss.flat[:, s:s + 1])
                    for c in range(8):
                        pi = G()
                        for s in range(4):
                            mm(pi, PS[pi][:, s * 128:(s + 1) * 128], xs.v[:, s, c * 128:(c + 1) * 128], xs.r(s),
                               identb.flat, identb.r(), True, True)
                        if c % 2:
                            actf(hT.v[:, c, :], [hT.r(c)], PS[pi][:, :], [pr(pi), gm.r(), modT.r()], AF.Identity,
                                 scale=gm.flat[:, gcol + c:gcol + c + 1], bias=modT.flat[:, shcol + c:shcol + c + 1])
                        else:
                            ts("dve", hT.v[:, c, :], [hT.r(c)], PS[pi][:, :], [pr(pi), gm.r(), modT.r()],
                               gm.flat[:, gcol + c:gcol + c + 1], modT.flat[:, shcol + c:shcol + c + 1],
                               ALU.mult, ALU.add)

                do_norm(0, 0)
                dump("hT", hT)
                ck(2, T0)

                phase()
                cqw = alloc((3, TT), BF16)
                sq = alloc((3, TT), BF16)
                ckvw = alloc((2, TT), BF16)
                rstdq = alloc((TT,), F32)
                rstdkv = alloc((TT,), F32)
                rstdkvt = alloc((4,), F32)
                rden = [alloc((TT,), F32) for _ in range(2)]
                posi = Buf(SB, rden[0].off, (TT,), I32)
                ang = alloc((TT,), F32)
                angk = alloc((TT,), F32)
                anki = Buf(SB, rden[1].off, (TT,), I32)
                cosT = alloc((TT,), F32)
                sinT = alloc((TT,), F32)
                cosr = alloc((TT,), F32)
                sinr = alloc((TT,), F32)
                t1 = alloc((TT,), F32)
                t2 = alloc((TT,), F32)
                QT = alloc((NH, TT), BF16)
                KTc = alloc((NH, TT), BF16)
                Vc = alloc((NH, 4, 65), BF16)
                PT = [alloc((TT,), BF16) for _ in range(4)]
                accsb = [alloc((TT,), F32) for _ in range(2)]
                for a_ in accsb:
                    memset("pool", a_.flat, [a_.r()], 0.0)
                RP = slice(64, 96)
                if t > 0:
                    for hh_ in range(2):
                        dma("pool", KH[hh_].flat[0:96, 0:T0], kc_d[hh_, :, 0:T0], [("kc", 0, T0)], [KH[hh_].r()], "kh%d" % hh_)
                        dma("pool", VH[hh_].v[:, 0:4 * t, :], vc_d[hh_, :, 0:4 * t, :], [("vc", 0, 4 * t)], [VH[hh_].r()], "vh%d" % hh_)
                dma("pool", posi.flat[RP, :], pos_in[:, T0:T0 + TT].partition_broadcast(32), [], [posi.r()], "posi")
                cp("dve", ang.flat[RP, :], [ang.r()], posi.flat[RP, :], [posi.r()])
                for (tab, phcol) in ((cosT, 257), (sinT, 258)):
                    ts("dve", angk.flat[RP, :], [angk.r()], ang.flat[RP, :], [ang.r(), cst.r()],
                       cst.v[RP, 256:257], cst.v[RP, phcol:phcol + 1], ALU.mult, ALU.add)
                    ts("dve", anki.flat[RP, :], [anki.r()], angk.flat[RP, :], [angk.r()],
                       float(1.0 / (2 * np.pi)), None, ALU.mult)
                    cp("dve", tab.flat[RP, :], [tab.r()], anki.flat[RP, :], [anki.r()])
                    stt("dve", angk.flat[RP, :], [angk.r()], tab.flat[RP, :], float(-2 * np.pi), angk.flat[RP, :],
                        [tab.r(), angk.r()], ALU.mult, ALU.add)
                    S.add("dve", lambda e, tab=tab: e.tensor_single_scalar(out=tab.flat[RP, :], in_=angk.flat[RP, :],
                                                                          scalar=float(np.pi), op=ALU.is_gt),
                          reads=[angk.r()], writes=[tab.r()])
                    stt("dve", angk.flat[RP, :], [angk.r()], tab.flat[RP, :], float(-2 * np.pi), angk.flat[RP, :],
                        [tab.r(), angk.r()], ALU.mult, ALU.add)
                    actf(tab.flat[RP, :], [tab.r()], angk.flat[RP, :], [angk.r()], AF.Sin)
                ck(21, T0)
                sq2 = alloc((2, TT), BF16)
                slabA, rA = wload(l, "w_in", 0, 8, C_CQ, C_CQ + 384)
                slabB, rB = wload(l, "w_in", 0, 8, C_CKV, C_CKV + 448)
                for m in range(3):
                    pi = G()
                    for kc in range(8):
                        mm(pi, PS[pi][:, :], slabA[:, kc, m * 128:(m + 1) * 128], rA, hT.v[:, kc, :], hT.r(kc),
                           kc == 0, kc == 7)
                    ts("dve", cqw.v[:, m, :], [cqw.r(m)], PS[pi][:, :], [pr(pi), pfm.r()], qnw[:, m:m + 1], None, ALU.mult)
                    actf(sq.v[:, m, :], [sq.r(m)], PS[pi][:, :], [pr(pi)], AF.Square)
                for m in range(2):
                    pi = G()
                    for kc in range(8):
                        mm(pi, PS[pi][:, :], slabB[:, kc, m * 128:(m + 1) * 128], rB, hT.v[:, kc, :], hT.r(kc),
                           kc == 0, kc == 7)
                    ts("dve", ckvw.v[:, m, :], [ckvw.r(m)], PS[pi][:, :], [pr(pi), pfm.r()], kvnw[:, m:m + 1], None, ALU.mult)
                    actf(sq2.v[:, m, :], [sq2.r(m)], PS[pi][:, :], [pr(pi)], AF.Square)
                pk1 = G()
                pk2 = G()
                for kc in range(8):
                    mm(pk1, PS[pk1][0:96, :], slabB[:, kc, 256:352], rB, hT.v[:, kc, :], hT.r(kc), kc == 0, kc == 7)
                for kc in range(8):
                    mm(pk2, PS[pk2][0:96, :], slabB[:, kc, 352:448], rB, hT.v[:, kc, :], hT.r(kc), kc == 0, kc == 7)
                tt("dve", t1.flat[RP, :], [t1.r()], PS[pk1][RP, :], cosT.flat[RP, :], [pr(pk1), cosT.r()], ALU.mult)
                tt("dve", t2.flat[RP, :], [t2.r()], PS[pk2][RP, :], sinT.flat[RP, :], [pr(pk2), sinT.r()], ALU.mult)
                tt("dve", t1.flat[RP, :], [t1.r()], t1.flat[RP, :], t2.flat[RP, :], [t1.r(), t2.r()], ALU.add)
                for h in range(NH):
                    cp("pool" if h % 2 else "dve", KTc.v[RP, h, :], [KTc.r(h)], t1.flat[RP, :], [t1.r()])
                pq = G()
                for m in range(3):
                    mm(pq, PS[pq][:, :], onesb.flat, onesb.r(), sq.v[:, m, :], sq.r(m), m == 0, m == 2)
                cp("dve", rstdq.flat, [rstdq.r()], PS[pq][:, :], [pr(pq)])
                pkv = G()
                for m in range(2):
                    mm(pkv, PS[pkv][:, :], onesb.flat, onesb.r(), sq2.v[:, m, :], sq2.r(m), m == 0, m == 1)
                cp("dve", rstdkv.flat, [rstdkv.r()], PS[pkv][:, :], [pr(pkv)])
                pkt = G()
                for s in range(4):
                    for m in range(2):
                        mm(pkt, PS[pkt][:, s:s + 1], sq2.v[:, m, s * 128:(s + 1) * 128], sq2.r(m), onesb.flat[:, 0:1], onesb.r(),
                           m == 0, m == 1)
                cp("dve", rstdkvt.flat, [rstdkvt.r()], PS[pkt][:, 0:4], [pr(pkt)])
                rsqrt_(rstdq.flat, rstdq.r(), 1.0 / QL, EPS)
                rsqrt_(rstdkv.flat, rstdkv.r(), 1.0 / KVL, EPS)
                rsqrt_(rstdkvt.flat, rstdkvt.r(), 1.0 / KVL, EPS)
                tt("dve", cosr.flat[RP, :], [cosr.r()], cosT.flat[RP, :], rstdq.flat[RP, :], [cosT.r(), rstdq.r()], ALU.mult)
                tt("dve", sinr.flat[RP, :], [sinr.r()], sinT.flat[RP, :], rstdq.flat[RP, :], [sinT.r(), rstdq.r()], ALU.mult)
                ck(215, T0)
                slabKV, rKV = wload(l, "w_ukv", 0, 2, 0, 1024)
                for h in range(NH):
                    pi = G()
                    for kc in range(2):
                        mm(pi, PS[pi][0:64, :], slabKV[:, kc, h * 64:(h + 1) * 64], rKV, ckvw.v[:, kc, :], ckvw.r(kc),
                           kc == 0, kc == 1)
                    tt("dve", KTc.v[0:64, h, :], [KTc.r(h)], PS[pi][0:64, :], rstdkv.flat[0:64, :], [pr(pi), rstdkv.r()], ALU.mult)
                memset("pool", Vc.v[:, :, :, 64:65], [Vc.r()], 1.0)
                for s in range(4):
                    pi = G()
                    for kc in range(2):
                        mm(pi, PS[pi][:, :], ckvw.v[:, kc, s * 128:(s + 1) * 128], ckvw.r(kc), slabKV[:, kc, 512:1024], rKV,
                           kc == 0, kc == 1)
                    ts("dve", Vc.v[:, :, s, 0:64], [Vc.r()], PS[pi][:, :].rearrange("p (h e) -> p h e", e=64),
                       [pr(pi), rstdkvt.r()], rstdkvt.flat[:, s:s + 1], None, ALU.mult)
                ck(22, T0)
                slabQ1, rQ1 = wload(l, "w_uq", 0, 3, 0, 768)
                slabQ2, rQ2 = wload(l, "w_uq", 0, 3, 768, 1536)
                tq = [(t1, t2), (ang, angk)]
                for h in range(NH):
                    p1 = G()
                    p2 = G()
                    ta, tb = tq[h % 2]
                    for kc in range(3):
                        mm(p1, PS[p1][0:96, :], slabQ1[:, kc, h * 96:(h + 1) * 96], rQ1, cqw.v[:, kc, :], cqw.r(kc),
                           kc == 0, kc == 2)
                    for kc in range(3):
                        mm(p2, PS[p2][0:96, :], slabQ2[:, kc, h * 96:(h + 1) * 96], rQ2, cqw.v[:, kc, :], cqw.r(kc),
                           kc == 0, kc == 2)
                    tt("dve", QT.v[0:64, h, :], [QT.r(h)], PS[p1][0:64, :], rstdq.flat[0:64, :], [pr(p1), rstdq.r()], ALU.mult)
                    tt("dve", ta.flat[RP, :], [ta.r()], PS[p1][RP, :], cosr.flat[RP, :], [pr(p1), cosr.r()], ALU.mult)
                    tt("dve", tb.flat[RP, :], [tb.r()], PS[p2][RP, :], sinr.flat[RP, :], [pr(p2), sinr.r()], ALU.mult)
                    tt("pool", QT.v[RP, h, :], [QT.r(h)], ta.flat[RP, :], tb.flat[RP, :], [ta.r(), tb.r()], ALU.add)
                if l == 0 and t == 0:
                    emit_casts(0)
                if t < NT - 1:
                    dma("pool", kc_d[:, :, T0:T0 + TT].rearrange("h d t -> d h t"), KTc.v[0:96, :, :], [KTc.r()],
                        [("kc", T0, T0 + TT)], "kcs")
                    dma("pool", vc_d[:, :, 4 * t:4 * t + 4, :].rearrange("h p s e -> p h s e"), Vc.v, [Vc.r()],
                        [("vc", 4 * t, 4 * t + 4)], "vcs")
                dump("QT", QT)
                dump("KTc", KTc)
                dump("Vc", Vc)
                ck(23, T0)
                LA = 2
                blocks = []
                for h in range(NH):
                    for kb in range(4 * t + 4):
                        blocks.append((h, kb))
                nkb = 4 * t + 4
                pend = {}

                def load_hist(hh):
                    sl_ = hh % 2
                    dma("pool", KH[sl_].flat[0:96, 0:T0], kc_d[hh, :, 0:T0], [("kc", 0, T0)], [KH[sl_].r()], "kh%d" % sl_)
                    dma("pool", VH[sl_].v[:, 0:4 * t, :], vc_d[hh, :, 0:4 * t, :], [("vc", 0, 4 * t)], [VH[sl_].r()], "vh%d" % sl_)

                def emit_score(i):
                    h, kb = blocks[i]
                    sl = h % 2
                    if t > 0 and kb == min(LA + 1, nkb - 1) and h + 1 < NH and h >= 1:
                        load_hist(h + 1)
                    if kb < 4 * t:
                        Ks, Kr = KH[sl].flat[0:96, kb * 128:(kb + 1) * 128], KH[sl].r()
                        Vs, Vr = VH[sl].v[:, kb, :], VH[sl].r()
                        q0 = 0
                    else:
                        j = kb - 4 * t
                        Ks, Kr = KTc.v[0:96, h, j * 128:(j + 1) * 128], KTc.r(h)
                        Vs, Vr = Vc.v[:, h, j, :], Vc.r()
                        q0 = j * 128
                    pi = G()
                    mm(pi, PS[pi][:, q0:TT], Ks, Kr, QT.v[0:96, h, q0:TT], QT.r(h), True, True)
                    pt = PT[i % len(PT)]
                    actf(pt.flat[:, q0:TT], [pt.r()], PS[pi][:, q0:TT], [pr(pi)], AF.Exp, scale=SCALE)
                    if kb >= 4 * t:
                        memset("pool", pt.flat[64:128, q0:q0 + 64], [pt.r()], 0.0)
                    pend[i] = (pt, Vs, Vr, q0)

                def emit_pv(i):
                    h, kb = blocks[i]
                    pt, Vs, Vr, q0 = pend.pop(i)
                    ai = h % 2
                    mm(ai, PS[ai][0:65, q0:TT], Vs, Vr, pt.flat[:, q0:TT], pt.r(), kb == 0, kb == nkb - 1)
                    if kb == nkb - 1:
                        ab = accsb[h % 2]
                        rd = rden[h % 2]
                        cp("dve", ab.flat[0:65, :], [ab.r()], PS[ai][0:65, :], [pr(ai)])
                        pd = G()
                        mm(pd, PS[pd][0:64, :], cst.v[:, 264:328], cst.r(), ab.flat, ab.r(), True, True)
                        recip("dve", rd.flat[0:64, :], [rd.r()], PS[pd][0:64, :], [pr(pd)])
                        r0 = (h % 2) * 64
                        tt("dve", yTa.v[r0:r0 + 64, h // 2, :], [yTa.r(h // 2)], ab.flat[0:64, :], rd.flat[0:64, :],
                           [ab.r(), rd.r()], ALU.mult)

                for i in range(len(blocks) + LA):
                    if i < len(blocks):
                        emit_score(i)
                    if i >= LA:
                        emit_pv(i - LA)
                dump("yTa", yTa)
                ck(3, T0)
                phase()
                if l == 0:
                    emit_casts(1, None if t == NT - 1 else (0 if (NT > 1 and t == 0) else 10))
                qkpre = alloc((8, TT + 3), F32)
                cacc = [alloc((TT,), F32) for _ in range(2)]
                QmZ = alloc((8, TT), BF16)
                memset("pool", QmZ.flat, [QmZ.r()], 0.0)
                KmT = alloc((4, TT), BF16)
                Km = alloc((4, TT), BF16)
                Vp = alloc((4, NH, 65), BF16)
                og = alloc((4, TT), BF16)
                gpre = alloc((4, 16), F32)
                lf = alloc((4, 8), F32)
                bcs = alloc((4, 8), F32)
                uu = alloc((4, 8), F32)
                bnd = alloc((4, 8), F32)
                ebl = alloc((8,), F32)
                pmall = [[alloc((4, 128), BF16) for _ in range(2)] for _ in range(4)]
                dd = alloc((8,), F32)
                hbuf4 = alloc((4, NH, 64), F32)
                hsq = alloc((NH, 64), F32)
                hss4 = alloc((4, 8), F32)
                stmp = alloc((4, 65), F32)
                ymls = alloc((4, TT), BF16)
                slabIF, rIF = wload(l, "w_in", 0, 8, C_IF, C_IF + 16)
                pg = G()
                for s in range(4):
                    for kc in range(8):
                        mm(pg, PS[pg][:, s * 16:(s + 1) * 16], hT.v[:, kc, s * 128:(s + 1) * 128], hT.r(kc),
                           slabIF[:, kc, :], rIF, kc == 0, kc == 7)
                tt("dve", gpre.v, [gpre.r()], PS[pg][:, 0:64].rearrange("p (s g) -> p s g", g=16),
                   gateb_bc.flat.unsqueeze(1).to_broadcast([128, 4, 16]), [pr(pg), gateb_bc.r()], ALU.add)
                actf(lf.v, [lf.r()], gpre.v[:, :, 8:16], [gpre.r()], AF.Exp, scale=-1.0)
                actf(lf.v, [lf.r()], lf.v, [lf.r()], AF.Ln, bias=1.0)
                ts("dve", lf.v, [lf.r()], lf.v, [lf.r()], -1.0, None, ALU.mult)
                cp("pool", qkpre.v[:, :, 0:3], [qkpre.r()], qkhalo.v, [qkhalo.r()])
                slabsD = [wload(l, "w_in", 0, 8, C_QM + half * 512, C_QM + (half + 1) * 512) for half in range(2)]

                def qk_mm(half, pr2):
                    slabD, rD = slabsD[half]
                    pair = []
                    for cc in (2 * pr2, 2 * pr2 + 1):
                        c = half * 4 + cc
                        pi = G()
                        for kc in range(8):
                            mm(pi, PS[pi][:, :], slabD[:, kc, cc * 128:(cc + 1) * 128], rD, hT.v[:, kc, :], hT.r(kc),
                               kc == 0, kc == 7)
                        actf(qkpre.v[:, c, 3:TT + 3], [qkpre.r(c)], PS[pi][:, :], [pr(pi)], AF.Copy)
                        pair.append((half, cc, c, cacc[cc % 2]))
                    return pair

                def qk_conv(pair):
                    for (half, cc, c, ca) in pair:
                        ts("dve", ca.flat, [ca.r()], qkpre.v[:, c, 0:TT], [qkpre.r(c), pfm.r()], mcw[:, c, 0:1], mcb[:, c:c + 1],
                           ALU.mult, ALU.add)
                    for j in range(1, 4):
                        for (half, cc, c, ca) in pair:
                            stt("dve", ca.flat, [ca.r()], qkpre.v[:, c, j:TT + j], mcw[:, c, j:j + 1], ca.flat,
                                [qkpre.r(c), pfm.r(), ca.r()], ALU.mult, ALU.add)
                    for (half, cc, c, ca) in pair:
                        if half == 0:
                            actf(QmZ.v[0:64, 2 * cc, :], [QmZ.r(2 * cc)], ca.flat[0:64, :], [ca.r()], AF.Silu)
                            actf(QmZ.v[64:128, 2 * cc + 1, :], [QmZ.r(2 * cc + 1)], ca.flat[64:128, :], [ca.r()], AF.Silu)
                        else:
                            actf(KmT.v[:, cc, :], [KmT.r(cc)], ca.flat, [ca.r()], AF.Silu)

                order = [(0, 0), (0, 1), (1, 0), (1, 1)]
                infl = [qk_mm(*order[0]), qk_mm(*order[1])]
                for k_ in range(4):
                    qk_conv(infl[k_])
                    if k_ + 2 < 4:
                        infl.append(qk_mm(*order[k_ + 2]))
                cp("pool", qkhalo.v, [qkhalo.r()], qkpre.v[:, :, TT:TT + 3], [qkpre.r()])
                ck(31, T0)
                slabV, rV = wload(l, "w_in", 0, 8, C_VM, C_VM + 512)
                pb = G()
                for s in range(4):
                    mm(pb, PS[pb][:, s * 8:(s + 1) * 8], triUf, cst.r(), lf.v[:, s, :], lf.r(), True, True)
                cp("dve", bcs.v, [bcs.r()], PS[pb][:, 0:32].rearrange("p (s h) -> p s h", h=8), [pr(pb)])
                tt("dve", uu.v, [uu.r()], gpre.v[:, :, 0:8], bcs.v, [gpre.r(), bcs.r()], ALU.subtract)
                actf(uu.v, [uu.r()], uu.v, [uu.r()], AF.Exp)
                actf(bnd.v, [bnd.r()], bcs.v, [bcs.r()], AF.Exp, scale=-1.0, bias=float(np.log(8.0)))
                for s in range(4):
                    pi = G()
                    for kc in range(8):
                        mm(pi, PS[pi][:, :], hT.v[:, kc, s * 128:(s + 1) * 128], hT.r(kc), slabV[:, kc, :], rV,
                           kc == 0, kc == 7)
                    tt("dve", Vp.v[:, s, :, 0:64], [Vp.r(s)], PS[pi][:, :].rearrange("p (h e) -> p h e", e=64),
                       uu.v[:, s, :].unsqueeze(2).to_broadcast([128, NH, 64]), [pr(pi), uu.r()], ALU.mult)
                    cp("dve", Vp.v[:, s, :, 64:65], [Vp.r(s)], uu.v[:, s, :].unsqueeze(2), [uu.r()])
                slabO, rO = wload(l, "w_in", 0, 8, C_OM, C_OM + 512)
                for s in range(4):
                    pi = G()
                    for kc in range(8):
                        mm(pi, PS[pi][:, :], hT.v[:, kc, s * 128:(s + 1) * 128], hT.r(kc), slabO[:, kc, :], rO,
                           kc == 0, kc == 7)
                    actf(og.v[:, s, :], [og.r(s)], PS[pi][:, :], [pr(pi)], AF.Sigmoid)
                    tt("pool", og.v[:, s, :], [og.r(s)], og.v[:, s, :], hnw_bc.flat, [og.r(s), hnw_bc.r()], ALU.mult)
                for s in range(4):
                    pi = G()
                    for c in range(4):
                        mm(pi, PS[pi][:, c * 128:(c + 1) * 128], KmT.v[:, c, s * 128:(s + 1) * 128], KmT.r(c),
                           identb.flat, identb.r(), True, True)
                    evac(Km.v[:, s, :], [Km.r(s)], PS[pi][:, :], [pr(pi)])
                ck(32, T0)
                snaps = [Sbf2[t % 2]] + [alloc((4, 65), BF16) for _ in range(3)] + [Sbf2[(t + 1) % 2]]
                ebls = alloc((4, 8), F32)
                for s in range(4):
                    pe_ = G()
                    mm(pe_, PS[pe_][:, 0:8], onesf.flat, onesf.r(), lf.v[:, s, :], lf.r(), True, True)
                    actf(ebls.v[:, s, :], [ebls.r(s)], PS[pe_][:, 0:8], [pr(pe_)], AF.Exp)
                for s in range(4):
                    pst = []
                    for b2 in range(2):
                        pu = G()
                        pst.append(pu)
                        for cc in range(2):
                            c = b2 * 2 + cc
                            mm(pu, PS[pu][:, cc * 130:(cc + 1) * 130], Km.v[:, s, c * 128:(c + 1) * 128], Km.r(s),
                               Vp.v[:, s, 2 * c:2 * c + 2, :], Vp.r(s), True, True)
                    eblv = ebls.v[:, s, :].rearrange("p (c two) -> p c two", two=2)
                    for b2 in range(2):
                        puv = PS[pst[b2]][:, 0:260].rearrange("p (c two e) -> p c two e", two=2, e=65)
                        for hf in range(2):
                            rs_ = slice(hf * 64, hf * 64 + 64)
                            tt("dve", stmp.v[rs_, b2 * 2:b2 * 2 + 2, :], [stmp.r()], puv[rs_, :, hf, :],
                               S32.v[rs_, b2 * 2:b2 * 2 + 2, :], [pr(pst[b2]), S32.r()], ALU.add)
                    for hf in range(2):
                        rs_ = slice(hf * 64, hf * 64 + 64)
                        tt("dve", S32.v[rs_, :, :], [S32.r()], stmp.v[rs_, :, :],
                           eblv[rs_, :, hf].unsqueeze(2).to_broadcast([64, 4, 65]), [stmp.r(), ebls.r(s)], ALU.mult)
                    cp("dve", snaps[s + 1].v, [snaps[s + 1].r()], S32.v, [S32.r()])
                for s in range(4):
                    tsl = slice(s * 128, (s + 1) * 128)
                    for b2 in range(2):
                        psc = G()
                        for hh in range(4):
                            h = b2 * 4 + hh
                            c = h // 2
                            mm(psc, PS[psc][:, hh * 128:(hh + 1) * 128], KmT.v[:, c, tsl], KmT.r(c),
                               QmZ.v[:, h, tsl], QmZ.r(h), True, True)
                        tt("dve", pmall[s][b2].v, [pmall[s][b2].r()], PS[psc][:, :].rearrange("p (h t) -> p h t", t=128),
                           triUb.flat.unsqueeze(1).to_broadcast([128, 4, 128]), [pr(psc), triUb.r()], ALU.mult)
                for s in range(4):
                    tsl = slice(s * 128, (s + 1) * 128)
                    nd = []
                    pm = pmall[s]
                    for b2 in range(2):
                        pn = G()
                        nd.append(pn)
                        for hh in range(4):
                            h = b2 * 4 + hh
                            c = h // 2
                            mm(pn, PS[pn][:, hh * 65:(hh + 1) * 65], pm[b2].v[:, hh, :], pm[b2].r(), Vp.v[:, s, h, :], Vp.r(s),
                               True, False)
                            mm(pn, PS[pn][:, hh * 65:(hh + 1) * 65], QmZ.v[:, h, tsl], QmZ.r(h),
                               snaps[s].v[:, c, :], snaps[s].r(), False, True)
                    ck(326, T0)
                    for b2 in range(2):
                        ndv = PS[nd[b2]][:, 0:260].rearrange("p (h e) -> p h e", e=65)
                        dsl = dd.flat[:, b2 * 4:(b2 + 1) * 4]
                        cp("dve", dsl, [dd.r()], ndv[:, :, 64], [pr(nd[b2])])
                        stt("dve", dsl, [dd.r()], dsl, -1.0, dsl, [dd.r()], ALU.mult, ALU.max)
                        tt("dve", dsl, [dd.r()], dsl, bnd.v[:, s, b2 * 4:(b2 + 1) * 4], [dd.r(), bnd.r()], ALU.max)
                        recip("dve", dsl, [dd.r()], dsl, [dd.r()])
                        tt("dve", hbuf4.v[:, s, b2 * 4:(b2 + 1) * 4, :], [hbuf4.r(s)], ndv[:, :, 0:64],
                           dsl.unsqueeze(2).to_broadcast([128, 4, 64]), [pr(nd[b2]), dd.r()], ALU.mult)
                    ck(33, T0)
                    tt("pool", hsq.v, [hsq.r()], hbuf4.v[:, s], hbuf4.v[:, s], [hbuf4.r(s)], ALU.mult)
                    S.add("dve", lambda e, s=s: e.tensor_reduce(out=hss4.v[:, s, :], in_=hsq.v, axis=AX.X, op=ALU.add),
                          reads=[hsq.r()], writes=[hss4.r(s)])
                    ck(34, T0)
                sa_all = Buf(SB, qkpre.off, (8, TT), BF16)
                sb_all = Buf(SB, qkpre.off + 8 * TT, (8, TT), BF16)
                for jj in range(2):
                    slabGa, rGa = wload(l, "w_in", 0, 8, C_GA + jj * 512, C_GA + (jj + 1) * 512)
                    slabGb, rGb = wload(l, "w_in", 0, 8, C_GB + jj * 512, C_GB + (jj + 1) * 512)
                    for j4 in range(4):
                        j = jj * 4 + j4
                        pga = G()
                        for kc in range(8):
                            mm(pga, PS[pga][:, :], slabGa[:, kc, j4 * 128:(j4 + 1) * 128], rGa, hT.v[:, kc, :], hT.r(kc),
                               kc == 0, kc == 7)
                        actf(sa_all.v[:, j, :], [sa_all.r(j)], PS[pga][:, :], [pr(pga)], AF.Sigmoid)
                        pgb = G()
                        for kc in range(8):
                            mm(pgb, PS[pgb][:, :], slabGb[:, kc, j4 * 128:(j4 + 1) * 128], rGb, hT.v[:, kc, :], hT.r(kc),
                               kc == 0, kc == 7)
                        actf(sb_all.v[:, j, :], [sb_all.r(j)], PS[pgb][:, :], [pr(pgb)], AF.Sigmoid)
                rsqrt_(hss4.flat, hss4.r(), 1.0 / 64, EPS)
                for s in range(4):
                    tt("dve", hbuf4.v[:, s], [hbuf4.r(s)], hbuf4.v[:, s],
                       hss4.v[:, s, :].unsqueeze(2).to_broadcast([128, NH, 64]), [hbuf4.r(s), hss4.r()], ALU.mult)
                    tt("dve", ymls.v[:, s, :].rearrange("p (h e) -> p h e", e=64), [ymls.r(s)], hbuf4.v[:, s],
                       og.v[:, s, :].rearrange("p (h e) -> p h e", e=64), [hbuf4.r(s), og.r(s)], ALU.mult)
                for c in range(4):
                    pi = G()
                    for s in range(4):
                        mm(pi, PS[pi][:, s * 128:(s + 1) * 128], ymls.v[:, s, c * 128:(c + 1) * 128], ymls.r(s),
                           identb.flat, identb.r(), True, True)
                    evac(yTb.v[:, c, :], [yTb.r(c)], PS[pi][:, :], [pr(pi)])

                dump("ymls", ymls)
                ck(4, T0)
                phase()
                _skip = alloc((8240,), BF16)
                m1 = [alloc((TT,), F32) for _ in range(2)]
                mT = alloc((8, TT), BF16)
                otmp = [alloc((TT,), F32) for _ in range(2)]
                for jj in range(2):
                    slabBa, rBa = wload(l, "w_bra", 0, 4, jj * 512, (jj + 1) * 512)
                    slabBb, rBb = wload(l, "w_brb", 0, 4, jj * 512, (jj + 1) * 512)
                    for j4 in range(4):
                        j = jj * 4 + j4
                        k2 = j % 2
                        pa, pb = G(), G()
                        for kc in range(4):
                            mm(pa, PS[pa][:, :], slabBa[:, kc, j4 * 128:(j4 + 1) * 128], rBa, yTa.v[:, kc, :], yTa.r(kc),
                               kc == 0, kc == 3)
                        for kc in range(4):
                            mm(pb, PS[pb][:, :], slabBb[:, kc, j4 * 128:(j4 + 1) * 128], rBb, yTb.v[:, kc, :], yTb.r(kc),
                               kc == 0, kc == 3)
                        tt("dve", m1[k2].flat, [m1[k2].r()], PS[pa][:, :], sa_all.v[:, j, :], [pr(pa), sa_all.r(j)], ALU.mult)
                        tt("dve", sb_all.v[:, j, :], [sb_all.r(j)], PS[pb][:, :], sb_all.v[:, j, :], [pr(pb), sb_all.r(j)], ALU.mult)
                        tt("dve" if (j % 2 == 1) else "pool", mT.v[:, j, :], [mT.r(j)], m1[k2].flat, sb_all.v[:, j, :],
                           [m1[k2].r(), sb_all.r(j)], ALU.add)

                def proj_residual(lhs_buf, nk, wname, gcol0):
                    oi = 0
                    for half in range(2):
                        slabs = []
                        k0 = 0
                        while k0 < nk:
                            k1 = min(nk, k0 + 8)
                            slabs.append((k0, k1) + wload(l, wname, k0, k1, half * 512, (half + 1) * 512))
                            k0 = k1
                        for s in range(4):
                            pi = G()
                            for (k0, k1, sv, sr) in slabs:
                                for kc in range(k0, k1):
                                    mm(pi, PS[pi][:, :], lhs_buf.v[:, kc, s * 128:(s + 1) * 128], lhs_buf.r(kc),
                                       sv[:, kc - k0, :], sr, kc == 0, kc == nk - 1)
                            ot = otmp[oi % 2]
                            oi += 1
                            gsl = gbc.flat[:, gcol0 + half * 512:gcol0 + (half + 1) * 512]
                            tt("dve", ot.flat, [ot.r()], PS[pi][:, :], gsl, [pr(pi), gbc.r()], ALU.mult)
                            xsl = xt.v[:, s, half * 512:(half + 1) * 512]
                            tt("dve" if (half == 1 and s >= 2) else "pool", xsl, [xt.r(s)], xsl, ot.flat, [xt.r(s), ot.r()], ALU.add)

                dump("mT", mT)
                proj_residual(mT, 8, "w_out", 0)
                dump("x1", xt)

                ck(5, T0)
                phase()
                gt_ = l * NT + t + 1
                if gt_ < 2 * NT:
                    l2_, t2_ = divmod(gt_, NT)
                    xn_ = xtb[gt_ % 2]
                    src_ = x_in if l2_ == 0 else xmid_d
                    res_ = [] if l2_ == 0 else [("xmid", t2_ * TT, (t2_ + 1) * TT)]
                    for s in range(4):
                        dma("pool", xn_.v[:, s, :], src_[t2_ * TT + s * 128:t2_ * TT + (s + 1) * 128, :], res_,
                            [xn_.r(s)], "xt%d" % (gt_ % 2))
                do_norm(8, 24)
                NRING = 6
                sta = [alloc((TT + 2,), F32) for _ in range(NRING)]
                facc = [alloc((TT,), F32) for _ in range(NRING)]
                ga = [alloc((TT,), F32) for _ in range(2)]
                actT = alloc((22, TT), BF16)
                otmp = [alloc((TT,), F32) for _ in range(2)]
                told = ftail[t % 2]
                tnew = ftail[(t + 1) % 2]
                slab_cache = {}

                def up_mm(jp):
                    sl_ = jp // 2
                    if sl_ not in slab_cache:
                        slab_cache.clear()
                        slab_cache[sl_] = wload(l, "w_up", 0, 8, sl_ * 512, (sl_ + 1) * 512)
                    slabU, rU = slab_cache[sl_]
                    grp = []
                    for av in range(2):
                        q4 = (jp % 2) * 2 + av
                        ch = jp * 2 + av
                        pi = G()
                        for kc in range(8):
                            mm(pi, PS[pi][:, :], slabU[:, kc, q4 * 128:(q4 + 1) * 128], rU,
                               hT.v[:, kc, :], hT.r(kc), kc == 0, kc == 7)
                        k_ = (jp * 2 + av) % NRING
                        sb_, fa = sta[k_], facc[k_]
                        actf(sb_.flat[:, 2:TT + 2], [sb_.r(2, TT + 2)], PS[pi][:, :], [pr(pi)], AF.Copy)
                        actf(fa.flat, [fa.r()], PS[pi][:, :], [pr(pi), pfm.r()], AF.Identity,
                             scale=fcw[:, ch, 2:3], bias=fcb[:, ch:ch + 1])
                        grp.append((ch, sb_, fa))
                    return grp

                def up_conv(jp, grp):
                    for (ch, sb_, fa) in grp:
                        cp("pool", sb_.flat[:, 0:2], [sb_.r(0, 2)], told.v[:, ch, :], [told.r(ch)])
                        cp("pool", tnew.v[:, ch, :], [tnew.r(ch)], sb_.flat[:, TT:TT + 2], [sb_.r(TT, TT + 2)])
                    for j in range(2):
                        for (ch, sb_, fa) in grp:
                            stt("dve", fa.flat, [fa.r()], sb_.flat[:, j:TT + j], fcw[:, ch, j:j + 1], fa.flat,
                                [sb_.r(j, TT + j), pfm.r(), fa.r()], ALU.mult, ALU.add)
                    g_ = ga[jp % 2]
                    fa_a = grp[0][2]
                    fa_v = grp[1][2]
                    actf(g_.flat, [g_.r()], fa_a.flat, [fa_a.r()], AF.Gelu)
                    tt("dve" if (jp >= 20 or jp % 2 == 0) else "pool", actT.v[:, jp, :], [actT.r(jp)], g_.flat, fa_v.flat, [g_.r(), fa_v.r()], ALU.mult)

                DEPTH = 2
                inflight = {}
                for jp in range(22 + DEPTH):
                    if jp < 22:
                        inflight[jp] = up_mm(jp)
                    if jp >= DEPTH:
                        up_conv(jp - DEPTH, inflight.pop(jp - DEPTH))
                proj_residual(actT, 22, "w_down", 1024)

                ck(6, T0)
                if l == 1:
                    ss2 = alloc((4,), F32)
                    junk2 = alloc((D,), BF16)
                    for s in range(4):
                        actf(junk2.flat, [junk2.r(), ss2.r()], xt.v[:, s, :], [xt.r(s)], AF.Square,
                             accum=ss2.flat[:, s:s + 1])
                    rsqrt_(ss2.flat, ss2.r(), 1.0 / D, EPS)
                    for s in range(4):
                        stt("dve", xt.v[:, s, :], [xt.r(s)], xt.v[:, s, :], ss2.flat[:, s:s + 1], fnw_bc.flat,
                            [xt.r(s), ss2.r(), fnw_bc.r()], ALU.mult, ALU.mult)
                for s in range(4):
                    dma("pool", x_dst[T0 + s * 128:T0 + (s + 1) * 128, :], xt.v[:, s, :], [xt.r(s)],
                        [("xmid" if l == 0 else "out", T0 + s * 128, T0 + (s + 1) * 128)],
                        "xst", outflag=(l == 1))
        S.emit(nc)


def _fm(v, k):
    return np.ascontiguousarray(v.reshape(k, 128).T)


def _wp(w):
    K, N = w.shape
    return np.ascontiguousarray(w.reshape(K // 128, 128, N).transpose(1, 0, 2))


def _prep_shared(inp):
    f32 = np.float32
    sh = {}
    cst = np.zeros((128, 328), f32)
    cst[64, 264:328] = 1.0
    cst[:, 0:128] = np.eye(128, dtype=f32)
    cst[:, 128:256] = np.triu(np.ones((128, 128), f32))
    half = ROPE // 2
    inv = (10000.0 ** (-np.arange(half, dtype=np.float32) / half)).astype(f32)
    for p in range(64, 96):
        cst[p, 256] = inv[(p - 64) % half]
        cst[p, 257] = np.pi / 2
        cst[p, 258] = np.pi if p < 80 else 0.0
    sh["cst"] = cst
    o = np.cumsum([0, 384, 256, 32, 512, 512, 512, 512, 8, 8, 1024, 1024])
    cq, ckv, kr, qm, km, vm, om, im, fm, ga, gb = [np.arange(o[i], o[i + 1]) for i in range(11)]
    krs = np.concatenate([kr[half:], kr[:half]])
    idx = np.concatenate([cq, ckv, ckv[:64], kr, ckv[:64], krs, qm, km, vm, om, im, fm, ga, gb])
    assert idx.size == 4944
    uq1, uq2 = [], []
    for h in range(NH):
        b = h * 96
        uq1.append(np.arange(b, b + 96))
        uq2.append(np.concatenate([np.arange(b, b + 64), np.arange(b + 64 + half, b + 96), np.arange(b + 64, b + 64 + half)]))
    uqi = np.concatenate(uq1 + uq2)
    kvi = np.concatenate([np.arange(h * 128, h * 128 + 64) for h in range(NH)] +
                         [np.arange(h * 128 + 64, h * 128 + 128) for h in range(NH)])
    upi = np.concatenate([np.concatenate([np.arange(j * 128, (j + 1) * 128), np.arange(DFF + j * 128, DFF + (j + 1) * 128)])
                          for j in range(22)])
    L = 2
    sh["w_in"] = np.stack([_wp(inp["w_in"][l][:, idx]) for l in range(L)])
    sh["w_uq"] = np.stack([_wp(inp["w_uq"][l][:, uqi]) for l in range(L)])
    sh["w_ukv"] = np.stack([_wp(inp["w_ukv"][l][:, kvi]) for l in range(L)])
    sh["w_bra"] = np.stack([_wp(inp["w_br_mla"][l]) for l in range(L)])
    sh["w_brb"] = np.stack([_wp(inp["w_br_mlstm"][l]) for l in range(L)])
    sh["w_out"] = np.stack([_wp(inp["w_out"][l]) for l in range(L)])
    sh["w_up"] = np.stack([_wp(inp["ffn_w_up"][l][:, upi]) for l in range(L)])
    sh["w_down"] = np.stack([_wp(inp["ffn_w_down"][l]) for l in range(L)])
    sh["ada_w"] = np.stack([_wp(inp["ada_w"][l]) for l in range(L)])
    pfm = np.zeros((L, 128, NPF), f32)
    prow = np.zeros((L, 1, NPR), f32)
    for l in range(L):
        pfm[l, :, 0:8] = _fm(inp["norm_mix_w"][l], 8)
        pfm[l, :, 8:16] = _fm(inp["norm_ffn_w"][l], 8)
        pfm[l, :, 16:19] = _fm(inp["q_norm_w"][l], 3)
        pfm[l, :, 19:21] = _fm(inp["kv_norm_w"][l], 2)
        cw = inp["mlstm_conv_w"][l]
        pfm[l, :, 21:53] = cw.reshape(4, 8, 128).transpose(2, 1, 0).reshape(128, 32)
        pfm[l, :, 53:61] = _fm(inp["mlstm_conv_b"][l], 8)
        fw = inp["ffn_conv_w"][l][:, upi]
        pfm[l, :, 61:193] = fw.reshape(3, 44, 128).transpose(2, 1, 0).reshape(128, 132)
        pfm[l, :, 193:237] = _fm(inp["ffn_conv_b"][l][upi], 44)
        pfm[l, :, 237:285] = _fm(inp["ada_b"][l], 48)
        prow[l, 0, 0:16] = inp["mlstm_gate_b"][l]
        prow[l, 0, 16:528] = inp["mlstm_head_norm_w"][l]
        prow[l, 0, 528:1552] = inp["ada_b"][l][2048:3072]
        prow[l, 0, 1552:2576] = inp["ada_b"][l][5120:6144]
    sh["pfm"] = pfm
    sh["prow"] = prow
    sh["fnw"] = np.ascontiguousarray(inp["final_norm_w"].reshape(1, D).astype(f32))
    return sh


_CACHE = {}


def kernel(**inputs):
    inp = {k: np.asarray(v) for k, v in inputs.items()}
    x = inp["x"]
    B, S_LEN, _ = x.shape
    sh = _prep_shared(inp)
    if S_LEN not in _CACHE:
        _CACHE[S_LEN] = build_nc(S_LEN)[0]
    nc = _CACHE[S_LEN]
    in_maps = []
    for b in range(B):
        m = dict(sh)
        m["x"] = np.ascontiguousarray(x[b].astype(np.float32))
        m["cT"] = _fm(inp["c"][b].astype(np.float32), 8)
        m["pos"] = np.ascontiguousarray(inp["positions"][b].reshape(1, S_LEN).astype(np.int32))
        in_maps.append(m)
    res = run_bass_kernel_spmd(nc, in_maps, core_ids=list(range(B)))
    out = np.stack([np.asarray(r["out"]) for r in res.results], axis=0)
    return out.astype(np.float32)
```
